# Optimizing a Trainium2 kernel written in Bass

```python
import math
import jax, jax.numpy as jnp
from jax import lax
import numpy as np

D_MODEL = 1024
BATCH = 32
SEQ = 256
DEPTH = 2
DEC_BATCH = 2
DEC_SEQ = 4096
PAST_LEN = 256

GRID_W = 64
N_GROUPS = 4
GROUP_W = D_MODEL // N_GROUPS
D_MIX = N_GROUPS * GROUP_W
HEAD_DIM = 64
N_HEADS_GROUP = GROUP_W // HEAD_DIM
N_SEGMENTS = 10
W_IN_COLS = N_SEGMENTS * GROUP_W
CONV_K = 31
NA_WIN_R = 8
NA_WIN_C = 16
DIFF_SUB = HEAD_DIM // 2
ROPE_BASE = 10000.0
CHUNK = 128
Q_BLOCK = 128
PEER_HEADS = 8
PEER_NKEYS = 128
PEER_EXPERTS = PEER_NKEYS * PEER_NKEYS
PEER_DKEY = 256
PEER_TOPK = 16
TOKEN_BLOCK = 128
EPS = 1e-6
NEG = -1e30

kernel_name = 'hybrid_conv_natten_diffattn_gmlp_peer_step'


def rms_norm(x, g):
    xf = x.astype(jnp.float32)
    y = xf * lax.rsqrt(jnp.mean(xf * xf, axis=-1, keepdims=True) + EPS)
    return (y * g.astype(jnp.float32)).astype(x.dtype)


def layer_norm(x, g, b):
    xf = x.astype(jnp.float32)
    mu = jnp.mean(xf, axis=-1, keepdims=True)
    xc = xf - mu
    y = xc * lax.rsqrt(jnp.mean(xc * xc, axis=-1, keepdims=True) + EPS)
    return (y * g.astype(jnp.float32) + b.astype(jnp.float32)).astype(x.dtype)


def adaln(cond, w_mod, b_mod):
    m = jax.nn.silu(cond) @ w_mod + b_mod
    return tuple(jnp.split(m[:, None, :], 6, axis=-1))


def split_heads(x):
    B, L, _ = x.shape
    return x.reshape(B, L, N_HEADS_GROUP, HEAD_DIM).transpose(0, 2, 1, 3)


def merge_heads(x):
    B, H, L, d = x.shape
    return x.transpose(0, 2, 1, 3).reshape(B, L, H * d)


def rope_1d(x, pos):
    half = x.shape[-1] // 2
    inv = ROPE_BASE ** (-jnp.arange(half, dtype=jnp.float32) / half)
    ang = pos[:, None] * inv[None, :]
    cos = jnp.cos(ang).astype(x.dtype)
    sin = jnp.sin(ang).astype(x.dtype)
    x1, x2 = x[..., :half], x[..., half:]
    return jnp.concatenate([x1 * cos - x2 * sin, x1 * sin + x2 * cos], axis=-1)


def axial_rope(x):
    t = jnp.arange(x.shape[-2])
    rows = (t // GRID_W).astype(jnp.float32)
    cols = (t % GRID_W).astype(jnp.float32)
    h = x.shape[-1] // 2
    return jnp.concatenate([rope_1d(x[..., :h], rows), rope_1d(x[..., h:], cols)], axis=-1)


def sweep_queries(fn, *qs):
    B, H, L, _ = qs[0].shape
    nb = L // Q_BLOCK
    blocks = tuple(q.reshape(B, H, nb, Q_BLOCK, q.shape[-1]).transpose(2, 0, 1, 3, 4) for q in qs)
    out = lax.map(lambda qb: fn(*qb), blocks)
    return out.transpose(1, 2, 0, 3, 4).reshape(B, H, L, out.shape[-1])


def softmax_attend(q, k, v):
    scale = q.shape[-1] ** -0.5
    def blk(qb):
        s = jnp.einsum('bhqd,bhkd->bhqk', qb, k).astype(jnp.float32) * scale
        p = jax.nn.softmax(s, axis=-1)
        return jnp.einsum('bhqk,bhkd->bhqd', p.astype(v.dtype), v)
    return sweep_queries(blk, q)


def diff_attend(q1, q2, k1, k2, v, lam):
    scale = DIFF_SUB ** -0.5
    def blk(q1b, q2b):
        s1 = jnp.einsum('bhqd,bhkd->bhqk', q1b, k1).astype(jnp.float32) * scale
        s2 = jnp.einsum('bhqd,bhkd->bhqk', q2b, k2).astype(jnp.float32) * scale
        a = jax.nn.softmax(s1, axis=-1) - lam * jax.nn.softmax(s2, axis=-1)
        return jnp.einsum('bhqk,bhkd->bhqd', a.astype(v.dtype), v)
    return sweep_queries(blk, q1, q2)


def neighbourhood_attend(q, k, v, ck, cv, rel_bias):
    B, H, S, d = q.shape
    rows = S // GRID_W
    wr = min(NA_WIN_R, rows)
    r = jnp.arange(rows)
    r0 = jnp.clip(r - wr // 2, 0, rows - wr)
    row_idx = r0[:, None] + jnp.arange(wr)[None, :]
    dr = row_idx - r[:, None]
    c = jnp.arange(GRID_W)
    c0 = jnp.clip(c - NA_WIN_C // 2, 0, GRID_W - NA_WIN_C)
    in_win = (c[None, :] >= c0[:, None]) & (c[None, :] < c0[:, None] + NA_WIN_C)
    dc_idx = jnp.clip(c[None, :] - c[:, None] + NA_WIN_C - 1, 0, 2 * NA_WIN_C - 2)
    rb = rel_bias[:, dr + NA_WIN_R - 1]
    bias = jnp.take(rb, dc_idx, axis=-1).transpose(0, 1, 3, 2, 4)
    qg = q.reshape(B, H, rows, GRID_W, d)
    kg = k.reshape(B, H, rows, GRID_W, d)[:, :, row_idx]
    vg = v.reshape(B, H, rows, GRID_W, d)[:, :, row_idx]
    scale = d ** -0.5
    s_loc = jnp.einsum('bhrqd,bhrwkd->bhrqwk', qg, kg).astype(jnp.float32) * scale + bias[None].astype(jnp.float32)
    s_loc = jnp.where(in_win[:, None, :], s_loc, NEG)
    s_ctx = jnp.einsum('bhrqd,bhmd->bhrqm', qg, ck).astype(jnp.float32) * scale
    n_loc = wr * GRID_W
    s = jnp.concatenate([s_loc.reshape(B, H, rows, GRID_W, n_loc), s_ctx], axis=-1)
    p = jax.nn.softmax(s, axis=-1).astype(v.dtype)
    p_loc = p[..., :n_loc].reshape(B, H, rows, GRID_W, wr, GRID_W)
    o = (jnp.einsum('bhrqwk,bhrwkd->bhrqd', p_loc, vg)
         + jnp.einsum('bhrqm,bhmd->bhrqd', p[..., n_loc:], cv))
    return o.reshape(B, H, S, d)


def conv_module(a, gate, w, b, ln_g, ln_b):
    y = a * jax.nn.sigmoid(gate)
    y = lax.conv_general_dilated(y, w[:, None, :], window_strides=(1,), padding='SAME',
                                 dimension_numbers=('NWC', 'WIO', 'NWC'),
                                 feature_group_count=GROUP_W) + b
    return jax.nn.silu(layer_norm(y, ln_g, ln_b))


def chunk_gmlp(u, v, ln_g, ln_b, ws, bs):
    u = jax.nn.gelu(u)
    v = layer_norm(jax.nn.gelu(v), ln_g, ln_b)
    B, L, _ = v.shape
    vg = v.reshape(B, L // CHUNK, CHUNK, N_HEADS_GROUP, GROUP_W // N_HEADS_GROUP)
    s = jnp.einsum('gij,bnjgc->bnigc', ws, vg) + bs.T[None, None, :, :, None]
    return u * s.reshape(B, L, GROUP_W)


def peer_ffn(h, wq, sub_keys, pu, pv):
    B, L, D = h.shape
    hb = h.reshape((B * L) // TOKEN_BLOCK, TOKEN_BLOCK, D)
    def blk(xb):
        q = (xb @ wq).reshape(TOKEN_BLOCK, PEER_HEADS, 2, PEER_DKEY // 2)
        s = jnp.einsum('thpk,pnk->thpn', q, sub_keys).astype(jnp.float32)
        sv, si = lax.top_k(s, PEER_TOPK)
        cand_s = (sv[:, :, 0, :, None] + sv[:, :, 1, None, :]).reshape(TOKEN_BLOCK, PEER_HEADS, PEER_TOPK * PEER_TOPK)
        cand_i = (si[:, :, 0, :, None] * PEER_NKEYS + si[:, :, 1, None, :]).reshape(TOKEN_BLOCK, PEER_HEADS, PEER_TOPK * PEER_TOPK)
        top_s, pos = lax.top_k(cand_s, PEER_TOPK)
        eidx = jnp.take_along_axis(cand_i, pos, axis=-1)
        g = jax.nn.softmax(top_s, axis=-1)
        a = jnp.einsum('thkd,td->thk', pu[eidx], xb)
        w = (jax.nn.gelu(a.astype(jnp.float32)) * g).astype(xb.dtype)
        return jnp.einsum('thk,thkd->td', w, pv[eidx])
    return lax.map(blk, hb).reshape(B, L, D)


def trunk_layer(x, mod, p, lam_init, ctx_kv=None):
    latent = ctx_kv is not None
    B, L, _ = x.shape
    sh1, sc1, g1, sh2, sc2, g2 = mod
    h = rms_norm(x, p['norm1']) * (1 + sc1) + sh1
    parts = jnp.split(h @ p['w_in'], N_SEGMENTS, axis=-1)
    conv_o = conv_module(parts[0], parts[1], p['conv_w'], p['conv_b'], p['conv_ln_g'], p['conv_ln_b'])
    nq = rms_norm(split_heads(parts[2]), p['na_qn'])
    nk = rms_norm(split_heads(parts[3]), p['na_kn'])
    nv = split_heads(parts[4])
    if latent:
        na_o = neighbourhood_attend(nq, nk, nv, ctx_kv[0], ctx_kv[1], p['na_bias'])
    else:
        na_o = softmax_attend(nq, nk, nv)
    sub = (B, N_HEADS_GROUP, L, 2, DIFF_SUB)
    dq = rms_norm(split_heads(parts[5]).reshape(sub), p['diff_qn'])
    dk = rms_norm(split_heads(parts[6]).reshape(sub), p['diff_kn'])
    dv = split_heads(parts[7])
    dq1, dq2, dk1, dk2 = dq[..., 0, :], dq[..., 1, :], dk[..., 0, :], dk[..., 1, :]
    if latent:
        ck, cv = ctx_kv[2], ctx_kv[3]
        dq1, dq2 = axial_rope(dq1), axial_rope(dq2)
        dk1 = jnp.concatenate([axial_rope(dk1), ck[..., :DIFF_SUB]], axis=2)
        dk2 = jnp.concatenate([axial_rope(dk2), ck[..., DIFF_SUB:]], axis=2)
        dvals = jnp.concatenate([dv, cv], axis=2)
    else:
        dvals = dv
    dl = p['diff_lambda'].astype(jnp.float32)
    lam = jnp.exp(jnp.sum(dl[0] * dl[1])) - jnp.exp(jnp.sum(dl[2] * dl[3])) + lam_init
    diff_o = diff_attend(dq1, dq2, dk1, dk2, dvals, lam)
    diff_o = rms_norm(diff_o, p['diff_subln']) * (1.0 - lam_init)
    gm_o = chunk_gmlp(parts[8], parts[9], p['gmlp_ln_g'], p['gmlp_ln_b'], p['gmlp_ws'], p['gmlp_bs'])
    mixed = jnp.concatenate([conv_o, merge_heads(na_o), merge_heads(diff_o), gm_o], axis=-1) @ p['w_out']
    x = x + g1 * mixed
    h2 = rms_norm(x, p['norm2']) * (1 + sc2) + sh2
    x = x + g2 * peer_ffn(h2, p['peer_wq'], p['peer_sub_keys'], p['peer_u'], p['peer_v'])
    if latent:
        return x, None
    dk_flat = dk.reshape(B, N_HEADS_GROUP, L, HEAD_DIM)
    return x, (jnp.stack([nk, nv], axis=1), jnp.stack([dk_flat, dv], axis=1))


def setup_inputs(seed: int = 0) -> dict:
    key = jax.random.key(seed)
    ks = iter(jax.random.split(key, 40))
    def nrm(shape, s):
        return jax.random.normal(next(ks), shape, jnp.float32) * s
    cache_shape = (DEC_BATCH, DEPTH, 2, N_HEADS_GROUP, PAST_LEN, HEAD_DIM)
    return {
        'x_prompt': nrm((BATCH, SEQ, D_MODEL), 1.0),
        'x_sample': nrm((DEC_BATCH, DEC_SEQ, D_MODEL), 1.0),
        'cache_na_kv': nrm(cache_shape, 1.0),
        'cache_diff_kv': nrm(cache_shape, 1.0),
        'c': nrm((DEC_BATCH, D_MODEL), 1.0),
        'c_ctx': nrm((D_MODEL,), 1.0),
        'w_mod': nrm((DEPTH, D_MODEL, 6 * D_MODEL), 0.5 * D_MODEL ** -0.5),
        'b_mod': nrm((DEPTH, 6 * D_MODEL), 0.02),
        'norm1_g': 1.0 + nrm((DEPTH, D_MODEL), 0.1),
        'norm2_g': 1.0 + nrm((DEPTH, D_MODEL), 0.1),
        'w_in': nrm((DEPTH, D_MODEL, W_IN_COLS), D_MODEL ** -0.5),
        'conv_w': nrm((DEPTH, CONV_K, GROUP_W), CONV_K ** -0.5),
        'conv_b': nrm((DEPTH, GROUP_W), 0.02),
        'conv_ln_g': 1.0 + nrm((DEPTH, GROUP_W), 0.1),
        'conv_ln_b': nrm((DEPTH, GROUP_W), 0.02),
        'na_qn_g': 1.0 + nrm((DEPTH, HEAD_DIM), 0.1),
        'na_kn_g': 1.0 + nrm((DEPTH, HEAD_DIM), 0.1),
        'na_rel_bias': nrm((DEPTH, N_HEADS_GROUP, 2 * NA_WIN_R - 1, 2 * NA_WIN_C - 1), 0.1),
        'diff_qn_g': 1.0 + nrm((DEPTH, DIFF_SUB), 0.1),
        'diff_kn_g': 1.0 + nrm((DEPTH, DIFF_SUB), 0.1),
        'diff_lambda': nrm((DEPTH, 4, DIFF_SUB), 0.1),
        'diff_subln_g': 1.0 + nrm((DEPTH, HEAD_DIM), 0.1),
        'gmlp_ln_g': 1.0 + nrm((DEPTH, GROUP_W), 0.1),
        'gmlp_ln_b': nrm((DEPTH, GROUP_W), 0.02),
        'gmlp_ws': nrm((DEPTH, N_HEADS_GROUP, CHUNK, CHUNK), CHUNK ** -0.5),
        'gmlp_bs': 1.0 + nrm((DEPTH, N_HEADS_GROUP, CHUNK), 0.1),
        'w_out': nrm((DEPTH, D_MIX, D_MODEL), D_MIX ** -0.5),
        'peer_wq': nrm((DEPTH, D_MODEL, PEER_HEADS * PEER_DKEY), D_MODEL ** -0.5),
        'peer_sub_keys': nrm((DEPTH, 2, PEER_NKEYS, PEER_DKEY // 2), (PEER_DKEY // 2) ** -0.5),
        'peer_u': nrm((DEPTH, PEER_EXPERTS, D_MODEL), D_MODEL ** -0.5),
        'peer_v': nrm((DEPTH, PEER_EXPERTS, D_MODEL), 0.25),
    }


def reference(x_prompt, x_sample, cache_na_kv, cache_diff_kv, c, c_ctx, w_mod, b_mod, norm1_g, norm2_g,
              w_in, conv_w, conv_b, conv_ln_g, conv_ln_b, na_qn_g, na_kn_g, na_rel_bias, diff_qn_g,
              diff_kn_g, diff_lambda, diff_subln_g, gmlp_ln_g, gmlp_ln_b, gmlp_ws, gmlp_bs, w_out,
              peer_wq, peer_sub_keys, peer_u, peer_v):
    y_prompt, y_sample = x_prompt, x_sample
    na_states, diff_states = [], []
    for l in range(DEPTH):
        p = {
            'norm1': norm1_g[l], 'norm2': norm2_g[l], 'w_in': w_in[l],
            'conv_w': conv_w[l], 'conv_b': conv_b[l], 'conv_ln_g': conv_ln_g[l], 'conv_ln_b': conv_ln_b[l],
            'na_qn': na_qn_g[l], 'na_kn': na_kn_g[l], 'na_bias': na_rel_bias[l],
            'diff_qn': diff_qn_g[l], 'diff_kn': diff_kn_g[l], 'diff_lambda': diff_lambda[l],
            'diff_subln': diff_subln_g[l],
            'gmlp_ln_g': gmlp_ln_g[l], 'gmlp_ln_b': gmlp_ln_b[l], 'gmlp_ws': gmlp_ws[l], 'gmlp_bs': gmlp_bs[l],
            'w_out': w_out[l], 'peer_wq': peer_wq[l], 'peer_sub_keys': peer_sub_keys[l],
            'peer_u': peer_u[l], 'peer_v': peer_v[l],
        }
        lam_init = 0.8 - 0.6 * math.exp(-0.3 * l)
        mod_ctx = adaln(c_ctx[None, :], w_mod[l], b_mod[l])
        y_prompt, (na_kv, diff_kv) = trunk_layer(y_prompt, mod_ctx, p, lam_init)
        na_states.append(na_kv)
        diff_states.append(diff_kv)
        mod_lat = adaln(c, w_mod[l], b_mod[l])
        ctx = (cache_na_kv[:, l, 0], cache_na_kv[:, l, 1], cache_diff_kv[:, l, 0], cache_diff_kv[:, l, 1])
        y_sample, _ = trunk_layer(y_sample, mod_lat, p, lam_init, ctx)
    new_na_kv = jnp.stack(na_states, axis=1)
    new_diff_kv = jnp.stack(diff_states, axis=1)
    return (y_prompt, y_sample, new_na_kv, new_diff_kv)
```

```python
import math
import numpy as np
from contextlib import ExitStack
import concourse.bass as bass
import concourse.mybir as mybir
from concourse.bass_utils import run_bass_kernel_spmd

F32 = mybir.dt.float32
BF16 = mybir.dt.bfloat16
U32 = mybir.dt.uint32
AF = mybir.ActivationFunctionType
ALU = mybir.AluOpType
AX = mybir.AxisListType

NCORES = 8
D = 1024
KC = 8
TB = 256
L = 2
EPS = 1e-6
NEXP_C = 128
ND = 12


class Buf:
    __slots__ = ("name", "w", "r")

    def __init__(self, name):
        self.name = name
        self.w = None
        self.r = {}


class Tl:
    def __init__(self, t, name):
        self.t = t
        self.b = Buf(name)

    def __getitem__(self, idx):
        return self.t[idx]


class FW:
    def __init__(self, nc, es):
        self.nc = nc
        self.es = es
        self.eng = {"pe": nc.tensor, "dve": nc.vector, "act": nc.scalar, "pool": nc.gpsimd, "sp": nc.sync}
        self.sem = {k: es.enter_context(nc.semaphore("sem_" + k)) for k in ["pe", "dve", "act", "pool"]}
        self.cnt = {k: 0 for k in self.sem}
        self.seen = {k: {} for k in self.eng}
        self.dsem = {q: [es.enter_context(nc.semaphore("d_%s_%d" % (q, i))) for i in range(ND)] for q in ["sp", "pool"]}
        self.dcnt = {q: [0] * ND for q in self.dsem}
        self.dnext = {q: 0 for q in self.dsem}
        self.nins = 0

    def _wait(self, E, tok):
        if tok is None:
            return
        if tok[0] == "c":
            _, Dn, n = tok
            if Dn == E and E == "pe":
                return
            key = ("c", Dn)
            if self.seen[E].get(key, 0) >= n:
                return
            self.eng[E].wait_ge(self.sem[Dn], n)
            self.seen[E][key] = n
        else:
            _, q, i, n = tok
            key = ("d", q, i)
            if self.seen[E].get(key, 0) >= n:
                return
            self.eng[E].wait_ge(self.dsem[q][i], n)
            self.seen[E][key] = n

    def _deps(self, E, reads, writes, disjoint=False):
        for b in reads:
            self._wait(E, b.w)
        for b in writes:
            if not (disjoint and b.w is not None and b.w[0] == "c" and b.w[1] == E):
                self._wait(E, b.w)
            for t in list(b.r.values()):
                if disjoint and t[0] == "c" and t[1] == E:
                    continue
                self._wait(E, t)

    def _upd(self, E, tok, reads, writes):
        for b in reads:
            b.r[E] = tok
        for b in writes:
            b.w = tok
            b.r = {}

    def op(self, E, fn, reads=(), writes=(), inc=True, disjoint=False):
        reads = [x.b if isinstance(x, Tl) else x for x in reads]
        writes = [x.b if isinstance(x, Tl) else x for x in writes]
        self._deps(E, reads, writes, disjoint)
        ins = fn()
        self.nins += 1
        if inc:
            ins.then_inc(self.sem[E], 1)
            self.cnt[E] += 1
            tok = ("c", E, self.cnt[E])
        else:
            tok = ("c", E, self.cnt[E] + 1)
        self._upd(E, tok, reads, writes)

    def dma(self, q, out, in_, reads=(), writes=(), **kw):
        reads = [x.b if isinstance(x, Tl) else x for x in reads]
        writes = [x.b if isinstance(x, Tl) else x for x in writes]
        i = self.dnext[q]
        self.dnext[q] = (i + 1) % ND
        if self.dcnt[q][i] > 0:
            self._wait(q, ("d", q, i, self.dcnt[q][i]))
        self._deps(q, reads, writes)
        ins = self.eng[q].dma_start(out=out, in_=in_, **kw)
        self.nins += 1
        ins.then_inc(self.dsem[q][i], 16)
        self.dcnt[q][i] += 16
        tok = ("d", q, i, self.dcnt[q][i])
        self._upd(q, tok, reads, writes)

    def barrier(self):
        for E in ["pe", "dve", "act", "pool", "sp"]:
            for Dn in self.sem:
                if Dn != E and self.cnt[Dn] > 0:
                    self._wait(E, ("c", Dn, self.cnt[Dn]))
            for q in self.dsem:
                for i in range(ND):
                    if self.dcnt[q][i] > 0:
                        self._wait(E, ("d", q, i, self.dcnt[q][i]))

    def finish(self):
        for q in self.dsem:
            for i in range(ND):
                if self.dcnt[q][i] > 0:
                    self._wait("sp", ("d", q, i, self.dcnt[q][i]))
        for Dn in self.sem:
            if self.cnt[Dn] > 0:
                self._wait("sp", ("c", Dn, self.cnt[Dn]))


PC_N1G, PC_N2G, PC_BMOD, PC_CONVW, PC_CONVB, PC_CLNG, PC_CLNB = 0, 8, 16, 64, 126, 128, 130
PC_NAQ, PC_NAK, PC_DQ1, PC_DQ2, PC_DK, PC_SUB = 132, 133, 134, 135, 136, 137
PC_NAQE, PC_NAQO, PC_D1E, PC_D1O, PC_D2E, PC_D2O = 138, 139, 140, 141, 142, 143
PC_DQ = 144
NPC = 145
CC_ID, CC_B64, CC_B32, CC_O1024, CC_O256, CC_OE, CC_OO, CC_IOTA = 0, 128, 256, 384, 512, 640, 768, 896
CC_IOTA16 = 1024
CC_MASK = 1040
CC_PSW = 1044
NCC = 1044 + 128


def build_program(n_seq=4, debug=False, stage=9, nl=L, n_lat=16, lat_stage=9):
    nc = bass.Bass("TRN2", target_bir_lowering=False)
    NT = max(n_seq, 1) * TB
    dr = {}

    def din(name, shape, dt=F32):
        dr[name] = nc.dram_tensor(name, list(shape), dt, kind="ExternalInput").ap()
        return dr[name]

    def dout(name, shape, dt=F32):
        dr[name] = nc.dram_tensor(name, list(shape), dt, kind="ExternalOutput").ap()
        return dr[name]

    xc = din("xc", [128, KC, NT])
    condT = din("condT", [128, KC, 2])
    consts = din("consts", [128, NCC])
    pp = din("pp", [L, 128, NPC])
    rowp = din("rowp", [L, 512])
    bsT = din("bsT", [L, 128, 2, 128])
    wsT = din("wsT", [L, 128, 4, 128])
    dlam = din("dlam", [L, 128])
    w_mod = din("w_mod", [L, D, 6 * D])
    w_in = din("w_in", [L, D, 2560])
    w_out = din("w_out", [L, D, D])
    wq = din("wq", [L, D, 2048])
    skT = din("skT", [L, 128, 2, 128])
    puT = din("puT", [L, NEXP_C, 128, KC * 128])
    pv = din("pv", [L, 128 * 128, D])
    yc = dout("yc", [128, KC, NT])
    nkT = dout("nkT", [L, 128, 2, NT])
    dkT = dout("dkT", [L, 128, 2, NT])
    nvo = dout("nvo", [L, 128, NT // 128, 256])
    dvo = dout("dvo", [L, 128, NT // 128, 256])
    if debug:
        dbg = dout("dbg", [128, 8, TB])
    pubf = nc.dram_tensor("pubf", [L, NEXP_C, 128, KC * 128], BF16, kind="Internal").ap()
    pvbf = nc.dram_tensor("pvbf", [L, NEXP_C, 128, D], BF16, kind="Internal").ap()
    if n_lat:
        xl = din("xl", [128, KC, SEQL])
        nckT = din("nckT", [L, 128, 2, 256])
        ncv = din("ncv", [L, 128, 2, 256])
        dckT = din("dckT", [L, 128, 2, 256])
        dcv = din("dcv", [L, 128, 2, 256])
        biasT = din("biasT", [L, 128, NCOMBO * 4, 64])
        cosT = din("cosT", [128, SEQL])
        sinT = din("sinT", [128, SEQL])
        yl = dout("yl", [128, KC, SEQL])
        ybuf = nc.dram_tensor("ybuf", [128, 2, SEQL + 30], BF16, kind="Internal").ap()
        nks = nc.dram_tensor("nks", [128, 2, SEQL], BF16, kind="Internal").ap()
        nvs = nc.dram_tensor("nvs", [128, SEQL // 128, 256], BF16, kind="Internal").ap()

    with ExitStack() as es:
        fw = FW(nc, es)

        uid = [0]
        ALLOC = {}
        ALLOCN = {}

        def sb(name, shape, dt=F32, ctx=None):
            uid[0] += 1
            name = "%s_%d" % (name, uid[0])
            nb = int(np.prod(shape[1:])) * (2 if dt == BF16 else 4)
            cx = ctx or es
            if not hasattr(cx, "_bytes"):
                cx._bytes = 0
                cx._nm = name
                ALLOC[len(ALLOC)] = cx
            cx._bytes += nb
            return Tl((ctx or es).enter_context(nc.sbuf_tensor(name, list(shape), dt)), name)

        V, A_, PE_, PO = "dve", "act", "pe", "pool"

        def rs(out_ap, in_ap, rt, wt):
            fw.op(A_, lambda: nc.scalar.activation(out=out_ap, in_=in_ap, func=AF.Sqrt, bias=epst[:, 0:1], scale=1.0), list(rt) + [epst], wt)
            fw.op(V, lambda: nc.vector.reciprocal(out=out_ap, in_=out_ap), wt, wt)

        cst = sb("cst", [128, NCC])
        identF = cst[:, CC_ID:CC_ID + 128]
        blk64 = cst[:, CC_B64:CC_B64 + 128]
        blk32 = cst[:, CC_B32:CC_B32 + 128]
        o1024 = cst[:, CC_O1024:CC_O1024 + 128]
        o256 = cst[:, CC_O256:CC_O256 + 128]
        iotaN = cst[:, CC_IOTA:CC_IOTA + 128]
        iota16 = cst[:, CC_IOTA16:CC_IOTA16 + 16]
        cstb = sb("cstb", [128, 4, 128], BF16)
        ppt = sb("ppt", [128, L, NPC])
        modt = sb("modt", [128, L, 48, 2])
        sc1 = sb("sc1", [128, L, 2, 2, KC])
        lamt = sb("lamt", [128, L, 4])
        subg = sb("subg", [128, L])
        xt = sb("xt", [128, KC, TB])
        ps = [Tl(es.enter_context(nc.psum_tensor("ps%d" % i, [128, 512], F32)), "ps%d" % i) for i in range(8)]

        epst = sb("epst", [128, 1])
        fw.op(V, lambda: nc.vector.memset(epst[:, :], EPS), [], [epst])
        fw.dma("sp", cst[:, :], consts[:, :], writes=[cst])
        fw.dma("sp", ppt[:, :, :], pp.rearrange("l p n -> p l n"), writes=[ppt])
        fw.op(V, lambda: nc.vector.tensor_copy(out=cstb[:, 0, :], in_=identF), [cst], [cstb])
        fw.op(V, lambda: nc.vector.tensor_copy(out=cstb[:, 1, :], in_=cst[:, CC_OE:CC_OE + 128]), [cst], [cstb])
        fw.op(V, lambda: nc.vector.tensor_copy(out=cstb[:, 2, :], in_=cst[:, CC_OO:CC_OO + 128]), [cst], [cstb])
        fw.op(V, lambda: nc.vector.tensor_copy(out=cstb[:, 3, :], in_=iotaN), [cst], [cstb])
        identB, onesE, onesO, iotaB = cstb[:, 0, :], cstb[:, 1, :], cstb[:, 2, :], cstb[:, 3, :]

        import os
        sub = int(os.environ.get("SUB", "99"))
        with ExitStack() as ph:
            cdt = sb("cdt", [128, KC, 2], ctx=ph)
            sct = sb("sct", [128, KC, 2], ctx=ph)
            wm = [sb("wm%d" % i, [128, KC, 512], ctx=ph) for i in range(2)]
            dl = sb("dl", [128, L, 128], ctx=ph)
            dl2 = sb("dl2", [128, L, 2, 32], ctx=ph)
            dl3 = sb("dl3", [128, L, 2], ctx=ph)
            fw.dma("sp", cdt[:, :, :], condT[:, :, :], writes=[cdt])
            fw.op(A_, lambda: nc.scalar.activation(out=sct[:, :, :], in_=cdt[:, :, :], func=AF.Silu), [cdt], [sct])
            fw.dma("sp", dl[:, :, :], dlam.partition_broadcast(128), writes=[dl])
            for l in range(L if sub >= 1 else 0):
                for sl in range(12 if sub >= 2 else 0):
                    w = wm[sl % 2]
                    fw.dma("sp", w[:, :, :], w_mod[l].rearrange("(k p) n -> p k n", p=128)[:, :, sl * 512:(sl + 1) * 512],
                           writes=[w])
                    for oc4 in range(4):
                        oc = sl * 4 + oc4
                        for k in range(KC):
                            fw.op(PE_, lambda k=k, oc=oc, oc4=oc4, w=w: nc.tensor.matmul(
                                ps[0][:, oc * 2:oc * 2 + 2], lhsT=w[:, k, oc4 * 128:(oc4 + 1) * 128], rhs=sct[:, k, :],
                                start=(k == 0), stop=(k == KC - 1)), [w, sct], [ps[0]], inc=(k == KC - 1))
                if sub < 3:
                    continue
                fw.op(V, lambda l=l: nc.vector.tensor_tensor(
                    out=modt[:, l, :, :], in0=ps[0][:, 0:96].rearrange("p (a b) -> p a b", b=2),
                    in1=ppt[:, l, PC_BMOD:PC_BMOD + 48].unsqueeze(2).to_broadcast([128, 48, 2]), op=ALU.add),
                    [ps[0], ppt], [modt])
                for j, (vec, pc) in enumerate([(1, PC_N1G), (4, PC_N2G)]):
                    for cd in range(2):
                        fw.op(V, lambda l=l, j=j, vec=vec, pc=pc, cd=cd: nc.vector.scalar_tensor_tensor(
                            out=sc1[:, l, j, cd, :], in0=modt[:, l, vec * 8:(vec + 1) * 8, cd], scalar=1.0,
                            in1=ppt[:, l, pc:pc + 8], op0=ALU.add, op1=ALU.mult), [modt, ppt], [sc1])
                if sub < 4:
                    continue
                fw.op(V, lambda l=l: nc.vector.tensor_tensor(
                    out=dl2[:, l, :, :], in0=dl[:, l, :].rearrange("p (a b c) -> p a b c", a=2, b=2)[:, :, 0, :],
                    in1=dl[:, l, :].rearrange("p (a b c) -> p a b c", a=2, b=2)[:, :, 1, :], op=ALU.mult), [dl], [dl2])
                fw.op(V, lambda l=l: nc.vector.tensor_reduce(out=dl3[:, l, :], in_=dl2[:, l, :, :], axis=AX.X, op=ALU.add),
                      [dl2], [dl3])
                fw.op(A_, lambda l=l: nc.scalar.activation(out=dl3[:, l, :], in_=dl3[:, l, :], func=AF.Exp), [dl3], [dl3])
                lam_init = 0.8 - 0.6 * math.exp(-0.3 * l)
                fw.op(V, lambda l=l, li=lam_init: nc.vector.scalar_tensor_tensor(
                    out=lamt[:, l, 0:1], in0=dl3[:, l, 1:2], scalar=-li, in1=dl3[:, l, 0:1],
                    op0=ALU.add, op1=ALU.subtract), [dl3], [lamt])
                fw.op(V, lambda l=l, li=lam_init: nc.vector.tensor_scalar(
                    out=subg[:, l:l + 1], in0=ppt[:, l, PC_SUB:PC_SUB + 1], scalar1=(1.0 - li), scalar2=None,
                    op0=ALU.mult), [ppt], [subg])
            fw.barrier()

        WSC = Buf("wscratch")
        with ExitStack() as ph:
            cvt = [sb("cvt%d" % i, [128, 8, D], BF16, ctx=ph) for i in range(3)]
            ci_ = 0
            for l in range(nl):
                pvv = pv[l].rearrange("(n c) d -> c n d", c=128)
                for c0 in range(0, NEXP_C, 8):
                    for (src_, dst_) in [(puT[l, c0:c0 + 8], pubf[l, c0:c0 + 8]), (pvv[c0:c0 + 8], pvbf[l, c0:c0 + 8])]:
                        t_ = cvt[ci_ % 3]
                        ci_ += 1
                        fw.dma(PO, t_[:, :, :], src_.rearrange("c p n -> p c n"), writes=[t_])
                        fw.dma("sp", dst_.rearrange("c p n -> p c n"), t_[:, :, :], reads=[t_], writes=[WSC])
            fw.barrier()

        def norm_tmps(ctx, nb=0):
            return ([sb("nsq%d_%d" % (nb, i), [128, TB], ctx=ctx) for i in range(2)], sb("nrstd%d" % nb, [128, TB], ctx=ctx),
                    [sb("ntmp%d_%d" % (nb, i), [128, TB], ctx=ctx) for i in range(2)])

        def norm_mod(ctx, l, which, hb, nb, cond=0, xt=xt, P=None, tmps=None):
            sq, rstd, tmp = tmps if tmps is not None else norm_tmps(ctx, nb)
            P = P if P is not None else ps[1]
            for k in range(KC):
                s = sq[k % 2]
                fw.op(A_, lambda k=k, s=s: nc.scalar.activation(out=s[:, :], in_=xt[:, k, :], func=AF.Square), [xt], [s])
                fw.op(PE_, lambda k=k, s=s: nc.tensor.matmul(P[:, 0:TB], lhsT=o1024, rhs=s[:, :], start=(k == 0),
                                                             stop=(k == KC - 1)), [cst, s], [P], inc=True)
            rs(rstd[:, :], P[:, 0:TB], [P], [rstd])
            shv = 0 if which == 0 else 3
            for k in range(KC):
                t = tmp[k % 2]
                fw.op(V, lambda k=k, t=t: nc.vector.tensor_tensor(out=t[:, :], in0=xt[:, k, :], in1=rstd[:, :], op=ALU.mult),
                      [xt, rstd], [t])
                fw.op(A_, lambda k=k, t=t: nc.scalar.activation(
                    out=hb[:, k, :], in_=t[:, :], func=AF.Identity, scale=sc1[:, l, which, cond, k:k + 1],
                    bias=modt[:, l, shv * 8 + k, cond:cond + 1]), [t, sc1, modt], [hb])

        def load_w(q, tile_, src, **kw):
            fw.dma(q, tile_[:], src, writes=[tile_], **kw)

        def mixers(l, s):
            t0 = s * TB
            with ExitStack() as ph:
                hb = sb("hb", [128, KC, TB], BF16, ctx=ph)
                mixed = sb("mixed", [128, KC, TB], BF16, ctx=ph)
                wseg = [sb("wseg%d" % i, [128, KC, 256], BF16, ctx=ph) for i in range(3)]
                wo = sb("wo", [128, KC, D], BF16, ctx=ph)
                wsb = sb("wsb", [128, 4, 128], BF16, ctx=ph)
                rowt = sb("rowt", [128, 512], ctx=ph)
                bst = sb("bst", [128, 2, 128], ctx=ph)
                diagw = sb("diagw", [128, 2, 31, 128], BF16, ctx=ph)
                ypad = sb("ypad", [128, 2, TB + 30], BF16, ctx=ph)
                f = [sb("f%d" % i, [128, 512], ctx=ph) for i in range(8)]
                qb = sb("qb", [128, 2, 2, TB], BF16, ctx=ph)
                q2b = sb("q2b", [128, 2, 2, TB], BF16, ctx=ph)
                kb = sb("kb", [128, 2, TB], BF16, ctx=ph)
                kout = sb("kout", [128, 2, TB], ctx=ph)
                vout = sb("vout", [128, 2, 256], ctx=ph)
                vpad = sb("vpad", [128, 2, 4, 128], BF16, ctx=ph)
                pt = [sb("pt%d" % i, [128, 512], BF16, ctx=ph) for i in range(2)]
                gu = sb("gu", [128, 2, TB], ctx=ph)
                bnst = sb("bnst", [128, 2, 8], ctx=ph)

                norm_mod(ph, l, 0, hb, 0)
                load_w(PO, wo, w_out[l].rearrange("(k p) n -> p k n", p=128))
                load_w(PO, wsb, wsT[l])
                fw.dma("sp", rowt[:, :], rowp[l].partition_broadcast(128), writes=[rowt])
                fw.dma("sp", bst[:, :, :], bsT[l], writes=[bst])
                fw.op(V, lambda: nc.vector.memset(ypad[:, :, :], 0.0), [], [ypad])
                fw.op(V, lambda: nc.vector.memset(vpad[:, :, :, :], 0.0), [], [vpad])
                for cc in range(2):
                    for tap in range(31):
                        fw.op(V, lambda cc=cc, tap=tap, e=nc.vector: e.tensor_scalar(
                            out=diagw[:, cc, tap, :], in0=identB, scalar1=ppt[:, l, PC_CONVW + cc * 31 + tap:PC_CONVW + cc * 31 + tap + 1],
                            scalar2=None, op0=ALU.mult), [cstb, ppt], [diagw])

                msub = float(os.environ.get("MSUB", "99"))
                if msub < 2:
                    fw.barrier()
                    return
                wslot = [0]

                def seg_w(seg):
                    w = wseg[wslot[0] % 3]
                    wslot[0] += 1
                    load_w(PO, w, w_in[l].rearrange("(k p) n -> p k n", p=128)[:, :, seg * 256:(seg + 1) * 256])
                    return w

                def fm(w, cc, P, col=0):
                    for k in range(KC):
                        fw.op(PE_, lambda k=k: nc.tensor.matmul(P[:, col:col + TB], lhsT=w[:, k, cc * 128:(cc + 1) * 128],
                                                                rhs=hb[:, k, :], start=(k == 0), stop=(k == KC - 1)),
                              [w, hb], [P], inc=(k == KC - 1))

                def tm(w, tl, P, col=0):
                    for k in range(KC):
                        fw.op(PE_, lambda k=k: nc.tensor.matmul(P[:, col:col + 256], lhsT=hb[:, k, tl * 128:(tl + 1) * 128],
                                                                rhs=w[:, k, :], start=(k == 0), stop=(k == KC - 1)),
                              [w, hb], [P], inc=(k == KC - 1))

                wa = seg_w(0)
                wg = seg_w(1)
                for cc in range(2):
                    fm(wa, cc, ps[2], 0)
                    fm(wg, cc, ps[2], TB)
                    fw.op(A_, lambda: nc.scalar.activation(out=f[0][:, 0:TB], in_=ps[2][:, TB:2 * TB], func=AF.Sigmoid),
                          [ps[2]], [f[0]])
                    fw.op(V, lambda cc=cc: nc.vector.tensor_tensor(out=ypad[:, cc, 15:15 + TB], in0=ps[2][:, 0:TB],
                                                                    in1=f[0][:, 0:TB], op=ALU.mult), [ps[2], f[0]], [ypad])
                for cc in range(2):
                    for tap in range(31):
                        fw.op(PE_, lambda cc=cc, tap=tap: nc.tensor.matmul(
                            ps[3][:, cc * TB:(cc + 1) * TB], lhsT=diagw[:, cc, tap, :], rhs=ypad[:, cc, tap:tap + TB],
                            start=(tap == 0), stop=(tap == 30)), [diagw, ypad], [ps[3]], inc=(tap == 30))
                    fw.op(A_, lambda cc=cc: nc.scalar.activation(
                        out=f[1][:, cc * TB:(cc + 1) * TB], in_=ps[3][:, cc * TB:(cc + 1) * TB], func=AF.Identity,
                        bias=ppt[:, l, PC_CONVB + cc:PC_CONVB + cc + 1], scale=1.0), [ps[3], ppt], [f[1]])
                for cc in range(2):
                    fw.op(PE_, lambda cc=cc: nc.tensor.matmul(ps[2][:, 0:TB], lhsT=o256, rhs=f[1][:, cc * TB:(cc + 1) * TB],
                                                              start=(cc == 0), stop=(cc == 1)), [cst, f[1]], [ps[2]], inc=(cc == 1))
                for cc in range(2):
                    fw.op(V, lambda cc=cc: nc.vector.tensor_tensor(out=f[2][:, cc * TB:(cc + 1) * TB], in0=f[1][:, cc * TB:(cc + 1) * TB],
                                                                    in1=ps[2][:, 0:TB], op=ALU.subtract), [f[1], ps[2]], [f[2]])
                fw.op(A_, lambda: nc.scalar.activation(out=f[3][:, :], in_=f[2][:, :], func=AF.Square), [f[2]], [f[3]])
                for cc in range(2):
                    fw.op(PE_, lambda cc=cc: nc.tensor.matmul(ps[3][:, 0:TB], lhsT=o256, rhs=f[3][:, cc * TB:(cc + 1) * TB],
                                                              start=(cc == 0), stop=(cc == 1)), [cst, f[3]], [ps[3]], inc=(cc == 1))
                rs(f[4][:, 0:TB], ps[3][:, 0:TB], [ps[3]], [f[4]])
                for cc in range(2):
                    fw.op(V, lambda cc=cc: nc.vector.tensor_tensor(out=f[3][:, cc * TB:(cc + 1) * TB], in0=f[2][:, cc * TB:(cc + 1) * TB],
                                                                    in1=f[4][:, 0:TB], op=ALU.mult), [f[2], f[4]], [f[3]])
                    fw.op(A_, lambda cc=cc: nc.scalar.activation(
                        out=mixed[:, cc, :], in_=f[3][:, cc * TB:(cc + 1) * TB], func=AF.Silu,
                        scale=ppt[:, l, PC_CLNG + cc:PC_CLNG + cc + 1], bias=ppt[:, l, PC_CLNB + cc:PC_CLNB + cc + 1]),
                        [f[3], ppt], [mixed])

                if msub < 3:
                    fw.barrier()
                    return
                def qk_norm(w, cc, blk, gcol, dst_list, fout=None):
                    P = ps[4]
                    fm(w, cc, P, 0)
                    fw.op(A_, lambda: nc.scalar.activation(out=f[5][:, 0:TB], in_=P[:, 0:TB], func=AF.Square), [P], [f[5]])
                    fw.op(PE_, lambda: nc.tensor.matmul(P[:, TB:2 * TB], lhsT=blk, rhs=f[5][:, 0:TB], start=True, stop=True),
                          [cst, f[5]], [P])
                    rs(f[6][:, 0:TB], P[:, TB:2 * TB], [P], [f[6]])
                    for (gc, dst, tl_) in dst_list:
                        fw.op(V, lambda gc=gc, dst=dst: nc.vector.scalar_tensor_tensor(
                            out=dst, in0=P[:, 0:TB], scalar=ppt[:, l, gc:gc + 1], in1=f[6][:, 0:TB],
                            op0=ALU.mult, op1=ALU.mult), [P, ppt, f[6]], [tl_])

                def v_proj(w, outd):
                    for tl in range(2):
                        P = ps[5]
                        tm(w, tl, P, 0)
                        fw.op(A_, lambda tl=tl: nc.scalar.copy(out=vout[:, tl, :], in_=P[:, 0:256]), [P], [vout])
                        for par in range(2 if os.environ.get("VP", "1") == "1" else 0):
                            fw.op(A_, lambda tl=tl, par=par: nc.scalar.activation(
                                out=vpad[:, tl, :, :].rearrange("p (c r) n -> p c r n", r=2)[:, :, par, par * 64:par * 64 + 64],
                                in_=P[:, 0:256].rearrange("p (c r n) -> p c r n", c=2, r=2)[:, :, par, :], func=AF.Identity), [P], [vpad])
                    fw.dma("sp", outd[l, :, 2 * s:2 * s + 2, :], vout[:, :, :], reads=[vout])

                def attend(cc, qlist, scale):
                    outs = []
                    att = int(os.environ.get("ATT", "9"))
                    for qi, qt_ in enumerate(qlist):
                        OP = ps[6 + qi]
                        for kt in range(2):
                            SP = ps[(kt + 2 * qi) % 4]
                            p_ = pt[(kt + qi) % 2]
                            for par in range(2):
                                fw.op(PE_, lambda par=par, kt=kt, qt_=qt_, SP=SP: nc.tensor.matmul(
                                    SP[:, par * TB:(par + 1) * TB], lhsT=kb[:, cc, kt * 128:(kt + 1) * 128],
                                    rhs=qt_[:, par, cc, :], start=True, stop=True), [kb, qt_], [SP], inc=(par == 1))
                            fw.op(A_, lambda SP=SP, p_=p_: nc.scalar.activation(out=p_[:, :], in_=SP[:, :], func=AF.Exp, scale=scale),
                                  [SP], [p_])
                            if att < 2:
                                continue
                            for par in range(2):
                                fw.op(PE_, lambda par=par, kt=kt, p_=p_, OP=OP: nc.tensor.matmul(
                                    OP[:, 0:TB], lhsT=vpad[:, kt, 2 * cc + par, :], rhs=p_[:, par * TB:(par + 1) * TB],
                                    start=(kt == 0 and par == 0), stop=(kt == 1 and par == 1), skip_group_check=True),
                                    [vpad, p_], [OP], inc=False)
                            for par in range(2):
                                fw.op(PE_, lambda par=par, kt=kt, p_=p_, OP=OP: nc.tensor.matmul(
                                    OP[:, TB:2 * TB], lhsT=(onesE if par == 0 else onesO), rhs=p_[:, par * TB:(par + 1) * TB],
                                    start=False, stop=(kt == 1 and par == 1), skip_group_check=True),
                                    [cstb, p_], [OP], inc=(par == 1))
                        outs.append(OP)
                    return outs

                wqn = seg_w(2)
                wkn = seg_w(3)
                wvn = seg_w(4)
                for cc in range(2):
                    qk_norm(wqn, cc, blk64, None, [(PC_NAQE, qb[:, 0, cc, :], qb), (PC_NAQO, qb[:, 1, cc, :], qb)])
                    qk_norm(wkn, cc, blk64, None, [(PC_NAK, kout[:, cc, :], kout)])
                    fw.op(A_, lambda cc=cc: nc.scalar.copy(out=kb[:, cc, :], in_=kout[:, cc, :]), [kout], [kb])
                fw.dma("sp", nkT[l, :, :, t0:t0 + TB], kout[:, :, :], reads=[kout])
                if msub < 3.3:
                    fw.barrier()
                    return
                v_proj(wvn, nvo)
                if msub < 3.6:
                    fw.barrier()
                    return
                for cc in range(2):
                    (OP,) = attend(cc, [qb], 0.125)
                    if int(os.environ.get("ATT", "9")) < 3:
                        continue
                    fw.op(V, lambda OP=OP: nc.vector.reciprocal(out=f[7][:, 0:TB], in_=OP[:, TB:2 * TB]), [OP], [f[7]])
                    fw.op(V, lambda cc=cc, OP=OP: nc.vector.tensor_tensor(out=mixed[:, 2 + cc, :], in0=OP[:, 0:TB], in1=f[7][:, 0:TB],
                                                                           op=ALU.mult), [OP, f[7]], [mixed])

                if msub < 4:
                    fw.barrier()
                    return
                wqd = seg_w(5)
                wkd = seg_w(6)
                wvd = seg_w(7)
                for cc in range(2):
                    qk_norm(wqd, cc, blk32, None, [(PC_D1E, qb[:, 0, cc, :], qb), (PC_D1O, qb[:, 1, cc, :], qb),
                                                   (PC_D2E, q2b[:, 0, cc, :], q2b), (PC_D2O, q2b[:, 1, cc, :], q2b)])
                    qk_norm(wkd, cc, blk32, None, [(PC_DK, kout[:, cc, :], kout)])
                    fw.op(A_, lambda cc=cc: nc.scalar.copy(out=kb[:, cc, :], in_=kout[:, cc, :]), [kout], [kb])
                fw.dma("sp", dkT[l, :, :, t0:t0 + TB], kout[:, :, :], reads=[kout])
                v_proj(wvd, dvo)
                for cc in range(2):
                    O1, O2 = attend(cc, [qb, q2b], 32 ** -0.5)
                    fw.op(V, lambda O1=O1: nc.vector.reciprocal(out=f[7][:, 0:TB], in_=O1[:, TB:2 * TB]), [O1], [f[7]])
                    fw.op(V, lambda O2=O2: nc.vector.reciprocal(out=f[7][:, TB:2 * TB], in_=O2[:, TB:2 * TB]), [O2], [f[7]])
                    fw.op(V, lambda O1=O1: nc.vector.tensor_tensor(out=f[5][:, 0:TB], in0=O1[:, 0:TB], in1=f[7][:, 0:TB], op=ALU.mult),
                          [O1, f[7]], [f[5]])
                    fw.op(V, lambda O2=O2: nc.vector.tensor_tensor(out=f[5][:, TB:2 * TB], in0=O2[:, 0:TB], in1=f[7][:, TB:2 * TB],
                                                                    op=ALU.mult), [O2, f[7]], [f[5]])
                    fw.op(V, lambda: nc.vector.scalar_tensor_tensor(out=f[6][:, 0:TB], in0=f[5][:, TB:2 * TB], scalar=lamt[:, l, 0:1],
                                                                    in1=f[5][:, 0:TB], op0=ALU.mult, op1=ALU.add), [f[5], lamt], [f[6]])
                    fw.op(A_, lambda: nc.scalar.activation(out=f[6][:, TB:2 * TB], in_=f[6][:, 0:TB], func=AF.Square), [f[6]], [f[6]])
                    fw.op(PE_, lambda: nc.tensor.matmul(ps[4][:, 0:TB], lhsT=blk64, rhs=f[6][:, TB:2 * TB], start=True, stop=True),
                          [cst, f[6]], [ps[4]])
                    rs(f[7][:, 0:TB], ps[4][:, 0:TB], [ps[4]], [f[7]])
                    fw.op(V, lambda cc=cc: nc.vector.scalar_tensor_tensor(out=mixed[:, 4 + cc, :], in0=f[6][:, 0:TB], scalar=subg[:, l:l + 1],
                                                                           in1=f[7][:, 0:TB], op0=ALU.mult, op1=ALU.mult),
                          [f[6], subg, f[7]], [mixed])

                if msub < 5:
                    fw.barrier()
                    return
                wu = seg_w(8)
                wv = seg_w(9)
                for cc in range(2):
                    fm(wu, cc, ps[0], 0)
                    fw.op(A_, lambda cc=cc: nc.scalar.activation(out=gu[:, cc, :], in_=ps[0][:, 0:TB], func=AF.Gelu_apprx_tanh),
                          [ps[0]], [gu])
                fw.op(V, lambda: nc.vector.memset(vpad[:, :, :, :], 0.0), [], [vpad])
                for tl in range(2):
                    P = ps[1]
                    tm(wv, tl, P, 0)
                    fw.op(A_, lambda: nc.scalar.activation(out=f[0][:, 0:256], in_=P[:, 0:256], func=AF.Gelu_apprx_tanh), [P], [f[0]])
                    fw.op(V, lambda: nc.vector.bn_stats(out=bnst[:, 0, 0:6], in_=f[0][:, 0:256]), [f[0]], [bnst])
                    fw.op(V, lambda: nc.vector.bn_aggr(out=bnst[:, 1, 0:2], in_=bnst[:, 0, 0:6]), [bnst], [bnst])
                    rs(bnst[:, 1, 2:3], bnst[:, 1, 1:2], [bnst], [bnst])
                    fw.op(V, lambda: nc.vector.tensor_scalar(out=f[0][:, 256:512], in0=f[0][:, 0:256], scalar1=bnst[:, 1, 0:1],
                                                             scalar2=bnst[:, 1, 2:3], op0=ALU.subtract, op1=ALU.mult), [f[0], bnst], [f[0]])
                    fw.op(V, lambda: nc.vector.tensor_tensor(out=f[0][:, 0:256], in0=f[0][:, 256:512], in1=rowt[:, 0:256], op=ALU.mult),
                          [f[0], rowt], [f[0]])
                    for par in range(2):
                        fw.op(V, lambda tl=tl, par=par: nc.vector.tensor_tensor(
                            out=vpad[:, tl, :, :].rearrange("p (c r) n -> p c r n", r=2)[:, :, par, par * 64:par * 64 + 64],
                            in0=f[0][:, 0:256].rearrange("p (c r n) -> p c r n", c=2, r=2)[:, :, par, :],
                            in1=rowt[:, 256:512].rearrange("p (c r n) -> p c r n", c=2, r=2)[:, :, par, :], op=ALU.add),
                            [f[0], rowt], [vpad])
                for cc in range(2):
                    P = ps[2]
                    for tl in range(2):
                        for par in range(2):
                            fw.op(PE_, lambda tl=tl, par=par: nc.tensor.matmul(
                                P[:, tl * 128:(tl + 1) * 128], lhsT=vpad[:, tl, 2 * cc + par, :], rhs=wsb[:, 2 * cc + par, :],
                                start=(par == 0), stop=(par == 1)), [vpad, wsb], [P], inc=(par == 1))
                    fw.op(V, lambda cc=cc: nc.vector.tensor_tensor(
                        out=f[1][:, 0:TB].rearrange("p (a b) -> p a b", a=2), in0=P[:, 0:TB].rearrange("p (a b) -> p a b", a=2),
                        in1=bst[:, cc, :].unsqueeze(1).to_broadcast([128, 2, 128]), op=ALU.add), [P, bst], [f[1]])
                    fw.op(V, lambda cc=cc: nc.vector.tensor_tensor(out=mixed[:, 6 + cc, :], in0=f[1][:, 0:TB], in1=gu[:, cc, :], op=ALU.mult),
                          [f[1], gu], [mixed])

                if debug and l == 0:
                    dbgt = sb("dbgt", [128, KC, TB], ctx=ph)
                    fw.op(A_, lambda: nc.scalar.copy(out=dbgt[:, :, :], in_=mixed[:, :, :]), [mixed], [dbgt])
                    fw.dma("sp", dbg[:, :, :], dbgt[:, :, :], reads=[dbgt])
                if msub < 6:
                    fw.barrier()
                    return
                for j in range(KC):
                    P = ps[3 + (j % 2)]
                    for k in range(KC):
                        fw.op(PE_, lambda k=k, j=j, P=P: nc.tensor.matmul(P[:, 0:TB], lhsT=wo[:, k, j * 128:(j + 1) * 128], rhs=mixed[:, k, :],
                                                                          start=(k == 0), stop=(k == KC - 1)), [wo, mixed], [P], inc=(k == KC - 1))
                    fw.op(V, lambda j=j, P=P: nc.vector.scalar_tensor_tensor(
                        out=xt[:, j, :], in0=P[:, 0:TB], scalar=modt[:, l, 16 + j, 0:1], in1=xt[:, j, :], op0=ALU.mult, op1=ALU.add),
                        [P, modt, xt], [xt])
                fw.barrier()

        def peer(l, s, cond=0):
            with ExitStack() as ph:
                h2 = sb("h2", [128, KC, TB], BF16, ctx=ph)
                wqs = [sb("wqs%d" % i, [128, KC, 256], BF16, ctx=ph) for i in range(2)]
                skb = sb("skb", [128, 2, 128], BF16, ctx=ph)
                qT = sb("qT", [128, 16, TB], BF16, ctx=ph)
                ssb = sb("ssb", [128, 16, 128], ctx=ph)
                swk = sb("swk", [128, 16, 128], ctx=ph)
                sv = sb("sv", [128, 16, 16], ctx=ph)
                cwk_v = swk[:, :, :].rearrange("p (h t) n -> p h (t n)", t=2)
                oh_v = ssb[:, :, :].rearrange("p (h t) (a b) -> p h (t a) b", t=2, a=8)
                si = sb("si", [128, 16, 16], U32, ctx=ph)
                sif = sb("sif", [128, 16, 16], ctx=ph)
                cs = sb("cs", [128, 8, 256], ctx=ph)
                tv = sb("tv", [128, 8, 16], ctx=ph)
                tpos = sb("tpos", [128, 8, 16], U32, ctx=ph)
                tu = sb("tu", [128, 2, 8, 16], U32, ctx=ph)
                tf = sb("tf", [128, 2, 8, 16], ctx=ph)
                trip = sb("trip", [128, 3, 8, 16], ctx=ph)
                tripT = sb("tripT", [128, 3, 128], ctx=ph)
                zz = sb("zz", [128, 8, 2], ctx=ph)
                NTS = 16
                Lhs = [sb("Lh%d" % i, [128, NTS, 128], BF16, ctx=ph) for i in range(2)]
                Rhs = [sb("Rh%d" % i, [128, NTS, 128], BF16, ctx=ph) for i in range(2)]
                Gall = sb("Gall", [128, TB, 128], BF16, ctx=ph)
                NSL = 4
                pus = [sb("pus%d" % i, [128, KC * 128], BF16, ctx=ph) for i in range(NSL)]
                pvs = [sb("pvs%d" % i, [128, D], BF16, ctx=ph) for i in range(NSL)]

                norm_mod(ph, l, 1, h2, 1, cond)
                load_w(PO, skb, skT[l])
                for jp in range(8):
                    w = wqs[jp % 2]
                    load_w(PO, w, wq[l].rearrange("(k p) n -> p k n", p=128)[:, :, jp * 256:(jp + 1) * 256])
                    P = ps[jp % 2]
                    for jj in range(2):
                        for k in range(KC):
                            fw.op(PE_, lambda k=k, jj=jj, w=w, P=P: nc.tensor.matmul(
                                P[:, jj * TB:(jj + 1) * TB], lhsT=w[:, k, jj * 128:(jj + 1) * 128], rhs=h2[:, k, :],
                                start=(k == 0), stop=(k == KC - 1)), [w, h2], [P], inc=(k == KC - 1))
                    fw.op(A_, lambda jp=jp, P=P: nc.scalar.copy(out=qT[:, 2 * jp:2 * jp + 2, :],
                                                                  in_=P[:, :].rearrange("p (a b) -> p a b", a=2)), [P], [qT])
                for tl in range(2):
                    tok = slice(tl * 128, (tl + 1) * 128)
                    for j in range(16):
                        P = ps[2 + j // 4]
                        fw.op(PE_, lambda j=j, P=P: nc.tensor.matmul(P[:, (j % 4) * 128:(j % 4 + 1) * 128], lhsT=qT[:, j, tok],
                                                                     rhs=skb[:, j % 2, :], start=True, stop=True), [qT, skb], [P],
                              inc=(j % 4 == 3))
                    for b4 in range(4):
                        fw.op(A_, lambda b4=b4: nc.scalar.copy(out=ssb[:, 4 * b4:4 * b4 + 4, :],
                                                                 in_=ps[2 + b4][:, :].rearrange("p (a b) -> p a b", a=4)), [ps[2 + b4]], [ssb])
                    for j in range(16):
                        fw.op(V, lambda j=j: nc.vector.max(out=sv[:, j, 0:8], in_=ssb[:, j, :]), [ssb], [sv])
                    for j in range(16):
                        fw.op(V, lambda j=j: nc.vector.max_index(out=si[:, j, 0:8], in_max=sv[:, j, 0:8], in_values=ssb[:, j, :]),
                              [sv, ssb], [si])
                    for j in range(16):
                        fw.op(V, lambda j=j: nc.vector.match_replace(out=swk[:, j, :], in_to_replace=sv[:, j, 0:8],
                                                                     in_values=ssb[:, j, :], imm_value=-1e30), [sv, ssb], [swk])
                    for j in range(16):
                        fw.op(V, lambda j=j: nc.vector.max(out=sv[:, j, 8:16], in_=swk[:, j, :]), [swk], [sv])
                    for j in range(16):
                        fw.op(V, lambda j=j: nc.vector.max_index(out=si[:, j, 8:16], in_max=sv[:, j, 8:16], in_values=swk[:, j, :]),
                              [sv, swk], [si])
                    fw.op(V, lambda: nc.vector.tensor_copy(out=sif[:, :, :], in_=si[:, :, :]), [si], [sif])
                    sv4 = sv[:, :, :].rearrange("p (h t) a -> p h t a", t=2)
                    fw.op(V, lambda: nc.vector.tensor_tensor(
                        out=cs[:, :, :].rearrange("p h (a b) -> p h a b", a=16),
                        in0=sv4[:, :, 0, :].unsqueeze(3).to_broadcast([128, 8, 16, 16]),
                        in1=sv4[:, :, 1, :].unsqueeze(2).to_broadcast([128, 8, 16, 16]), op=ALU.add), [sv], [cs])
                    for h in range(8):
                        fw.op(V, lambda h=h: nc.vector.max(out=tv[:, h, 0:8], in_=cs[:, h, :]), [cs], [tv])
                    for h in range(8):
                        fw.op(V, lambda h=h: nc.vector.max_index(out=tpos[:, h, 0:8], in_max=tv[:, h, 0:8], in_values=cs[:, h, :]),
                              [tv, cs], [tpos])
                    for h in range(8):
                        fw.op(V, lambda h=h: nc.vector.match_replace(out=cwk_v[:, h, :], in_to_replace=tv[:, h, 0:8],
                                                                     in_values=cs[:, h, :], imm_value=-1e30), [tv, cs], [swk])
                    for h in range(8):
                        fw.op(V, lambda h=h: nc.vector.max(out=tv[:, h, 8:16], in_=cwk_v[:, h, :]), [swk], [tv])
                    for h in range(8):
                        fw.op(V, lambda h=h: nc.vector.max_index(out=tpos[:, h, 8:16], in_max=tv[:, h, 8:16], in_values=cwk_v[:, h, :]),
                              [tv, swk], [tpos])
                    fw.op(V, lambda: nc.vector.tensor_tensor(out=trip[:, 2, :, :], in0=tv[:, :, :],
                                                             in1=tv[:, :, 0:1].to_broadcast([128, 8, 16]), op=ALU.subtract), [tv], [trip])
                    fw.op(A_, lambda: nc.scalar.activation(out=trip[:, 2, :, :], in_=trip[:, 2, :, :], func=AF.Exp), [trip], [trip])
                    fw.op(V, lambda: nc.vector.tensor_reduce(out=zz[:, :, 0], in_=trip[:, 2, :, :], axis=AX.X, op=ALU.add), [trip], [zz])
                    fw.op(V, lambda: nc.vector.reciprocal(out=zz[:, :, 1], in_=zz[:, :, 0]), [zz], [zz])
                    fw.op(V, lambda: nc.vector.tensor_tensor(out=trip[:, 2, :, :], in0=trip[:, 2, :, :],
                                                             in1=zz[:, :, 1:2].to_broadcast([128, 8, 16]), op=ALU.mult), [trip, zz], [trip])
                    fw.op(V, lambda: nc.vector.tensor_single_scalar(out=tu[:, 0, :, :], in_=tpos[:, :, :], scalar=4,
                                                                    op=ALU.logical_shift_right), [tpos], [tu])
                    fw.op(V, lambda: nc.vector.tensor_single_scalar(out=tu[:, 1, :, :], in_=tpos[:, :, :], scalar=15,
                                                                    op=ALU.bitwise_and), [tpos], [tu])
                    fw.op(V, lambda: nc.vector.tensor_copy(out=tf[:, :, :, :], in_=tu[:, :, :, :]), [tu], [tf])
                    sif4 = sif[:, :, :].rearrange("p (h t) a -> p h t a", t=2)
                    for w_ in range(2):
                        fw.op(V, lambda w_=w_: nc.vector.tensor_tensor(
                            out=oh_v[:, :, :, :], in0=iota16.unsqueeze(1).unsqueeze(1).to_broadcast([128, 8, 16, 16]),
                            in1=tf[:, w_, :, :].unsqueeze(3).to_broadcast([128, 8, 16, 16]), op=ALU.is_equal), [cst, tf], [ssb])
                        fw.op(V, lambda w_=w_: nc.vector.tensor_tensor(
                            out=oh_v[:, :, :, :], in0=oh_v[:, :, :, :],
                            in1=sif4[:, :, w_, :].unsqueeze(2).to_broadcast([128, 8, 16, 16]), op=ALU.mult), [ssb, sif], [ssb])
                        fw.op(V, lambda w_=w_: nc.vector.tensor_reduce(out=trip[:, w_, :, :], in_=oh_v[:, :, :, :], axis=AX.X, op=ALU.add),
                              [ssb], [trip])
                    P = ps[6]
                    for w_ in range(3):
                        fw.op(PE_, lambda w_=w_: nc.tensor.transpose(out=P[:, w_ * 128:(w_ + 1) * 128],
                                                                     in_=trip[:, w_, :, :].rearrange("p h k -> p (h k)"), identity=identF),
                              [trip, cst], [P], inc=(w_ == 2))
                    fw.op(A_, lambda: nc.scalar.copy(out=tripT[:, :, :], in_=P[:, 0:384].rearrange("p (a b) -> p a b", a=3)), [P], [tripT])
                    for t4 in range(128 // NTS):
                        tsl = slice(t4 * NTS, (t4 + 1) * NTS)
                        Lh = Lhs[t4 % 2]
                        Rh = Rhs[t4 % 2]
                        for tt_ in range(NTS):
                            tg = t4 * NTS + tt_
                            fw.op(V, lambda tt_=tt_, tg=tg, Lh=Lh: nc.vector.tensor_scalar(
                                out=Lh[:, tt_, :], in0=iotaN, scalar1=tripT[:, 0, tg:tg + 1], scalar2=None, op0=ALU.is_equal),
                                [cst, tripT], [Lh], disjoint=(tt_ > 0))
                            fw.op(V, lambda tt_=tt_, tg=tg, Rh=Rh: nc.vector.tensor_scalar(
                                out=Rh[:, tt_, :], in0=iotaN, scalar1=tripT[:, 1, tg:tg + 1], scalar2=tripT[:, 2, tg:tg + 1],
                                op0=ALU.is_equal, op1=ALU.mult), [cst, tripT], [Rh], disjoint=(tt_ > 0))
                        for g4 in range(NTS // 4):
                            P = ps[g4 % 2]
                            for u in range(4):
                                tt_ = g4 * 4 + u
                                fw.op(PE_, lambda tt_=tt_, u=u, P=P, Lh=Lh, Rh=Rh: nc.tensor.matmul(P[:, u * 128:(u + 1) * 128], lhsT=Lh[:, tt_, :],
                                                                                      rhs=Rh[:, tt_, :], start=True, stop=True),
                                      [Lh, Rh], [P], inc=(u == 3))
                            tb_ = tl * 128 + t4 * NTS + g4 * 4
                            fw.op(A_, lambda tb_=tb_, P=P: nc.scalar.copy(out=Gall[:, tb_:tb_ + 4, :],
                                                                          in_=P[:, :].rearrange("p (a b) -> p a b", a=4)), [P], [Gall])
                Aslot = [Buf("aslot%d" % i) for i in range(4)]
                ga = [sb("gax%d" % i, [128, TB], BF16, ctx=ph) for i in range(3)]
                wgt = [sb("wgx%d" % i, [128, TB], BF16, ctx=ph) for i in range(3)]

                def a_part(c):
                    pu_ = pus[c % NSL]
                    pv_ = pvs[c % NSL]
                    fw.dma("sp", pu_[:, :], pubf[l, c], reads=[WSC], writes=[pu_])
                    fw.dma("sp", pv_[:, :], pvbf[l, c], reads=[WSC], writes=[pv_])
                    P = ps[c % 4]
                    col = 0
                    wr = [Aslot[c % 4]] + ([P] if c < 4 else [])
                    for k in range(KC):
                        fw.op(PE_, lambda k=k, pu_=pu_, P=P, col=col: nc.tensor.matmul(
                            P[:, col:col + TB], lhsT=pu_[:, k * 128:(k + 1) * 128], rhs=h2[:, k, :], start=(k == 0), stop=(k == KC - 1)),
                            [pu_, h2], wr, inc=(k == KC - 1))

                def o_part(c):
                    pv_ = pvs[c % NSL]
                    P = ps[c % 4]
                    col = 0
                    g_ = ga[c % 3]
                    w_ = wgt[c % 3]
                    fw.op(A_, lambda g_=g_, P=P, col=col: nc.scalar.activation(out=g_[:, :], in_=P[:, col:col + TB], func=AF.Gelu_apprx_tanh),
                          [Aslot[c % 4]], [g_])
                    fw.op(V, lambda g_=g_, w_=w_, c=c: nc.vector.tensor_tensor(out=w_[:, :], in0=g_[:, :], in1=Gall[:, :, c], op=ALU.mult),
                          [g_, Gall], [w_])
                    for j in range(KC):
                        OPS = ps[4 + j // 2]
                        fw.op(PE_, lambda j=j, OPS=OPS, pv_=pv_, w_=w_, c=c: nc.tensor.matmul(
                            OPS[:, (j % 2) * TB:(j % 2 + 1) * TB], lhsT=pv_[:, j * 128:(j + 1) * 128], rhs=w_[:, :],
                            start=(c == 0 and j % 2 == 0), stop=(c == NEXP_C - 1), skip_group_check=True), [pv_, w_], [OPS], inc=(j == KC - 1))

                LOOK = int(os.environ.get("LOOK", "2"))
                for c in range(min(LOOK, NEXP_C)):
                    a_part(c)
                for c in range(NEXP_C):
                    if LOOK == 0:
                        a_part(c)
                    elif c + LOOK < NEXP_C:
                        a_part(c + LOOK)
                    o_part(c)
                for j in range(KC):
                    OPS = ps[4 + j // 2]
                    fw.op(V, lambda j=j, OPS=OPS: nc.vector.scalar_tensor_tensor(
                        out=xt[:, j, :], in0=OPS[:, (j % 2) * TB:(j % 2 + 1) * TB], scalar=modt[:, l, 40 + j, cond:cond + 1], in1=xt[:, j, :],
                        op0=ALU.mult, op1=ALU.add), [OPS, modt, xt], [xt])
                fw.barrier()


        def peer_pass(l, nblk, cond, xsrc_of, xdst, SBx):
            with ExitStack() as ph:
                xts = [sb("pxt%d" % i, [128, KC, TB], ctx=ph) for i in range(2)]
                h2s = [sb("ph2%d" % i, [128, KC, TB], BF16, ctx=ph) for i in range(2)]
                tTs = [[sb("ptT%d%d" % (i, t), [128, 3, 128], ctx=ph) for t in range(2)] for i in range(2)]
                ntm = norm_tmps(ph, 7)
                wqs = [sb("wqs%d" % i, [128, KC, 256], BF16, ctx=ph) for i in range(2)]
                skb = sb("skb", [128, 2, 128], BF16, ctx=ph)
                qT = sb("qT", [128, 16, TB], BF16, ctx=ph)
                ssb = sb("ssb", [128, 16, 128], ctx=ph)
                swk = sb("swk", [128, 16, 128], ctx=ph)
                sv = sb("sv", [128, 16, 16], ctx=ph)
                cwk_v = swk[:, :, :].rearrange("p (h t) n -> p h (t n)", t=2)
                oh_v = ssb[:, :, :].rearrange("p (h t) (a b) -> p h (t a) b", t=2, a=8)
                si = sb("si", [128, 16, 16], U32, ctx=ph)
                sif = sb("sif", [128, 16, 16], ctx=ph)
                cs = sb("cs", [128, 8, 256], ctx=ph)
                tv = sb("tv", [128, 8, 16], ctx=ph)
                tpos = sb("tpos", [128, 8, 16], U32, ctx=ph)
                tu = sb("tu", [128, 2, 8, 16], U32, ctx=ph)
                tf = sb("tf", [128, 2, 8, 16], ctx=ph)
                trip = sb("trip", [128, 3, 8, 16], ctx=ph)
                zz = sb("zz", [128, 8, 2], ctx=ph)
                NTS = 8
                Lhs = [sb("Lh%d" % i, [128, NTS, 128], BF16, ctx=ph) for i in range(2)]
                Rhs = [sb("Rh%d" % i, [128, NTS, 128], BF16, ctx=ph) for i in range(2)]
                Gall = sb("Gall", [128, TB, 128], BF16, ctx=ph)
                NSL = 4
                pus = [sb("pus%d" % i, [128, KC * 128], BF16, ctx=ph) for i in range(NSL)]
                pvs = [sb("pvs%d" % i, [128, D], BF16, ctx=ph) for i in range(NSL)]
                ga = [sb("gax%d" % i, [128, TB], BF16, ctx=ph) for i in range(3)]
                wgt = [sb("wgx%d" % i, [128, TB], BF16, ctx=ph) for i in range(3)]
                PB = ps[3]
                load_w(PO, skb, skT[l])
                sv4 = sv[:, :, :].rearrange("p (h t) a -> p h t a", t=2)
                sif4 = sif[:, :, :].rearrange("p (h t) a -> p h t a", t=2)

                def pre(s):
                    xt_ = xts[s % 2]
                    h2 = h2s[s % 2]
                    t0 = s * TB
                    fw.dma("sp", xt_[:, :, :], xsrc_of(s), reads=[SBx], writes=[xt_])
                    norm_mod(ph, l, 1, h2, 1, cond, xt=xt_, P=PB, tmps=ntm)
                    yield
                    for jp in range(8):
                        w = wqs[jp % 2]
                        load_w(PO, w, wq[l].rearrange("(k p) n -> p k n", p=128)[:, :, jp * 256:(jp + 1) * 256])
                        for jj in range(2):
                            for k in range(KC):
                                fw.op(PE_, lambda k=k, jj=jj, w=w: nc.tensor.matmul(
                                    PB[:, jj * TB:(jj + 1) * TB], lhsT=w[:, k, jj * 128:(jj + 1) * 128], rhs=h2[:, k, :],
                                    start=(k == 0), stop=(k == KC - 1)), [w, h2], [PB], inc=(k == KC - 1))
                        fw.op(A_, lambda jp=jp: nc.scalar.copy(out=qT[:, 2 * jp:2 * jp + 2, :],
                                                                 in_=PB[:, :].rearrange("p (a b) -> p a b", a=2)), [PB], [qT])
                        yield
                    for tl in range(2):
                        tok = slice(tl * 128, (tl + 1) * 128)
                        tripT = tTs[s % 2][tl]
                        for b4 in range(4):
                            for u in range(4):
                                j = b4 * 4 + u
                                fw.op(PE_, lambda j=j, u=u: nc.tensor.matmul(PB[:, u * 128:(u + 1) * 128], lhsT=qT[:, j, tok],
                                                                             rhs=skb[:, j % 2, :], start=True, stop=True), [qT, skb], [PB],
                                      inc=(u == 3))
                            fw.op(A_, lambda b4=b4: nc.scalar.copy(out=ssb[:, 4 * b4:4 * b4 + 4, :],
                                                                     in_=PB[:, :].rearrange("p (a b) -> p a b", a=4)), [PB], [ssb])
                            yield
                        for j in range(16):
                            fw.op(V, lambda j=j: nc.vector.max(out=sv[:, j, 0:8], in_=ssb[:, j, :]), [ssb], [sv])
                        yield
                        for j in range(16):
                            fw.op(V, lambda j=j: nc.vector.max_index(out=si[:, j, 0:8], in_max=sv[:, j, 0:8], in_values=ssb[:, j, :]),
                                  [sv, ssb], [si])
                        yield
                        for j in range(16):
                            fw.op(V, lambda j=j: nc.vector.match_replace(out=swk[:, j, :], in_to_replace=sv[:, j, 0:8],
                                                                         in_values=ssb[:, j, :], imm_value=-1e30), [sv, ssb], [swk])
                        yield
                        for j in range(16):
                            fw.op(V, lambda j=j: nc.vector.max(out=sv[:, j, 8:16], in_=swk[:, j, :]), [swk], [sv])
                        yield
                        for j in range(16):
                            fw.op(V, lambda j=j: nc.vector.max_index(out=si[:, j, 8:16], in_max=sv[:, j, 8:16], in_values=swk[:, j, :]),
                                  [sv, swk], [si])
                        fw.op(V, lambda: nc.vector.tensor_copy(out=sif[:, :, :], in_=si[:, :, :]), [si], [sif])
                        fw.op(V, lambda: nc.vector.tensor_tensor(
                            out=cs[:, :, :].rearrange("p h (a b) -> p h a b", a=16),
                            in0=sv4[:, :, 0, :].unsqueeze(3).to_broadcast([128, 8, 16, 16]),
                            in1=sv4[:, :, 1, :].unsqueeze(2).to_broadcast([128, 8, 16, 16]), op=ALU.add), [sv], [cs])
                        yield
                        for h in range(8):
                            fw.op(V, lambda h=h: nc.vector.max(out=tv[:, h, 0:8], in_=cs[:, h, :]), [cs], [tv])
                        for h in range(8):
                            fw.op(V, lambda h=h: nc.vector.max_index(out=tpos[:, h, 0:8], in_max=tv[:, h, 0:8], in_values=cs[:, h, :]),
                                  [tv, cs], [tpos])
                        yield
                        for h in range(8):
                            fw.op(V, lambda h=h: nc.vector.match_replace(out=cwk_v[:, h, :], in_to_replace=tv[:, h, 0:8],
                                                                         in_values=cs[:, h, :], imm_value=-1e30), [tv, cs], [swk])
                        for h in range(8):
                            fw.op(V, lambda h=h: nc.vector.max(out=tv[:, h, 8:16], in_=cwk_v[:, h, :]), [swk], [tv])
                        yield
                        for h in range(8):
                            fw.op(V, lambda h=h: nc.vector.max_index(out=tpos[:, h, 8:16], in_max=tv[:, h, 8:16], in_values=cwk_v[:, h, :]),
                                  [tv, swk], [tpos])
                        fw.op(V, lambda: nc.vector.tensor_tensor(out=trip[:, 2, :, :], in0=tv[:, :, :],
                                                                 in1=tv[:, :, 0:1].to_broadcast([128, 8, 16]), op=ALU.subtract), [tv], [trip])
                        fw.op(A_, lambda: nc.scalar.activation(out=trip[:, 2, :, :], in_=trip[:, 2, :, :], func=AF.Exp), [trip], [trip])
                        fw.op(V, lambda: nc.vector.tensor_reduce(out=zz[:, :, 0], in_=trip[:, 2, :, :], axis=AX.X, op=ALU.add), [trip], [zz])
                        fw.op(V, lambda: nc.vector.reciprocal(out=zz[:, :, 1], in_=zz[:, :, 0]), [zz], [zz])
                        fw.op(V, lambda: nc.vector.tensor_tensor(out=trip[:, 2, :, :], in0=trip[:, 2, :, :],
                                                                 in1=zz[:, :, 1:2].to_broadcast([128, 8, 16]), op=ALU.mult), [trip, zz], [trip])
                        yield
                        fw.op(V, lambda: nc.vector.tensor_single_scalar(out=tu[:, 0, :, :], in_=tpos[:, :, :], scalar=4,
                                                                        op=ALU.logical_shift_right), [tpos], [tu])
                        fw.op(V, lambda: nc.vector.tensor_single_scalar(out=tu[:, 1, :, :], in_=tpos[:, :, :], scalar=15,
                                                                        op=ALU.bitwise_and), [tpos], [tu])
                        fw.op(V, lambda: nc.vector.tensor_copy(out=tf[:, :, :, :], in_=tu[:, :, :, :]), [tu], [tf])
                        for w_ in range(2):
                            fw.op(V, lambda w_=w_: nc.vector.tensor_tensor(
                                out=oh_v[:, :, :, :], in0=iota16.unsqueeze(1).unsqueeze(1).to_broadcast([128, 8, 16, 16]),
                                in1=tf[:, w_, :, :].unsqueeze(3).to_broadcast([128, 8, 16, 16]), op=ALU.is_equal), [cst, tf], [ssb])
                            fw.op(V, lambda w_=w_: nc.vector.tensor_tensor(
                                out=oh_v[:, :, :, :], in0=oh_v[:, :, :, :],
                                in1=sif4[:, :, w_, :].unsqueeze(2).to_broadcast([128, 8, 16, 16]), op=ALU.mult), [ssb, sif], [ssb])
                            fw.op(V, lambda w_=w_: nc.vector.tensor_reduce(out=trip[:, w_, :, :], in_=oh_v[:, :, :, :], axis=AX.X, op=ALU.add),
                                  [ssb], [trip])
                            yield
                        for w_ in range(3):
                            fw.op(PE_, lambda w_=w_: nc.tensor.transpose(out=PB[:, w_ * 128:(w_ + 1) * 128],
                                                                         in_=trip[:, w_, :, :].rearrange("p h k -> p (h k)"), identity=identF),
                                  [trip, cst], [PB], inc=(w_ == 2))
                        fw.op(A_, lambda tripT=tripT: nc.scalar.copy(out=tripT[:, :, :], in_=PB[:, 0:384].rearrange("p (a b) -> p a b", a=3)),
                              [PB], [tripT])
                        yield

                def gbuild(s):
                    gi = 0
                    for tl in range(2):
                        tripT = tTs[s % 2][tl]
                        for t4 in range(128 // NTS):
                            Lh = Lhs[gi % 2]
                            Rh = Rhs[gi % 2]
                            for tt_ in range(NTS):
                                tg = t4 * NTS + tt_
                                fw.op(V, lambda tt_=tt_, tg=tg, Lh=Lh, tripT=tripT: nc.vector.tensor_scalar(
                                    out=Lh[:, tt_, :], in0=iotaN, scalar1=tripT[:, 0, tg:tg + 1], scalar2=None, op0=ALU.is_equal),
                                    [cst, tripT], [Lh], disjoint=(tt_ > 0))
                                fw.op(V, lambda tt_=tt_, tg=tg, Rh=Rh, tripT=tripT: nc.vector.tensor_scalar(
                                    out=Rh[:, tt_, :], in0=iotaN, scalar1=tripT[:, 1, tg:tg + 1], scalar2=tripT[:, 2, tg:tg + 1],
                                    op0=ALU.is_equal, op1=ALU.mult), [cst, tripT], [Rh], disjoint=(tt_ > 0))
                            for g4 in range(NTS // 4):
                                P = ps[gi % 2 * 2 + g4 % 2] if False else ps[(2 * gi + g4) % 3]
                                for u in range(4):
                                    tt_ = g4 * 4 + u
                                    fw.op(PE_, lambda tt_=tt_, u=u, P=P, Lh=Lh, Rh=Rh: nc.tensor.matmul(P[:, u * 128:(u + 1) * 128], lhsT=Lh[:, tt_, :],
                                                                                                          rhs=Rh[:, tt_, :], start=True, stop=True),
                                          [Lh, Rh], [P], inc=(u == 3))
                                tb_ = tl * 128 + t4 * NTS + g4 * 4
                                fw.op(A_, lambda tb_=tb_, P=P: nc.scalar.copy(out=Gall[:, tb_:tb_ + 4, :],
                                                                              in_=P[:, :].rearrange("p (a b) -> p a b", a=4)), [P], [Gall])
                            gi += 1

                def sweep(s, gen):
                    xt_ = xts[s % 2]
                    h2 = h2s[s % 2]
                    Aslot = [Buf("aslot%d" % i) for i in range(3)]

                    def a_part(c):
                        pu_ = pus[c % NSL]
                        pv_ = pvs[c % NSL]
                        fw.dma("sp", pu_[:, :], pubf[l, c], reads=[WSC], writes=[pu_])
                        fw.dma("sp", pv_[:, :], pvbf[l, c], reads=[WSC], writes=[pv_])
                        P = ps[c % 3]
                        wr = [Aslot[c % 3]] + ([P] if c < 3 else [])
                        for k in range(KC):
                            fw.op(PE_, lambda k=k, pu_=pu_, P=P: nc.tensor.matmul(
                                P[:, 0:TB], lhsT=pu_[:, k * 128:(k + 1) * 128], rhs=h2[:, k, :], start=(k == 0), stop=(k == KC - 1)),
                                [pu_, h2], wr, inc=(k == KC - 1))

                    def o_part(c):
                        pv_ = pvs[c % NSL]
                        P = ps[c % 3]
                        g_ = ga[c % 3]
                        w_ = wgt[c % 3]
                        fw.op(A_, lambda g_=g_, P=P: nc.scalar.activation(out=g_[:, :], in_=P[:, 0:TB], func=AF.Gelu_apprx_tanh),
                              [Aslot[c % 3]], [g_])
                        if c % 3 == 2:
                            fw.op(PO, lambda g_=g_, w_=w_, c=c: nc.gpsimd.tensor_tensor(out=w_[:, :], in0=g_[:, :], in1=Gall[:, :, c], op=ALU.mult),
                                  [g_, Gall], [w_])
                        else:
                            fw.op(V, lambda g_=g_, w_=w_, c=c: nc.vector.tensor_tensor(out=w_[:, :], in0=g_[:, :], in1=Gall[:, :, c], op=ALU.mult),
                                  [g_, Gall], [w_])
                        for j in range(KC):
                            OPS = ps[4 + j // 2]
                            fw.op(PE_, lambda j=j, OPS=OPS, pv_=pv_, w_=w_, c=c: nc.tensor.matmul(
                                OPS[:, (j % 2) * TB:(j % 2 + 1) * TB], lhsT=pv_[:, j * 128:(j + 1) * 128], rhs=w_[:, :],
                                start=(c == 0 and j % 2 == 0), stop=(c == NEXP_C - 1), skip_group_check=True), [pv_, w_], [OPS], inc=(j == KC - 1))

                    NSTEP = 60
                    done = [0]

                    def advance(upto):
                        while gen is not None and done[0] < upto:
                            try:
                                next(gen)
                            except StopIteration:
                                done[0] = 10 ** 9
                                return
                            done[0] += 1
                    for c in range(2):
                        a_part(c)
                    for c in range(NEXP_C):
                        if c + 2 < NEXP_C:
                            a_part(c + 2)
                        o_part(c)
                        advance((c + 1) * NSTEP // NEXP_C)
                    advance(10 ** 8)
                    for j in range(KC):
                        OPS = ps[4 + j // 2]
                        fw.op(V, lambda j=j, OPS=OPS: nc.vector.scalar_tensor_tensor(
                            out=xt_[:, j, :], in0=OPS[:, (j % 2) * TB:(j % 2 + 1) * TB], scalar=modt[:, l, 40 + j, cond:cond + 1], in1=xt_[:, j, :],
                            op0=ALU.mult, op1=ALU.add), [OPS, modt, xt_], [xt_])
                    fw.dma("sp", xdst[:, :, s * TB:(s + 1) * TB], xt_[:, :, :], reads=[xt_], writes=[SBx])

                g0 = pre(0)
                for _ in g0:
                    pass
                for s in range(nblk):
                    gbuild(s)
                    sweep(s, pre(s + 1) if s + 1 < nblk else None)
                fw.barrier()
        def load_seg(l, seg, w):
            load_w(PO, w, w_in[l].rearrange("(k p) n -> p k n", p=128)[:, :, seg * 256:(seg + 1) * 256])
            return w

        def pfm(w, hb, cc, P, col=0):
            for k in range(KC):
                fw.op(PE_, lambda k=k: nc.tensor.matmul(P[:, col:col + TB], lhsT=w[:, k, cc * 128:(cc + 1) * 128],
                                                        rhs=hb[:, k, :], start=(k == 0), stop=(k == KC - 1)),
                      [w, hb], [P], inc=(k == KC - 1))

        def ptm(w, hb, tl, P, col=0):
            for k in range(KC):
                fw.op(PE_, lambda k=k: nc.tensor.matmul(P[:, col:col + 256], lhsT=hb[:, k, tl * 128:(tl + 1) * 128],
                                                        rhs=w[:, k, :], start=(k == 0), stop=(k == KC - 1)),
                      [w, hb], [P], inc=(k == KC - 1))

        def rope(src_t, src_ap, tA, tB, cs_, sn_, dst_ap, dst_t, P):
            fw.op(PE_, lambda: nc.tensor.matmul(P[:, 0:TB], lhsT=cst[:, CC_PSW:CC_PSW + 128], rhs=src_ap, start=True, stop=True),
                  [cst, src_t], [P])
            fw.op(V, lambda: nc.vector.tensor_tensor(out=tA[:, 0:TB], in0=src_ap, in1=cs_[:, :], op=ALU.mult), [src_t, cs_], [tA])
            fw.op(V, lambda: nc.vector.tensor_tensor(out=tB[:, 0:TB], in0=P[:, 0:TB], in1=sn_[:, :], op=ALU.mult), [P, sn_], [tB])
            fw.op(V, lambda: nc.vector.tensor_tensor(out=dst_ap, in0=tA[:, 0:TB], in1=tB[:, 0:TB], op=ALU.add), [tA, tB], [dst_t])

        def latA(l, s, src, dkg, dvg, SB):
            t0 = s * TB
            with ExitStack() as ph:
                hb = sb("hb", [128, KC, TB], BF16, ctx=ph)
                wseg = [sb("wseg%d" % i, [128, KC, 256], BF16, ctx=ph) for i in range(3)]
                f = [sb("f%d" % i, [128, 512], ctx=ph) for i in range(6)]
                kb = sb("kb", [128, 2, TB], BF16, ctx=ph)
                yb = sb("yb", [128, 2, TB], BF16, ctx=ph)
                vob = sb("vob", [128, 2, 256], BF16, ctx=ph)
                cs_ = sb("cs_", [128, TB], ctx=ph)
                sn_ = sb("sn_", [128, TB], ctx=ph)
                fw.dma("sp", xt[:, :, :], src[:, :, t0:t0 + TB], writes=[xt])
                fw.dma("sp", cs_[:, :], cosT[:, t0:t0 + TB], writes=[cs_])
                fw.dma("sp", sn_[:, :], sinT[:, t0:t0 + TB], writes=[sn_])
                norm_mod(ph, l, 0, hb, 0, 1)
                ws = [0]

                def seg(i):
                    w = wseg[ws[0] % 3]
                    ws[0] += 1
                    return load_seg(l, i, w)
                wa = seg(0)
                wg = seg(1)
                for cc in range(2):
                    pfm(wa, hb, cc, ps[2], 0)
                    pfm(wg, hb, cc, ps[2], TB)
                    fw.op(A_, lambda: nc.scalar.activation(out=f[0][:, 0:TB], in_=ps[2][:, TB:2 * TB], func=AF.Sigmoid), [ps[2]], [f[0]])
                    fw.op(V, lambda cc=cc: nc.vector.tensor_tensor(out=yb[:, cc, :], in0=ps[2][:, 0:TB], in1=f[0][:, 0:TB], op=ALU.mult),
                          [ps[2], f[0]], [yb])
                fw.dma("sp", ybuf[:, :, 15 + t0:15 + t0 + TB], yb[:, :, :], reads=[yb], writes=[SB["y"]])
                wk = seg(3)
                for cc in range(2):
                    P = ps[4]
                    pfm(wk, hb, cc, P, 0)
                    fw.op(A_, lambda: nc.scalar.activation(out=f[1][:, 0:TB], in_=P[:, 0:TB], func=AF.Square), [P], [f[1]])
                    fw.op(PE_, lambda: nc.tensor.matmul(P[:, TB:2 * TB], lhsT=blk64, rhs=f[1][:, 0:TB], start=True, stop=True), [cst, f[1]], [P])
                    rs(f[2][:, 0:TB], P[:, TB:2 * TB], [P], [f[2]])
                    fw.op(V, lambda cc=cc: nc.vector.scalar_tensor_tensor(out=kb[:, cc, :], in0=P[:, 0:TB], scalar=ppt[:, l, PC_NAK:PC_NAK + 1],
                                                                           in1=f[2][:, 0:TB], op0=ALU.mult, op1=ALU.mult), [P, ppt, f[2]], [kb])
                fw.dma("sp", nks[:, :, t0:t0 + TB], kb[:, :, :], reads=[kb], writes=[SB["k"]])
                wv = seg(4)
                for tl in range(2):
                    ptm(wv, hb, tl, ps[5], 0)
                    fw.op(A_, lambda tl=tl: nc.scalar.copy(out=vob[:, tl, :], in_=ps[5][:, 0:256]), [ps[5]], [vob])
                fw.dma("sp", nvs[:, 2 * s:2 * s + 2, :], vob[:, :, :], reads=[vob], writes=[SB["v"]])
                wk2 = seg(6)
                for cc in range(2):
                    P = ps[4]
                    pfm(wk2, hb, cc, P, 0)
                    fw.op(A_, lambda: nc.scalar.activation(out=f[1][:, 0:TB], in_=P[:, 0:TB], func=AF.Square), [P], [f[1]])
                    fw.op(PE_, lambda: nc.tensor.matmul(P[:, TB:2 * TB], lhsT=blk32, rhs=f[1][:, 0:TB], start=True, stop=True), [cst, f[1]], [P])
                    rs(f[2][:, 0:TB], P[:, TB:2 * TB], [P], [f[2]])
                    fw.op(V, lambda: nc.vector.scalar_tensor_tensor(out=f[3][:, 0:TB], in0=P[:, 0:TB], scalar=ppt[:, l, PC_DK:PC_DK + 1],
                                                                    in1=f[2][:, 0:TB], op0=ALU.mult, op1=ALU.mult), [P, ppt, f[2]], [f[3]])
                    rope(f[3], f[3][:, 0:TB], f[4], f[5], cs_, sn_, dkg[:, cc, t0:t0 + TB], dkg, ps[3])
                wv2 = seg(7)
                for tl in range(2):
                    ptm(wv2, hb, tl, ps[5], 0)
                    fw.op(A_, lambda tl=tl: nc.scalar.copy(out=dvg[:, 2 * s + tl, :], in_=ps[5][:, 0:256]), [ps[5]], [dvg])
                fw.barrier()

        def latB(l, s, src, dkg, dvg, SB, CA):
            t0 = s * TB
            nck, ncvt, dck, dcvt, biast, qsc = CA
            ones64 = onesE[:, 0:64]
            with ExitStack() as ph:
                hb = sb("hb", [128, KC, TB], BF16, ctx=ph)
                mixed = sb("mixed", [128, KC, TB], BF16, ctx=ph)
                mixh = sb("mixh", [64, 8, TB], BF16, ctx=ph)
                wseg = [sb("wseg%d" % i, [128, KC, 256], BF16, ctx=ph) for i in range(2)]
                wo = sb("wo", [128, 4, D], BF16, ctx=ph)
                woh = sb("woh", [64, 8, D], BF16, ctx=ph)
                wsb = sb("wsb", [128, 4, 128], BF16, ctx=ph)
                rowt = sb("rowt", [128, 512], ctx=ph)
                bst = sb("bst", [128, 2, 128], ctx=ph)
                diagw = sb("diagw", [128, 2, 31, 128], BF16, ctx=ph)
                ywin = sb("ywin", [128, 2, TB + 30], BF16, ctx=ph)
                f = [sb("f%d" % i, [128, 512], ctx=ph) for i in range(6)]
                qb = sb("qb", [128, 2, 2, TB], BF16, ctx=ph)
                dqv = sb("dqv", [128, 4, 2, TB], BF16, ctx=ph)
                nkw = sb("nkw", [128, 2, 768], BF16, ctx=ph)
                nvw = sb("nvw", [128, 6, 256], BF16, ctx=ph)
                pt = [sb("pt%d" % i, [128, 512], BF16, ctx=ph) for i in range(4)]
                pt2 = sb("pt2", [128, 1024], BF16, ctx=ph)
                gu = sb("gu", [128, 2, TB], ctx=ph)
                vpad = sb("vpad", [128, 2, 4, 128], BF16, ctx=ph)
                bnst = sb("bnst", [128, 2, 8], ctx=ph)
                cs_ = sb("cs_", [128, TB], ctx=ph)
                sn_ = sb("sn_", [128, TB], ctx=ph)
                fw.dma("sp", xt[:, :, :], src[:, :, t0:t0 + TB], writes=[xt])
                fw.dma("sp", cs_[:, :], cosT[:, t0:t0 + TB], writes=[cs_])
                fw.dma("sp", sn_[:, :], sinT[:, t0:t0 + TB], writes=[sn_])
                fw.dma("sp", ywin[:, :, :], ybuf[:, :, t0:t0 + TB + 30], reads=[SB["y"]], writes=[ywin])
                jlo, jhi = max(0, 2 * s - 2), min(SEQL // 128 - 1, 2 * s + 3)
                nt = jhi - jlo + 1
                fw.dma("sp", nkw[:, :, 0:nt * 128], nks[:, :, jlo * 128:(jhi + 1) * 128], reads=[SB["k"]], writes=[nkw])
                fw.dma("sp", nvw[:, 0:nt, :], nvs[:, jlo:jhi + 1, :], reads=[SB["v"]], writes=[nvw])
                norm_mod(ph, l, 0, hb, 0, 1)
                wrows = w_out[l].rearrange("(k p) n -> p k n", p=128)
                fw.dma(PO, wo[:, 0:2, :], wrows[:, 0:2, :], writes=[wo])
                fw.dma(PO, wo[:, 2:4, :], wrows[:, 6:8, :], writes=[wo])
                fw.dma(PO, woh[:, :, :], w_out[l][256:768, :].rearrange("(h p) n -> p h n", p=64), writes=[woh])
                load_w(PO, wsb, wsT[l])
                fw.dma("sp", rowt[:, :], rowp[l].partition_broadcast(128), writes=[rowt])
                fw.dma("sp", bst[:, :, :], bsT[l], writes=[bst])
                fw.op(V, lambda: nc.vector.memset(vpad[:, :, :, :], 0.0), [], [vpad])
                for cc in range(2):
                    for tap in range(31):
                        fw.op(V, lambda cc=cc, tap=tap, e=nc.vector: e.tensor_scalar(
                            out=diagw[:, cc, tap, :], in0=identB, scalar1=ppt[:, l, PC_CONVW + cc * 31 + tap:PC_CONVW + cc * 31 + tap + 1],
                            scalar2=None, op0=ALU.mult), [cstb, ppt], [diagw])
                ws = [0]

                def seg(i):
                    w = wseg[ws[0] % 2]
                    ws[0] += 1
                    return load_seg(l, i, w)

                for cc in range(2):
                    for tap in range(31):
                        fw.op(PE_, lambda cc=cc, tap=tap: nc.tensor.matmul(
                            ps[3][:, cc * TB:(cc + 1) * TB], lhsT=diagw[:, cc, tap, :], rhs=ywin[:, cc, tap:tap + TB],
                            start=(tap == 0), stop=(tap == 30)), [diagw, ywin], [ps[3]], inc=(tap == 30))
                    fw.op(A_, lambda cc=cc: nc.scalar.activation(
                        out=f[1][:, cc * TB:(cc + 1) * TB], in_=ps[3][:, cc * TB:(cc + 1) * TB], func=AF.Identity,
                        bias=ppt[:, l, PC_CONVB + cc:PC_CONVB + cc + 1], scale=1.0), [ps[3], ppt], [f[1]])
                for cc in range(2):
                    fw.op(PE_, lambda cc=cc: nc.tensor.matmul(ps[2][:, 0:TB], lhsT=o256, rhs=f[1][:, cc * TB:(cc + 1) * TB],
                                                              start=(cc == 0), stop=(cc == 1)), [cst, f[1]], [ps[2]], inc=(cc == 1))
                for cc in range(2):
                    fw.op(V, lambda cc=cc: nc.vector.tensor_tensor(out=f[2][:, cc * TB:(cc + 1) * TB], in0=f[1][:, cc * TB:(cc + 1) * TB],
                                                                    in1=ps[2][:, 0:TB], op=ALU.subtract), [f[1], ps[2]], [f[2]])
                fw.op(A_, lambda: nc.scalar.activation(out=f[3][:, :], in_=f[2][:, :], func=AF.Square), [f[2]], [f[3]])
                for cc in range(2):
                    fw.op(PE_, lambda cc=cc: nc.tensor.matmul(ps[3][:, 0:TB], lhsT=o256, rhs=f[3][:, cc * TB:(cc + 1) * TB],
                                                              start=(cc == 0), stop=(cc == 1)), [cst, f[3]], [ps[3]], inc=(cc == 1))
                rs(f[4][:, 0:TB], ps[3][:, 0:TB], [ps[3]], [f[4]])
                for cc in range(2):
                    fw.op(V, lambda cc=cc: nc.vector.tensor_tensor(out=f[3][:, cc * TB:(cc + 1) * TB], in0=f[2][:, cc * TB:(cc + 1) * TB],
                                                                    in1=f[4][:, 0:TB], op=ALU.mult), [f[2], f[4]], [f[3]])
                    fw.op(A_, lambda cc=cc: nc.scalar.activation(
                        out=mixed[:, cc, :], in_=f[3][:, cc * TB:(cc + 1) * TB], func=AF.Silu,
                        scale=ppt[:, l, PC_CLNG + cc:PC_CLNG + cc + 1], bias=ppt[:, l, PC_CLNB + cc:PC_CLNB + cc + 1]),
                        [f[3], ppt], [mixed])
                if lat_stage < 2:
                    fw.barrier()
                    return

                wq_ = seg(2)
                for cc in range(2):
                    P = ps[2]
                    pfm(wq_, hb, cc, P, 0)
                    fw.op(A_, lambda: nc.scalar.activation(out=f[1][:, 0:TB], in_=P[:, 0:TB], func=AF.Square), [P], [f[1]])
                    fw.op(PE_, lambda: nc.tensor.matmul(P[:, TB:2 * TB], lhsT=blk64, rhs=f[1][:, 0:TB], start=True, stop=True), [cst, f[1]], [P])
                    rs(f[2][:, 0:TB], P[:, TB:2 * TB], [P], [f[2]])
                    for par in range(2):
                        fw.op(V, lambda cc=cc, par=par: nc.vector.scalar_tensor_tensor(
                            out=qb[:, par, cc, :], in0=P[:, 0:TB], scalar=qsc[:, l, par:par + 1], in1=f[2][:, 0:TB],
                            op0=ALU.mult, op1=ALU.mult), [P, qsc, f[2]], [qb])
                first = [True] * 4
                items = [(rr, j, d_, v0, v1) for rr in range(4) for (j, d_, v0, v1) in na_tiles(4 * s + rr)]

                def na_s(i):
                    rr, j, d_, v0, v1 = items[i]
                    SP = ps[i % 2]
                    ci = COMBOS.index((d_, v0, v1))
                    for h in range(4):
                        cc, par = h // 2, h % 2
                        fw.op(PE_, lambda h=h, cc=cc, par=par, j=j, SP=SP, rr=rr: nc.tensor.matmul(
                            SP[:, h * 64:(h + 1) * 64], lhsT=nkw[:, cc, (j - jlo) * 128:(j - jlo + 1) * 128],
                            rhs=qb[:, par, cc, rr * 64:(rr + 1) * 64], start=True, stop=False, skip_group_check=True),
                            [nkw, qb], [SP], inc=False)
                        fw.op(PE_, lambda h=h, ci=ci, SP=SP: nc.tensor.matmul(
                            SP[:, h * 64:(h + 1) * 64], lhsT=identB, rhs=biast[:, ci * 4 + h, :], start=False, stop=True,
                            skip_group_check=True), [cstb, biast], [SP], inc=(h == 3))

                def na_od(i):
                    rr, j, d_, v0, v1 = items[i]
                    SP = ps[i % 2]
                    p_ = pt[i % 2]
                    fw.op(A_, lambda SP=SP, p_=p_: nc.scalar.activation(out=p_[:, 0:256], in_=SP[:, 0:256], func=AF.Exp), [SP], [p_])
                    for h in range(4):
                        OD = ps[4 + h]
                        fw.op(PE_, lambda h=h, j=j, OD=OD, p_=p_, rr=rr, st=first[h]: nc.tensor.matmul(
                            OD[0:64, rr * 64:(rr + 1) * 64], lhsT=nvw[:, j - jlo, h * 64:(h + 1) * 64], rhs=p_[:, h * 64:(h + 1) * 64],
                            start=st, stop=False, skip_group_check=True), [nvw, p_], [OD], inc=False)
                        first[h] = False
                        fw.op(PE_, lambda h=h, OD=OD, p_=p_, rr=rr: nc.tensor.matmul(
                            OD[0:64, TB + rr * 64:TB + (rr + 1) * 64], lhsT=ones64, rhs=p_[:, h * 64:(h + 1) * 64],
                            start=False, stop=False, skip_group_check=True), [cstb, p_], [OD], inc=(h == 3))

                na_s(0)
                for i in range(len(items)):
                    if i + 1 < len(items):
                        na_s(i + 1)
                    na_od(i)
                for kt in range(2):
                    for h in range(4):
                        cc, par = h // 2, h % 2
                        SP = ps[h // 2]
                        fw.op(PE_, lambda h=h, cc=cc, par=par, SP=SP, kt=kt: nc.tensor.matmul(
                            SP[:, (h % 2) * 256:(h % 2 + 1) * 256], lhsT=nck[:, l, cc, kt * 128:(kt + 1) * 128], rhs=qb[:, par, cc, :],
                            start=True, stop=True), [nck, qb], [SP], inc=(h % 2 == 1))
                    for hh in range(2):
                        fw.op(A_, lambda hh=hh: nc.scalar.activation(out=pt2[:, hh * 512:(hh + 1) * 512], in_=ps[hh][:, :], func=AF.Exp),
                              [ps[hh]], [pt2])
                    for h in range(4):
                        OD = ps[4 + h]
                        fw.op(PE_, lambda h=h, OD=OD, kt=kt: nc.tensor.matmul(
                            OD[0:64, 0:TB], lhsT=ncvt[:, l, kt, h * 64:(h + 1) * 64], rhs=pt2[:, h * 256:(h + 1) * 256],
                            start=False, stop=(kt == 1), skip_group_check=True), [ncvt, pt2], [OD], inc=False)
                        fw.op(PE_, lambda h=h, OD=OD, kt=kt: nc.tensor.matmul(
                            OD[0:64, TB:2 * TB], lhsT=ones64, rhs=pt2[:, h * 256:(h + 1) * 256],
                            start=False, stop=(kt == 1), skip_group_check=True), [cstb, pt2], [OD], inc=True)
                for h in range(4):
                    OD = ps[4 + h]
                    fw.op(V, lambda OD=OD: nc.vector.reciprocal(out=f[5][0:64, 0:TB], in_=OD[0:64, TB:2 * TB]), [OD], [f[5]])
                    fw.op(V, lambda OD=OD, h=h: nc.vector.tensor_tensor(out=mixh[0:64, h, :], in0=OD[0:64, 0:TB], in1=f[5][0:64, 0:TB],
                                                                         op=ALU.mult), [OD, f[5]], [mixh])
                if lat_stage < 3:
                    fw.barrier()
                    return

                wq_ = seg(5)
                for cc in range(2):
                    P = ps[2]
                    pfm(wq_, hb, cc, P, 0)
                    fw.op(A_, lambda: nc.scalar.activation(out=f[1][:, 0:TB], in_=P[:, 0:TB], func=AF.Square), [P], [f[1]])
                    fw.op(PE_, lambda: nc.tensor.matmul(P[:, TB:2 * TB], lhsT=blk32, rhs=f[1][:, 0:TB], start=True, stop=True), [cst, f[1]], [P])
                    rs(f[2][:, 0:TB], P[:, TB:2 * TB], [P], [f[2]])
                    fw.op(V, lambda: nc.vector.scalar_tensor_tensor(out=f[3][:, 0:TB], in0=P[:, 0:TB], scalar=ppt[:, l, PC_DQ:PC_DQ + 1],
                                                                    in1=f[2][:, 0:TB], op0=ALU.mult, op1=ALU.mult), [P, ppt, f[2]], [f[3]])
                    rope(f[3], f[3][:, 0:TB], f[4], f[5], cs_, sn_, f[0][:, 0:TB], f[0], ps[3])
                    for v in range(4):
                        fw.op(V, lambda v=v, cc=cc: nc.vector.tensor_scalar(out=dqv[:, v, cc, :], in0=f[0][:, 0:TB],
                                                                             scalar1=cst[:, CC_MASK + v:CC_MASK + v + 1], scalar2=None,
                                                                             op0=ALU.mult), [f[0], cst], [dqv])
                sc32 = 32 ** -0.5
                for cc in range(2):
                    firstb = [True] * 4
                    ntile = SEQL // 128 + 2
                    def tile_src(ti):
                        if ti < SEQL // 128:
                            return (dkg[:, cc, ti * 128:(ti + 1) * 128], dkg, dvg, (lambda h, ti=ti: dvg[:, ti, h * 64:(h + 1) * 64]))
                        kt = ti - SEQL // 128
                        return (dck[:, l, cc, kt * 128:(kt + 1) * 128], dck, dcvt, (lambda h, kt=kt: dcvt[:, l, kt, h * 64:(h + 1) * 64]))

                    def s_part(ti):
                        Kap, Kt, Vt, vsl = tile_src(ti)
                        st_ = ti % 2
                        SPs = [ps[2 * st_], ps[2 * st_ + 1]]
                        for sub in range(2):
                            for par in range(2):
                                fw.op(PE_, lambda sub=sub, par=par, Kap=Kap, SPs=SPs: nc.tensor.matmul(
                                    SPs[sub][:, par * 256:(par + 1) * 256], lhsT=Kap, rhs=dqv[:, sub * 2 + par, cc, :],
                                    start=True, stop=True), [Kt, dqv], [SPs[sub]], inc=(par == 1))

                    def od_part(ti):
                        Kap, Kt, Vt, vsl = tile_src(ti)
                        st_ = ti % 2
                        SPs = [ps[2 * st_], ps[2 * st_ + 1]]
                        pts = [pt[2 * st_], pt[2 * st_ + 1]]
                        for sub in range(2):
                            fw.op(A_, lambda sub=sub, SPs=SPs, pts=pts: nc.scalar.activation(out=pts[sub][:, :], in_=SPs[sub][:, :],
                                                                                              func=AF.Exp, scale=sc32), [SPs[sub]], [pts[sub]])
                        for sub in range(2):
                            for par in range(2):
                                bi = sub * 2 + par
                                OD = ps[4 + bi]
                                h = 2 * cc + par
                                fw.op(PE_, lambda OD=OD, h=h, sub=sub, par=par, pts=pts, vsl=vsl, st=firstb[bi]: nc.tensor.matmul(
                                    OD[0:64, 0:TB], lhsT=vsl(h), rhs=pts[sub][:, par * 256:(par + 1) * 256],
                                    start=st, stop=False, skip_group_check=True), [Vt, pts[sub]], [OD], inc=False)
                                firstb[bi] = False
                                fw.op(PE_, lambda OD=OD, sub=sub, par=par, pts=pts: nc.tensor.matmul(
                                    OD[0:64, TB:2 * TB], lhsT=ones64, rhs=pts[sub][:, par * 256:(par + 1) * 256],
                                    start=False, stop=False, skip_group_check=True), [cstb, pts[sub]], [OD], inc=True)

                    s_part(0)
                    for ti in range(ntile):
                        if ti + 1 < ntile:
                            s_part(ti + 1)
                        od_part(ti)
                    for par in range(2):
                        h = 2 * cc + par
                        O1, O2 = ps[4 + par], ps[6 + par]
                        fw.op(V, lambda O1=O1: nc.vector.reciprocal(out=f[5][0:64, 0:TB], in_=O1[0:64, TB:2 * TB]), [O1], [f[5]])
                        fw.op(V, lambda O2=O2: nc.vector.reciprocal(out=f[5][0:64, TB:2 * TB], in_=O2[0:64, TB:2 * TB]), [O2], [f[5]])
                        fw.op(V, lambda O1=O1: nc.vector.tensor_tensor(out=f[0][0:64, 0:TB], in0=O1[0:64, 0:TB], in1=f[5][0:64, 0:TB], op=ALU.mult),
                              [O1, f[5]], [f[0]])
                        fw.op(V, lambda O2=O2: nc.vector.tensor_tensor(out=f[0][0:64, TB:2 * TB], in0=O2[0:64, 0:TB], in1=f[5][0:64, TB:2 * TB],
                                                                        op=ALU.mult), [O2, f[5]], [f[0]])
                        fw.op(V, lambda: nc.vector.scalar_tensor_tensor(out=f[1][0:64, 0:TB], in0=f[0][0:64, TB:2 * TB], scalar=lamt[0:64, l, 0:1],
                                                                        in1=f[0][0:64, 0:TB], op0=ALU.mult, op1=ALU.add), [f[0], lamt], [f[1]])
                        fw.op(A_, lambda: nc.scalar.activation(out=f[1][0:64, TB:2 * TB], in_=f[1][0:64, 0:TB], func=AF.Square), [f[1]], [f[1]])
                        fw.op(PE_, lambda: nc.tensor.matmul(ps[2][0:64, 0:TB], lhsT=cst[0:64, CC_B64:CC_B64 + 64], rhs=f[1][0:64, TB:2 * TB],
                                                            start=True, stop=True), [cst, f[1]], [ps[2]])
                        fw.op(A_, lambda: nc.scalar.activation(out=f[2][0:64, 0:TB], in_=ps[2][0:64, 0:TB], func=AF.Sqrt, bias=epst[0:64, 0:1],
                                                               scale=1.0), [ps[2], epst], [f[2]])
                        fw.op(V, lambda: nc.vector.reciprocal(out=f[2][0:64, 0:TB], in_=f[2][0:64, 0:TB]), [f[2]], [f[2]])
                        fw.op(V, lambda h=h: nc.vector.scalar_tensor_tensor(out=mixh[0:64, 4 + h, :], in0=f[1][0:64, 0:TB], scalar=subg[0:64, l:l + 1],
                                                                             in1=f[2][0:64, 0:TB], op0=ALU.mult, op1=ALU.mult),
                              [f[1], subg, f[2]], [mixh])
                if lat_stage < 4:
                    fw.barrier()
                    return

                wu = seg(8)
                wv = seg(9)
                for cc in range(2):
                    pfm(wu, hb, cc, ps[0], 0)
                    fw.op(A_, lambda cc=cc: nc.scalar.activation(out=gu[:, cc, :], in_=ps[0][:, 0:TB], func=AF.Gelu_apprx_tanh),
                          [ps[0]], [gu])
                for tl in range(2):
                    P = ps[1]
                    ptm(wv, hb, tl, P, 0)
                    fw.op(A_, lambda: nc.scalar.activation(out=f[0][:, 0:256], in_=P[:, 0:256], func=AF.Gelu_apprx_tanh), [P], [f[0]])
                    fw.op(V, lambda: nc.vector.bn_stats(out=bnst[:, 0, 0:6], in_=f[0][:, 0:256]), [f[0]], [bnst])
                    fw.op(V, lambda: nc.vector.bn_aggr(out=bnst[:, 1, 0:2], in_=bnst[:, 0, 0:6]), [bnst], [bnst])
                    rs(bnst[:, 1, 2:3], bnst[:, 1, 1:2], [bnst], [bnst])
                    fw.op(V, lambda: nc.vector.tensor_scalar(out=f[0][:, 256:512], in0=f[0][:, 0:256], scalar1=bnst[:, 1, 0:1],
                                                             scalar2=bnst[:, 1, 2:3], op0=ALU.subtract, op1=ALU.mult), [f[0], bnst], [f[0]])
                    fw.op(V, lambda: nc.vector.tensor_tensor(out=f[0][:, 0:256], in0=f[0][:, 256:512], in1=rowt[:, 0:256], op=ALU.mult),
                          [f[0], rowt], [f[0]])
                    for par in range(2):
                        fw.op(V, lambda tl=tl, par=par: nc.vector.tensor_tensor(
                            out=vpad[:, tl, :, :].rearrange("p (c r) n -> p c r n", r=2)[:, :, par, par * 64:par * 64 + 64],
                            in0=f[0][:, 0:256].rearrange("p (c r n) -> p c r n", c=2, r=2)[:, :, par, :],
                            in1=rowt[:, 256:512].rearrange("p (c r n) -> p c r n", c=2, r=2)[:, :, par, :], op=ALU.add),
                            [f[0], rowt], [vpad])
                for cc in range(2):
                    P = ps[2]
                    for tl in range(2):
                        for par in range(2):
                            fw.op(PE_, lambda tl=tl, par=par, cc=cc: nc.tensor.matmul(
                                P[:, tl * 128:(tl + 1) * 128], lhsT=vpad[:, tl, 2 * cc + par, :], rhs=wsb[:, 2 * cc + par, :],
                                start=(par == 0), stop=(par == 1)), [vpad, wsb], [P], inc=(par == 1))
                    fw.op(V, lambda cc=cc: nc.vector.tensor_tensor(
                        out=f[1][:, 0:TB].rearrange("p (a b) -> p a b", a=2), in0=P[:, 0:TB].rearrange("p (a b) -> p a b", a=2),
                        in1=bst[:, cc, :].unsqueeze(1).to_broadcast([128, 2, 128]), op=ALU.add), [P, bst], [f[1]])
                    fw.op(V, lambda cc=cc: nc.vector.tensor_tensor(out=mixed[:, 6 + cc, :], in0=f[1][:, 0:TB], in1=gu[:, cc, :], op=ALU.mult),
                          [f[1], gu], [mixed])

                for j in range(KC):
                    P = ps[3 + (j % 2)]
                    jj = slice(j * 128, (j + 1) * 128)
                    n_mm = 12
                    mi = 0
                    for q_ in range(2):
                        fw.op(PE_, lambda q_=q_, P=P, jj=jj, mi=mi: nc.tensor.matmul(P[:, 0:TB], lhsT=wo[:, q_, jj], rhs=mixed[:, q_, :],
                                                                                       start=(mi == 0), stop=False), [wo, mixed], [P], inc=False)
                        mi += 1
                    for h in range(8):
                        fw.op(PE_, lambda h=h, P=P, jj=jj: nc.tensor.matmul(P[:, 0:TB], lhsT=woh[0:64, h, jj], rhs=mixh[0:64, h, :],
                                                                             start=False, stop=False), [woh, mixh], [P], inc=False)
                    for q_ in range(2):
                        fw.op(PE_, lambda q_=q_, P=P, jj=jj: nc.tensor.matmul(P[:, 0:TB], lhsT=wo[:, 2 + q_, jj], rhs=mixed[:, 6 + q_, :],
                                                                               start=False, stop=(q_ == 1)), [wo, mixed], [P], inc=(q_ == 1))
                    fw.op(V, lambda j=j, P=P: nc.vector.scalar_tensor_tensor(
                        out=xt[:, j, :], in0=P[:, 0:TB], scalar=modt[:, l, 16 + j, 1:2], in1=xt[:, j, :], op0=ALU.mult, op1=ALU.add),
                        [P, modt, xt], [xt])
                fw.dma("sp", yl[:, :, t0:t0 + TB], xt[:, :, :], reads=[xt], writes=[SB["x"]])
                fw.barrier()

        def latent():
            SB = {k: Buf("scr_" + k) for k in ["y", "k", "v", "x"]}
            with ExitStack() as gl0:
                nck = sb("nck", [128, L, 2, 256], BF16, ctx=gl0)
                ncvt = sb("ncvt", [128, L, 2, 256], BF16, ctx=gl0)
                dck = sb("dck", [128, L, 2, 256], BF16, ctx=gl0)
                dcvt = sb("dcvt", [128, L, 2, 256], BF16, ctx=gl0)
                qsc = sb("qsc", [128, L, 2], ctx=gl0)
                zt = sb("zt", [128, 2, 15], BF16, ctx=gl0)
                for (t_, d_) in [(nck, nckT), (ncvt, ncv), (dck, dckT), (dcvt, dcv)]:
                    fw.dma(PO, t_[:, :, :, :], d_.rearrange("l p a b -> p l a b"), writes=[t_])
                fw.op(V, lambda: nc.vector.tensor_scalar(out=qsc[:, :, :], in0=ppt[:, :, PC_NAQE:PC_NAQE + 2], scalar1=0.125, scalar2=None,
                                                         op0=ALU.mult), [ppt], [qsc])
                fw.op(V, lambda: nc.vector.memset(zt[:, :, :], 0.0), [], [zt])
                fw.dma("sp", ybuf[:, :, 0:15], zt[:, :, :], reads=[zt], writes=[SB["y"]])
                fw.dma("sp", ybuf[:, :, 15 + SEQL:30 + SEQL], zt[:, :, :], reads=[zt], writes=[SB["y"]])
                for l in range(nl):
                    src = xl if l == 0 else yl
                    with ExitStack() as gl:
                        dkg = sb("dkg", [128, 2, SEQL], BF16, ctx=gl)
                        dvg = sb("dvg", [128, SEQL // 128, 256], BF16, ctx=gl)
                        biast = sb("biast", [128, NCOMBO * 4, 64], BF16, ctx=gl)
                        fw.dma(PO, biast[:, :, :], biasT[l], writes=[biast])
                        for s in range(n_lat):
                            latA(l, s, src, dkg, dvg, SB)
                        if lat_stage >= 1:
                            for s in range(n_lat):
                                latB(l, s, src, dkg, dvg, SB, (nck, ncvt, dck, dcvt, biast, qsc))
                        fw.barrier()
                    if lat_stage >= 5:
                        fw.barrier()
                        peer_pass(l, n_lat, 1, (lambda s: yl[:, :, s * TB:(s + 1) * TB]), yl, SB["x"])

        SBc = Buf("scr_xc")
        for l in range(nl if n_seq else 0):
            for s in range(n_seq):
                srcx = xc if l == 0 else yc
                fw.dma("sp", xt[:, :, :], srcx[:, :, s * TB:(s + 1) * TB], reads=[SBc], writes=[xt])
                if stage >= 1:
                    mixers(l, s)
                fw.dma("sp", yc[:, :, s * TB:(s + 1) * TB], xt[:, :, :], reads=[xt], writes=[SBc])
                fw.barrier()
            if stage >= 2:
                peer_pass(l, n_seq, 0, (lambda s: yc[:, :, s * TB:(s + 1) * TB]), yc, SBc)
        if n_lat:
            latent()
        fw.finish()
        print("instructions:", fw.nins, fw.cnt)
        import collections
        agg = collections.defaultdict(int)
        for k, cx in ALLOC.items():
            agg[cx._nm.rsplit("_", 1)[0]] = max(agg[cx._nm.rsplit("_", 1)[0]], cx._bytes)
        print("alloc KB by ctx(first tile):", {k: round(v / 1024, 1) for k, v in agg.items()})
    return nc


def _consts():
    c = np.zeros((128, NCC), np.float32)
    p = np.arange(128)
    c[:, CC_ID:CC_ID + 128] = np.eye(128)
    c[:, CC_B64:CC_B64 + 128] = (p[:, None] // 64 == p[None, :] // 64) / 64.0
    c[:, CC_B32:CC_B32 + 128] = (p[:, None] // 32 == p[None, :] // 32) / 32.0
    c[:, CC_O1024:CC_O1024 + 128] = 1.0 / 1024
    c[:, CC_O256:CC_O256 + 128] = 1.0 / 256
    c[:, CC_OE:CC_OE + 128] = (p[None, :] < 64)
    c[:, CC_OO:CC_OO + 128] = (p[None, :] >= 64)
    c[:, CC_IOTA:CC_IOTA + 128] = p[None, :]
    c[:, CC_IOTA16:CC_IOTA16 + 16] = np.arange(16)[None, :]
    c[:, CC_MASK + 0] = ((p % 64) < 32) & (p < 64)
    c[:, CC_MASK + 1] = ((p % 64) < 32) & (p >= 64)
    c[:, CC_MASK + 2] = ((p % 64) >= 32) & (p < 64)
    c[:, CC_MASK + 3] = ((p % 64) >= 32) & (p >= 64)
    for m in range(128):
        if (m % 16) < 8:
            c[m + 8, CC_PSW + m] = -1.0
        else:
            c[m - 8, CC_PSW + m] = 1.0
    return c


def na_tiles(r):
    r0 = min(max(r - 4, 0), 56)
    out = []
    for j in range(r0 // 2, (r0 + 7) // 2 + 1):
        v0 = r0 <= 2 * j < r0 + 8
        v1 = r0 <= 2 * j + 1 < r0 + 8
        out.append((j, 2 * j - r, bool(v0), bool(v1)))
    return out


COMBOS = sorted(set((d, v0, v1) for r in range(64) for (_, d, v0, v1) in na_tiles(r)))
NCOMBO = len(COMBOS)
SEQL = 4096


def _rope_tables():
    f = np.arange(128)
    w = f % 32
    i = np.where(w < 16, w, w - 16)
    inv = (10000.0 ** (-(np.arange(8, dtype=np.float32)) / 8)).astype(np.float32)
    t = np.arange(SEQL)
    rows = (t // 64).astype(np.float32)
    cols = (t % 64).astype(np.float32)
    pos = np.where((w < 16)[:, None], rows[None, :], cols[None, :]).astype(np.float32)
    ang = (pos * inv[i % 8][:, None]).astype(np.float32)
    return np.cos(ang).astype(np.float32), np.sin(ang).astype(np.float32)


def _bias_tables(rel_bias):
    out = np.full((L, 128, NCOMBO * 4, 64), -1e30, np.float32)
    key = np.arange(128)
    wr = key // 64
    kc = key % 64
    qc = np.arange(64)
    c0 = np.clip(qc - 8, 0, 48)
    inwin = (kc[:, None] >= c0[None, :]) & (kc[:, None] < c0[None, :] + 16)
    dci = np.clip(kc[:, None] - qc[None, :] + 15, 0, 30)
    for ci, (d, v0, v1) in enumerate(COMBOS):
        valid = np.where(wr == 0, v0, v1)[:, None] & inwin
        dri = np.clip(d + wr + 7, 0, 14)
        for h in range(4):
            g = rel_bias[:, h][:, dri[:, None], dci]
            out[:, :, ci * 4 + h, :] = np.where(valid[None], g, -1e30)
    return out


def _fm(v):
    return np.ascontiguousarray(v.reshape(-1, 128).T)


def _pack_weights(inp):
    f = lambda a: np.asarray(a, np.float32)
    pp = np.zeros((L, 128, NPC), np.float32)
    p = np.arange(128)
    for l in range(L):
        pp[l, :, PC_N1G:PC_N1G + 8] = _fm(f(inp["norm1_g"][l]))
        pp[l, :, PC_N2G:PC_N2G + 8] = _fm(f(inp["norm2_g"][l]))
        pp[l, :, PC_BMOD:PC_BMOD + 48] = _fm(f(inp["b_mod"][l]))
        cw = f(inp["conv_w"][l])
        for cc in range(2):
            pp[l, :, PC_CONVW + cc * 31:PC_CONVW + (cc + 1) * 31] = cw[:, cc * 128:(cc + 1) * 128].T
        pp[l, :, PC_CONVB:PC_CONVB + 2] = _fm(f(inp["conv_b"][l]))
        pp[l, :, PC_CLNG:PC_CLNG + 2] = _fm(f(inp["conv_ln_g"][l]))
        pp[l, :, PC_CLNB:PC_CLNB + 2] = _fm(f(inp["conv_ln_b"][l]))
        pp[l, :, PC_NAQ] = f(inp["na_qn_g"][l])[p % 64]
        pp[l, :, PC_NAK] = f(inp["na_kn_g"][l])[p % 64]
        dq = f(inp["diff_qn_g"][l])[p % 32]
        pp[l, :, PC_DQ1] = np.where((p % 64) < 32, dq, 0.0)
        pp[l, :, PC_DQ2] = np.where((p % 64) >= 32, dq, 0.0)
        pp[l, :, PC_DK] = f(inp["diff_kn_g"][l])[p % 32]
        pp[l, :, PC_DQ] = dq
        pp[l, :, PC_NAQE] = np.where(p < 64, pp[l, :, PC_NAQ], 0.0)
        pp[l, :, PC_NAQO] = np.where(p >= 64, pp[l, :, PC_NAQ], 0.0)
        pp[l, :, PC_D1E] = np.where(p < 64, pp[l, :, PC_DQ1], 0.0)
        pp[l, :, PC_D1O] = np.where(p >= 64, pp[l, :, PC_DQ1], 0.0)
        pp[l, :, PC_D2E] = np.where(p < 64, pp[l, :, PC_DQ2], 0.0)
        pp[l, :, PC_D2O] = np.where(p >= 64, pp[l, :, PC_DQ2], 0.0)
        pp[l, :, PC_SUB] = f(inp["diff_subln_g"][l])[p % 64]
    rowp = np.concatenate([f(inp["gmlp_ln_g"]), f(inp["gmlp_ln_b"])], axis=1)
    bs = f(inp["gmlp_bs"])
    bsT = np.zeros((L, 128, 2, 128), np.float32)
    for cc in range(2):
        bsT[:, 0:64, cc, :] = bs[:, 2 * cc, None, :]
        bsT[:, 64:128, cc, :] = bs[:, 2 * cc + 1, None, :]
    wsT = np.ascontiguousarray(f(inp["gmlp_ws"]).transpose(0, 3, 1, 2))
    dlam = f(inp["diff_lambda"]).reshape(L, 128)
    skT = np.ascontiguousarray(f(inp["peer_sub_keys"]).transpose(0, 3, 1, 2))
    pu = f(inp["peer_u"])
    puT = np.ascontiguousarray(pu.reshape(L, 128, 128, KC, 128).transpose(0, 2, 4, 3, 1)).reshape(L, 128, 128, KC * 128)
    cosT, sinT = _rope_tables()
    return dict(cosT=cosT, sinT=sinT, biasT=_bias_tables(f(inp["na_rel_bias"])), consts=_consts(), pp=pp, rowp=rowp, bsT=bsT, wsT=wsT, dlam=dlam, skT=skT, puT=puT,
                w_mod=f(inp["w_mod"]), w_in=f(inp["w_in"]), w_out=f(inp["w_out"]), wq=f(inp["peer_wq"]), pv=f(inp["peer_v"]))


def _pack_latent(inp, b):
    f = lambda a: np.asarray(a, np.float32)
    xs = f(inp["x_sample"][b]).reshape(SEQL, KC, 128)
    m = {"xl": np.ascontiguousarray(xs.transpose(2, 1, 0))}
    for nm, key in [("n", "cache_na_kv"), ("d", "cache_diff_kv")]:
        c = f(inp[key][b])
        k = c[:, 0].transpose(0, 1, 3, 2).reshape(L, 2, 128, 256).transpose(0, 2, 1, 3)
        v = c[:, 1].transpose(0, 2, 1, 3).reshape(L, 2, 128, 256).transpose(0, 2, 1, 3)
        m[nm + "ckT"] = np.ascontiguousarray(k)
        m[nm + "cv"] = np.ascontiguousarray(v)
    return m


_NC_CACHE = {}


def kernel(**inputs):
    f = lambda a: np.asarray(a, np.float32)
    W = _pack_weights(inputs)
    xp = f(inputs["x_prompt"])
    c_ctx = f(inputs["c_ctx"])
    cvec = f(inputs["c"])
    if "nc" not in _NC_CACHE:
        _NC_CACHE["nc"] = build_program(4)
    nc = _NC_CACHE["nc"]
    in_maps = []
    lat = [_pack_latent(inputs, b) for b in range(2)]
    for c in range(NCORES):
        xs = xp[4 * c:4 * c + 4].reshape(1024, KC, 128)
        m = dict(W)
        m.update(lat[c // 4])
        m["xc"] = np.ascontiguousarray(xs.transpose(2, 1, 0))
        cond = np.stack([c_ctx, cvec[c // 4]], axis=1)
        m["condT"] = np.ascontiguousarray(cond.reshape(KC, 128, 2).transpose(1, 0, 2))
        in_maps.append(m)
    res = run_bass_kernel_spmd(nc, in_maps, core_ids=list(range(NCORES)))
    y_prompt = np.zeros((32, 256, 1024), np.float32)
    na_kv = np.zeros((32, L, 2, 4, 256, 64), np.float32)
    diff_kv = np.zeros((32, L, 2, 4, 256, 64), np.float32)
    for c in range(NCORES):
        r = res.results[c]
        y_prompt[4 * c:4 * c + 4] = r["yc"].transpose(2, 1, 0).reshape(4, 256, 1024)
        for (kT, vo, dst) in [("nkT", "nvo", na_kv), ("dkT", "dvo", diff_kv)]:
            k = r[kT]
            k = k.transpose(0, 2, 1, 3).reshape(L, 4, 64, 4, 256)
            dst[4 * c:4 * c + 4, :, 0] = k.transpose(3, 0, 1, 4, 2)
            v = r[vo]
            v = v.transpose(0, 2, 1, 3).reshape(L, 4, 256, 4, 64)
            dst[4 * c:4 * c + 4, :, 1] = v.transpose(1, 0, 3, 2, 4)
    y_sample = np.stack([res.results[4 * b]["yl"].transpose(2, 1, 0).reshape(SEQL, 1024) for b in range(2)], axis=0)
    return (y_prompt, y_sample, na_kv, diff_kv)
```

```python
import math
import numpy as np
from contextlib import ExitStack
import concourse.bass as bass
import concourse.mybir as mybir
from concourse.bass_utils import run_bass_kernel_spmd

F32 = mybir.dt.float32
BF16 = mybir.dt.bfloat16
U32 = mybir.dt.uint32
AF = mybir.ActivationFunctionType
ALU = mybir.AluOpType
AX = mybir.AxisListType

NCORES = 8
D = 1024
KC = 8
TB = 256
L = 2
EPS = 1e-6
NEXP_C = 128
ND = 12


class Buf:
    __slots__ = ("name", "w", "r")

    def __init__(self, name):
        self.name = name
        self.w = None
        self.r = {}


class Tl:
    def __init__(self, t, name):
        self.t = t
        self.b = Buf(name)

    def __getitem__(self, idx):
        return self.t[idx]


class FW:
    def __init__(self, nc, es):
        self.nc = nc
        self.es = es
        self.eng = {"pe": nc.tensor, "dve": nc.vector, "act": nc.scalar, "pool": nc.gpsimd, "sp": nc.sync}
        self.sem = {k: es.enter_context(nc.semaphore("sem_" + k)) for k in ["pe", "dve", "act", "pool"]}
        self.cnt = {k: 0 for k in self.sem}
        self.seen = {k: {} for k in self.eng}
        self.dsem = {q: [es.enter_context(nc.semaphore("d_%s_%d" % (q, i))) for i in range(ND)] for q in ["sp", "pool"]}
        self.dcnt = {q: [0] * ND for q in self.dsem}
        self.dnext = {q: 0 for q in self.dsem}
        self.nins = 0

    def _wait(self, E, tok):
        if tok is None:
            return
        if tok[0] == "c":
            _, Dn, n = tok
            if Dn == E and E == "pe":
                return
            key = ("c", Dn)
            if self.seen[E].get(key, 0) >= n:
                return
            self.eng[E].wait_ge(self.sem[Dn], n)
            self.seen[E][key] = n
        else:
            _, q, i, n = tok
            key = ("d", q, i)
            if self.seen[E].get(key, 0) >= n:
                return
            self.eng[E].wait_ge(self.dsem[q][i], n)
            self.seen[E][key] = n

    def _deps(self, E, reads, writes, disjoint=False):
        for b in reads:
            self._wait(E, b.w)
        for b in writes:
            if not (disjoint and b.w is not None and b.w[0] == "c" and b.w[1] == E):
                self._wait(E, b.w)
            for t in list(b.r.values()):
                if disjoint and t[0] == "c" and t[1] == E:
                    continue
                self._wait(E, t)

    def _upd(self, E, tok, reads, writes):
        for b in reads:
            b.r[E] = tok
        for b in writes:
            b.w = tok
            b.r = {}

    def op(self, E, fn, reads=(), writes=(), inc=True, disjoint=False):
        reads = [x.b if isinstance(x, Tl) else x for x in reads]
        writes = [x.b if isinstance(x, Tl) else x for x in writes]
        self._deps(E, reads, writes, disjoint)
        ins = fn()
        self.nins += 1
        if inc:
            ins.then_inc(self.sem[E], 1)
            self.cnt[E] += 1
            tok = ("c", E, self.cnt[E])
        else:
            tok = ("c", E, self.cnt[E] + 1)
        self._upd(E, tok, reads, writes)

    def dma(self, q, out, in_, reads=(), writes=(), **kw):
        reads = [x.b if isinstance(x, Tl) else x for x in reads]
        writes = [x.b if isinstance(x, Tl) else x for x in writes]
        i = self.dnext[q]
        self.dnext[q] = (i + 1) % ND
        if self.dcnt[q][i] > 0:
            self._wait(q, ("d", q, i, self.dcnt[q][i]))
        self._deps(q, reads, writes)
        ins = self.eng[q].dma_start(out=out, in_=in_, **kw)
        self.nins += 1
        ins.then_inc(self.dsem[q][i], 16)
        self.dcnt[q][i] += 16
        tok = ("d", q, i, self.dcnt[q][i])
        self._upd(q, tok, reads, writes)

    def barrier(self):
        for E in ["pe", "dve", "act", "pool", "sp"]:
            for Dn in self.sem:
                if Dn != E and self.cnt[Dn] > 0:
                    self._wait(E, ("c", Dn, self.cnt[Dn]))
            for q in self.dsem:
                for i in range(ND):
                    if self.dcnt[q][i] > 0:
                        self._wait(E, ("d", q, i, self.dcnt[q][i]))

    def finish(self):
        for q in self.dsem:
            for i in range(ND):
                if self.dcnt[q][i] > 0:
                    self._wait("sp", ("d", q, i, self.dcnt[q][i]))
        for Dn in self.sem:
            if self.cnt[Dn] > 0:
                self._wait("sp", ("c", Dn, self.cnt[Dn]))


PC_N1G, PC_N2G, PC_BMOD, PC_CONVW, PC_CONVB, PC_CLNG, PC_CLNB = 0, 8, 16, 64, 126, 128, 130
PC_NAQ, PC_NAK, PC_DQ1, PC_DQ2, PC_DK, PC_SUB = 132, 133, 134, 135, 136, 137
PC_NAQE, PC_NAQO, PC_D1E, PC_D1O, PC_D2E, PC_D2O = 138, 139, 140, 141, 142, 143
PC_DQ = 144
NPC = 145
CC_ID, CC_B64, CC_B32, CC_O1024, CC_O256, CC_OE, CC_OO, CC_IOTA = 0, 128, 256, 384, 512, 640, 768, 896
CC_IOTA16 = 1024
CC_MASK = 1040
CC_PSW = 1044
NCC = 1044 + 128


def build_program(n_seq=4, debug=False, stage=9, nl=L, n_lat=16, lat_stage=9):
    nc = bass.Bass("TRN2", target_bir_lowering=False)
    NT = max(n_seq, 1) * TB
    dr = {}

    def din(name, shape, dt=F32):
        dr[name] = nc.dram_tensor(name, list(shape), dt, kind="ExternalInput").ap()
        return dr[name]

    def dout(name, shape, dt=F32):
        dr[name] = nc.dram_tensor(name, list(shape), dt, kind="ExternalOutput").ap()
        return dr[name]

    xc = din("xc", [128, KC, NT])
    condT = din("condT", [128, KC, 2])
    consts = din("consts", [128, NCC])
    pp = din("pp", [L, 128, NPC])
    rowp = din("rowp", [L, 512])
    bsT = din("bsT", [L, 128, 2, 128])
    wsT = din("wsT", [L, 128, 4, 128])
    dlam = din("dlam", [L, 128])
    w_mod = din("w_mod", [L, D, 6 * D])
    w_in = din("w_in", [L, D, 2560])
    w_out = din("w_out", [L, D, D])
    wq = din("wq", [L, D, 2048])
    skT = din("skT", [L, 128, 2, 128])
    puT = din("puT", [L, NEXP_C, 128, KC * 128])
    pv = din("pv", [L, 128 * 128, D])
    yc = dout("yc", [128, KC, NT])
    nkT = dout("nkT", [L, 128, 2, NT])
    dkT = dout("dkT", [L, 128, 2, NT])
    nvo = dout("nvo", [L, 128, NT // 128, 256])
    dvo = dout("dvo", [L, 128, NT // 128, 256])
    if debug:
        dbg = dout("dbg", [128, 8, TB])
    pubf = nc.dram_tensor("pubf", [L, NEXP_C, 128, KC * 128], BF16, kind="Internal").ap()
    pvbf = nc.dram_tensor("pvbf", [L, NEXP_C, 128, D], BF16, kind="Internal").ap()
    if n_lat:
        xl = din("xl", [128, KC, SEQL])
        nckT = din("nckT", [L, 128, 2, 256])
        ncv = din("ncv", [L, 128, 2, 256])
        dckT = din("dckT", [L, 128, 2, 256])
        dcv = din("dcv", [L, 128, 2, 256])
        biasT = din("biasT", [L, 128, NCOMBO * 4, 64])
        cosT = din("cosT", [128, SEQL])
        sinT = din("sinT", [128, SEQL])
        yl = dout("yl", [128, KC, SEQL])
        ybuf = nc.dram_tensor("ybuf", [128, 2, SEQL + 30], BF16, kind="Internal").ap()
        nks = nc.dram_tensor("nks", [128, 2, SEQL], BF16, kind="Internal").ap()
        nvs = nc.dram_tensor("nvs", [128, SEQL // 128, 256], BF16, kind="Internal").ap()

    with ExitStack() as es:
        fw = FW(nc, es)

        uid = [0]
        ALLOC = {}
        ALLOCN = {}

        def sb(name, shape, dt=F32, ctx=None):
            uid[0] += 1
            name = "%s_%d" % (name, uid[0])
            nb = int(np.prod(shape[1:])) * (2 if dt == BF16 else 4)
            cx = ctx or es
            if not hasattr(cx, "_bytes"):
                cx._bytes = 0
                cx._nm = name
                ALLOC[len(ALLOC)] = cx
            cx._bytes += nb
            return Tl((ctx or es).enter_context(nc.sbuf_tensor(name, list(shape), dt)), name)

        V, A_, PE_, PO = "dve", "act", "pe", "pool"

        def rs(out_ap, in_ap, rt, wt):
            fw.op(A_, lambda: nc.scalar.activation(out=out_ap, in_=in_ap, func=AF.Sqrt, bias=epst[:, 0:1], scale=1.0), list(rt) + [epst], wt)
            fw.op(V, lambda: nc.vector.reciprocal(out=out_ap, in_=out_ap), wt, wt)

        cst = sb("cst", [128, NCC])
        identF = cst[:, CC_ID:CC_ID + 128]
        blk64 = cst[:, CC_B64:CC_B64 + 128]
        blk32 = cst[:, CC_B32:CC_B32 + 128]
        o1024 = cst[:, CC_O1024:CC_O1024 + 128]
        o256 = cst[:, CC_O256:CC_O256 + 128]
        iotaN = cst[:, CC_IOTA:CC_IOTA + 128]
        iota16 = cst[:, CC_IOTA16:CC_IOTA16 + 16]
        cstb = sb("cstb", [128, 4, 128], BF16)
        ppt = sb("ppt", [128, L, NPC])
        modt = sb("modt", [128, L, 48, 2])
        sc1 = sb("sc1", [128, L, 2, 2, KC])
        lamt = sb("lamt", [128, L, 4])
        subg = sb("subg", [128, L])
        xt = sb("xt", [128, KC, TB])
        ps = [Tl(es.enter_context(nc.psum_tensor("ps%d" % i, [128, 512], F32)), "ps%d" % i) for i in range(8)]

        epst = sb("epst", [128, 1])
        fw.op(V, lambda: nc.vector.memset(epst[:, :], EPS), [], [epst])
        fw.dma("sp", cst[:, :], consts[:, :], writes=[cst])
        fw.dma("sp", ppt[:, :, :], pp.rearrange("l p n -> p l n"), writes=[ppt])
        fw.op(V, lambda: nc.vector.tensor_copy(out=cstb[:, 0, :], in_=identF), [cst], [cstb])
        fw.op(V, lambda: nc.vector.tensor_copy(out=cstb[:, 1, :], in_=cst[:, CC_OE:CC_OE + 128]), [cst], [cstb])
        fw.op(V, lambda: nc.vector.tensor_copy(out=cstb[:, 2, :], in_=cst[:, CC_OO:CC_OO + 128]), [cst], [cstb])
        fw.op(V, lambda: nc.vector.tensor_copy(out=cstb[:, 3, :], in_=iotaN), [cst], [cstb])
        identB, onesE, onesO, iotaB = cstb[:, 0, :], cstb[:, 1, :], cstb[:, 2, :], cstb[:, 3, :]

        import os
        sub = int(os.environ.get("SUB", "99"))
        with ExitStack() as ph:
            cdt = sb("cdt", [128, KC, 2], ctx=ph)
            sct = sb("sct", [128, KC, 2], ctx=ph)
            wm = [sb("wm%d" % i, [128, KC, 512], ctx=ph) for i in range(2)]
            dl = sb("dl", [128, L, 128], ctx=ph)
            dl2 = sb("dl2", [128, L, 2, 32], ctx=ph)
            dl3 = sb("dl3", [128, L, 2], ctx=ph)
            fw.dma("sp", cdt[:, :, :], condT[:, :, :], writes=[cdt])
            fw.op(A_, lambda: nc.scalar.activation(out=sct[:, :, :], in_=cdt[:, :, :], func=AF.Silu), [cdt], [sct])
            fw.dma("sp", dl[:, :, :], dlam.partition_broadcast(128), writes=[dl])
            for l in range(L if sub >= 1 else 0):
                for sl in range(12 if sub >= 2 else 0):
                    w = wm[sl % 2]
                    fw.dma("sp", w[:, :, :], w_mod[l].rearrange("(k p) n -> p k n", p=128)[:, :, sl * 512:(sl + 1) * 512],
                           writes=[w])
                    for oc4 in range(4):
                        oc = sl * 4 + oc4
                        for k in range(KC):
                            fw.op(PE_, lambda k=k, oc=oc, oc4=oc4, w=w: nc.tensor.matmul(
                                ps[0][:, oc * 2:oc * 2 + 2], lhsT=w[:, k, oc4 * 128:(oc4 + 1) * 128], rhs=sct[:, k, :],
                                start=(k == 0), stop=(k == KC - 1)), [w, sct], [ps[0]], inc=(k == KC - 1))
                if sub < 3:
                    continue
                fw.op(V, lambda l=l: nc.vector.tensor_tensor(
                    out=modt[:, l, :, :], in0=ps[0][:, 0:96].rearrange("p (a b) -> p a b", b=2),
                    in1=ppt[:, l, PC_BMOD:PC_BMOD + 48].unsqueeze(2).to_broadcast([128, 48, 2]), op=ALU.add),
                    [ps[0], ppt], [modt])
                for j, (vec, pc) in enumerate([(1, PC_N1G), (4, PC_N2G)]):
                    for cd in range(2):
                        fw.op(V, lambda l=l, j=j, vec=vec, pc=pc, cd=cd: nc.vector.scalar_tensor_tensor(
                            out=sc1[:, l, j, cd, :], in0=modt[:, l, vec * 8:(vec + 1) * 8, cd], scalar=1.0,
                            in1=ppt[:, l, pc:pc + 8], op0=ALU.add, op1=ALU.mult), [modt, ppt], [sc1])
                if sub < 4:
                    continue
                fw.op(V, lambda l=l: nc.vector.tensor_tensor(
                    out=dl2[:, l, :, :], in0=dl[:, l, :].rearrange("p (a b c) -> p a b c", a=2, b=2)[:, :, 0, :],
                    in1=dl[:, l, :].rearrange("p (a b c) -> p a b c", a=2, b=2)[:, :, 1, :], op=ALU.mult), [dl], [dl2])
                fw.op(V, lambda l=l: nc.vector.tensor_reduce(out=dl3[:, l, :], in_=dl2[:, l, :, :], axis=AX.X, op=ALU.add),
                      [dl2], [dl3])
                fw.op(A_, lambda l=l: nc.scalar.activation(out=dl3[:, l, :], in_=dl3[:, l, :], func=AF.Exp), [dl3], [dl3])
                lam_init = 0.8 - 0.6 * math.exp(-0.3 * l)
                fw.op(V, lambda l=l, li=lam_init: nc.vector.scalar_tensor_tensor(
                    out=lamt[:, l, 0:1], in0=dl3[:, l, 1:2], scalar=-li, in1=dl3[:, l, 0:1],
                    op0=ALU.add, op1=ALU.subtract), [dl3], [lamt])
                fw.op(V, lambda l=l, li=lam_init: nc.vector.tensor_scalar(
                    out=subg[:, l:l + 1], in0=ppt[:, l, PC_SUB:PC_SUB + 1], scalar1=(1.0 - li), scalar2=None,
                    op0=ALU.mult), [ppt], [subg])
            fw.barrier()

        WSC = Buf("wscratch")
        with ExitStack() as ph:
            cvt = [sb("cvt%d" % i, [128, 8, D], BF16, ctx=ph) for i in range(3)]
            ci_ = 0
            for l in range(nl):
                pvv = pv[l].rearrange("(n c) d -> c n d", c=128)
                for c0 in range(0, NEXP_C, 8):
                    for (src_, dst_) in [(puT[l, c0:c0 + 8], pubf[l, c0:c0 + 8]), (pvv[c0:c0 + 8], pvbf[l, c0:c0 + 8])]:
                        t_ = cvt[ci_ % 3]
                        ci_ += 1
                        fw.dma(PO, t_[:, :, :], src_.rearrange("c p n -> p c n"), writes=[t_])
                        fw.dma("sp", dst_.rearrange("c p n -> p c n"), t_[:, :, :], reads=[t_], writes=[WSC])
            fw.barrier()

        def norm_tmps(ctx, nb=0):
            return ([sb("nsq%d_%d" % (nb, i), [128, TB], ctx=ctx) for i in range(2)], sb("nrstd%d" % nb, [128, TB], ctx=ctx),
                    [sb("ntmp%d_%d" % (nb, i), [128, TB], ctx=ctx) for i in range(2)])

        def norm_mod(ctx, l, which, hb, nb, cond=0, xt=xt, P=None, tmps=None):
            sq, rstd, tmp = tmps if tmps is not None else norm_tmps(ctx, nb)
            P = P if P is not None else ps[1]
            for k in range(KC):
                s = sq[k % 2]
                fw.op(A_, lambda k=k, s=s: nc.scalar.activation(out=s[:, :], in_=xt[:, k, :], func=AF.Square), [xt], [s])
                fw.op(PE_, lambda k=k, s=s: nc.tensor.matmul(P[:, 0:TB], lhsT=o1024, rhs=s[:, :], start=(k == 0),
                                                             stop=(k == KC - 1)), [cst, s], [P], inc=True)
            rs(rstd[:, :], P[:, 0:TB], [P], [rstd])
            shv = 0 if which == 0 else 3
            for k in range(KC):
                t = tmp[k % 2]
                fw.op(V, lambda k=k, t=t: nc.vector.tensor_tensor(out=t[:, :], in0=xt[:, k, :], in1=rstd[:, :], op=ALU.mult),
                      [xt, rstd], [t])
                fw.op(A_, lambda k=k, t=t: nc.scalar.activation(
                    out=hb[:, k, :], in_=t[:, :], func=AF.Identity, scale=sc1[:, l, which, cond, k:k + 1],
                    bias=modt[:, l, shv * 8 + k, cond:cond + 1]), [t, sc1, modt], [hb])

        def load_w(q, tile_, src, **kw):
            fw.dma(q, tile_[:], src, writes=[tile_], **kw)

        def mixers(l, s):
            t0 = s * TB
            with ExitStack() as ph:
                hb = sb("hb", [128, KC, TB], BF16, ctx=ph)
                mixed = sb("mixed", [128, KC, TB], BF16, ctx=ph)
                wseg = [sb("wseg%d" % i, [128, KC, 256], BF16, ctx=ph) for i in range(3)]
                wo = sb("wo", [128, KC, D], BF16, ctx=ph)
                wsb = sb("wsb", [128, 4, 128], BF16, ctx=ph)
                rowt = sb("rowt", [128, 512], ctx=ph)
                bst = sb("bst", [128, 2, 128], ctx=ph)
                diagw = sb("diagw", [128, 2, 31, 128], BF16, ctx=ph)
                ypad = sb("ypad", [128, 2, TB + 30], BF16, ctx=ph)
                f = [sb("f%d" % i, [128, 512], ctx=ph) for i in range(8)]
                qb = sb("qb", [128, 2, 2, TB], BF16, ctx=ph)
                q2b = sb("q2b", [128, 2, 2, TB], BF16, ctx=ph)
                kb = sb("kb", [128, 2, TB], BF16, ctx=ph)
                kout = sb("kout", [128, 2, TB], ctx=ph)
                vout = sb("vout", [128, 2, 256], ctx=ph)
                vpad = sb("vpad", [128, 2, 4, 128], BF16, ctx=ph)
                pt = [sb("pt%d" % i, [128, 512], BF16, ctx=ph) for i in range(2)]
                gu = sb("gu", [128, 2, TB], ctx=ph)
                bnst = sb("bnst", [128, 2, 8], ctx=ph)

                norm_mod(ph, l, 0, hb, 0)
                load_w(PO, wo, w_out[l].rearrange("(k p) n -> p k n", p=128))
                load_w(PO, wsb, wsT[l])
                fw.dma("sp", rowt[:, :], rowp[l].partition_broadcast(128), writes=[rowt])
                fw.dma("sp", bst[:, :, :], bsT[l], writes=[bst])
                fw.op(V, lambda: nc.vector.memset(ypad[:, :, :], 0.0), [], [ypad])
                fw.op(V, lambda: nc.vector.memset(vpad[:, :, :, :], 0.0), [], [vpad])
                for cc in range(2):
                    for tap in range(31):
                        fw.op(V, lambda cc=cc, tap=tap, e=nc.vector: e.tensor_scalar(
                            out=diagw[:, cc, tap, :], in0=identB, scalar1=ppt[:, l, PC_CONVW + cc * 31 + tap:PC_CONVW + cc * 31 + tap + 1],
                            scalar2=None, op0=ALU.mult), [cstb, ppt], [diagw])

                msub = float(os.environ.get("MSUB", "99"))
                if msub < 2:
                    fw.barrier()
                    return
                wslot = [0]

                def seg_w(seg):
                    w = wseg[wslot[0] % 3]
                    wslot[0] += 1
                    load_w(PO, w, w_in[l].rearrange("(k p) n -> p k n", p=128)[:, :, seg * 256:(seg + 1) * 256])
                    return w

                def fm(w, cc, P, col=0):
                    for k in range(KC):
                        fw.op(PE_, lambda k=k: nc.tensor.matmul(P[:, col:col + TB], lhsT=w[:, k, cc * 128:(cc + 1) * 128],
                                                                rhs=hb[:, k, :], start=(k == 0), stop=(k == KC - 1)),
                              [w, hb], [P], inc=(k == KC - 1))

                def tm(w, tl, P, col=0):
                    for k in range(KC):
                        fw.op(PE_, lambda k=k: nc.tensor.matmul(P[:, col:col + 256], lhsT=hb[:, k, tl * 128:(tl + 1) * 128],
                                                                rhs=w[:, k, :], start=(k == 0), stop=(k == KC - 1)),
                              [w, hb], [P], inc=(k == KC - 1))

                wa = seg_w(0)
                wg = seg_w(1)
                for cc in range(2):
                    fm(wa, cc, ps[2], 0)
                    fm(wg, cc, ps[2], TB)
                    fw.op(A_, lambda: nc.scalar.activation(out=f[0][:, 0:TB], in_=ps[2][:, TB:2 * TB], func=AF.Sigmoid),
                          [ps[2]], [f[0]])
                    fw.op(V, lambda cc=cc: nc.vector.tensor_tensor(out=ypad[:, cc, 15:15 + TB], in0=ps[2][:, 0:TB],
                                                                    in1=f[0][:, 0:TB], op=ALU.mult), [ps[2], f[0]], [ypad])
                for cc in range(2):
                    for tap in range(31):
                        fw.op(PE_, lambda cc=cc, tap=tap: nc.tensor.matmul(
                            ps[3][:, cc * TB:(cc + 1) * TB], lhsT=diagw[:, cc, tap, :], rhs=ypad[:, cc, tap:tap + TB],
                            start=(tap == 0), stop=(tap == 30)), [diagw, ypad], [ps[3]], inc=(tap == 30))
                    fw.op(A_, lambda cc=cc: nc.scalar.activation(
                        out=f[1][:, cc * TB:(cc + 1) * TB], in_=ps[3][:, cc * TB:(cc + 1) * TB], func=AF.Identity,
                        bias=ppt[:, l, PC_CONVB + cc:PC_CONVB + cc + 1], scale=1.0), [ps[3], ppt], [f[1]])
                for cc in range(2):
                    fw.op(PE_, lambda cc=cc: nc.tensor.matmul(ps[2][:, 0:TB], lhsT=o256, rhs=f[1][:, cc * TB:(cc + 1) * TB],
                                                              start=(cc == 0), stop=(cc == 1)), [cst, f[1]], [ps[2]], inc=(cc == 1))
                for cc in range(2):
                    fw.op(V, lambda cc=cc: nc.vector.tensor_tensor(out=f[2][:, cc * TB:(cc + 1) * TB], in0=f[1][:, cc * TB:(cc + 1) * TB],
                                                                    in1=ps[2][:, 0:TB], op=ALU.subtract), [f[1], ps[2]], [f[2]])
                fw.op(A_, lambda: nc.scalar.activation(out=f[3][:, :], in_=f[2][:, :], func=AF.Square), [f[2]], [f[3]])
                for cc in range(2):
                    fw.op(PE_, lambda cc=cc: nc.tensor.matmul(ps[3][:, 0:TB], lhsT=o256, rhs=f[3][:, cc * TB:(cc + 1) * TB],
                                                              start=(cc == 0), stop=(cc == 1)), [cst, f[3]], [ps[3]], inc=(cc == 1))
                rs(f[4][:, 0:TB], ps[3][:, 0:TB], [ps[3]], [f[4]])
                for cc in range(2):
                    fw.op(V, lambda cc=cc: nc.vector.tensor_tensor(out=f[3][:, cc * TB:(cc + 1) * TB], in0=f[2][:, cc * TB:(cc + 1) * TB],
                                                                    in1=f[4][:, 0:TB], op=ALU.mult), [f[2], f[4]], [f[3]])
                    fw.op(A_, lambda cc=cc: nc.scalar.activation(
                        out=mixed[:, cc, :], in_=f[3][:, cc * TB:(cc + 1) * TB], func=AF.Silu,
                        scale=ppt[:, l, PC_CLNG + cc:PC_CLNG + cc + 1], bias=ppt[:, l, PC_CLNB + cc:PC_CLNB + cc + 1]),
                        [f[3], ppt], [mixed])

                if msub < 3:
                    fw.barrier()
                    return
                def qk_norm(w, cc, blk, gcol, dst_list, fout=None):
                    P = ps[4]
                    fm(w, cc, P, 0)
                    fw.op(A_, lambda: nc.scalar.activation(out=f[5][:, 0:TB], in_=P[:, 0:TB], func=AF.Square), [P], [f[5]])
                    fw.op(PE_, lambda: nc.tensor.matmul(P[:, TB:2 * TB], lhsT=blk, rhs=f[5][:, 0:TB], start=True, stop=True),
                          [cst, f[5]], [P])
                    rs(f[6][:, 0:TB], P[:, TB:2 * TB], [P], [f[6]])
                    for (gc, dst, tl_) in dst_list:
                        fw.op(V, lambda gc=gc, dst=dst: nc.vector.scalar_tensor_tensor(
                            out=dst, in0=P[:, 0:TB], scalar=ppt[:, l, gc:gc + 1], in1=f[6][:, 0:TB],
                            op0=ALU.mult, op1=ALU.mult), [P, ppt, f[6]], [tl_])

                def v_proj(w, outd):
                    for tl in range(2):
                        P = ps[5]
                        tm(w, tl, P, 0)
                        fw.op(A_, lambda tl=tl: nc.scalar.copy(out=vout[:, tl, :], in_=P[:, 0:256]), [P], [vout])
                        for par in range(2 if os.environ.get("VP", "1") == "1" else 0):
                            fw.op(A_, lambda tl=tl, par=par: nc.scalar.activation(
                                out=vpad[:, tl, :, :].rearrange("p (c r) n -> p c r n", r=2)[:, :, par, par * 64:par * 64 + 64],
                                in_=P[:, 0:256].rearrange("p (c r n) -> p c r n", c=2, r=2)[:, :, par, :], func=AF.Identity), [P], [vpad])
                    fw.dma("sp", outd[l, :, 2 * s:2 * s + 2, :], vout[:, :, :], reads=[vout])

                def attend(cc, qlist, scale):
                    outs = []
                    att = int(os.environ.get("ATT", "9"))
                    for qi, qt_ in enumerate(qlist):
                        OP = ps[6 + qi]
                        for kt in range(2):
                            SP = ps[(kt + 2 * qi) % 4]
                            p_ = pt[(kt + qi) % 2]
                            for par in range(2):
                                fw.op(PE_, lambda par=par, kt=kt, qt_=qt_, SP=SP: nc.tensor.matmul(
                                    SP[:, par * TB:(par + 1) * TB], lhsT=kb[:, cc, kt * 128:(kt + 1) * 128],
                                    rhs=qt_[:, par, cc, :], start=True, stop=True), [kb, qt_], [SP], inc=(par == 1))
                            fw.op(A_, lambda SP=SP, p_=p_: nc.scalar.activation(out=p_[:, :], in_=SP[:, :], func=AF.Exp, scale=scale),
                                  [SP], [p_])
                            if att < 2:
                                continue
                            for par in range(2):
                                fw.op(PE_, lambda par=par, kt=kt, p_=p_, OP=OP: nc.tensor.matmul(
                                    OP[:, 0:TB], lhsT=vpad[:, kt, 2 * cc + par, :], rhs=p_[:, par * TB:(par + 1) * TB],
                                    start=(kt == 0 and par == 0), stop=(kt == 1 and par == 1), skip_group_check=True),
                                    [vpad, p_], [OP], inc=False)
                            for par in range(2):
                                fw.op(PE_, lambda par=par, kt=kt, p_=p_, OP=OP: nc.tensor.matmul(
                                    OP[:, TB:2 * TB], lhsT=(onesE if par == 0 else onesO), rhs=p_[:, par * TB:(par + 1) * TB],
                                    start=False, stop=(kt == 1 and par == 1), skip_group_check=True),
                                    [cstb, p_], [OP], inc=(par == 1))
                        outs.append(OP)
                    return outs

                wqn = seg_w(2)
                wkn = seg_w(3)
                wvn = seg_w(4)
                for cc in range(2):
                    qk_norm(wqn, cc, blk64, None, [(PC_NAQE, qb[:, 0, cc, :], qb), (PC_NAQO, qb[:, 1, cc, :], qb)])
                    qk_norm(wkn, cc, blk64, None, [(PC_NAK, kout[:, cc, :], kout)])
                    fw.op(A_, lambda cc=cc: nc.scalar.copy(out=kb[:, cc, :], in_=kout[:, cc, :]), [kout], [kb])
                fw.dma("sp", nkT[l, :, :, t0:t0 + TB], kout[:, :, :], reads=[kout])
                if msub < 3.3:
                    fw.barrier()
                    return
                v_proj(wvn, nvo)
                if msub < 3.6:
                    fw.barrier()
                    return
                for cc in range(2):
                    (OP,) = attend(cc, [qb], 0.125)
                    if int(os.environ.get("ATT", "9")) < 3:
                        continue
                    fw.op(V, lambda OP=OP: nc.vector.reciprocal(out=f[7][:, 0:TB], in_=OP[:, TB:2 * TB]), [OP], [f[7]])
                    fw.op(V, lambda cc=cc, OP=OP: nc.vector.tensor_tensor(out=mixed[:, 2 + cc, :], in0=OP[:, 0:TB], in1=f[7][:, 0:TB],
                                                                           op=ALU.mult), [OP, f[7]], [mixed])

                if msub < 4:
                    fw.barrier()
                    return
                wqd = seg_w(5)
                wkd = seg_w(6)
                wvd = seg_w(7)
                for cc in range(2):
                    qk_norm(wqd, cc, blk32, None, [(PC_D1E, qb[:, 0, cc, :], qb), (PC_D1O, qb[:, 1, cc, :], qb),
                                                   (PC_D2E, q2b[:, 0, cc, :], q2b), (PC_D2O, q2b[:, 1, cc, :], q2b)])
                    qk_norm(wkd, cc, blk32, None, [(PC_DK, kout[:, cc, :], kout)])
                    fw.op(A_, lambda cc=cc: nc.scalar.copy(out=kb[:, cc, :], in_=kout[:, cc, :]), [kout], [kb])
                fw.dma("sp", dkT[l, :, :, t0:t0 + TB], kout[:, :, :], reads=[kout])
                v_proj(wvd, dvo)
                for cc in range(2):
                    O1, O2 = attend(cc, [qb, q2b], 32 ** -0.5)
                    fw.op(V, lambda O1=O1: nc.vector.reciprocal(out=f[7][:, 0:TB], in_=O1[:, TB:2 * TB]), [O1], [f[7]])
                    fw.op(V, lambda O2=O2: nc.vector.reciprocal(out=f[7][:, TB:2 * TB], in_=O2[:, TB:2 * TB]), [O2], [f[7]])
                    fw.op(V, lambda O1=O1: nc.vector.tensor_tensor(out=f[5][:, 0:TB], in0=O1[:, 0:TB], in1=f[7][:, 0:TB], op=ALU.mult),
                          [O1, f[7]], [f[5]])
                    fw.op(V, lambda O2=O2: nc.vector.tensor_tensor(out=f[5][:, TB:2 * TB], in0=O2[:, 0:TB], in1=f[7][:, TB:2 * TB],
                                                                    op=ALU.mult), [O2, f[7]], [f[5]])
                    fw.op(V, lambda: nc.vector.scalar_tensor_tensor(out=f[6][:, 0:TB], in0=f[5][:, TB:2 * TB], scalar=lamt[:, l, 0:1],
                                                                    in1=f[5][:, 0:TB], op0=ALU.mult, op1=ALU.add), [f[5], lamt], [f[6]])
                    fw.op(A_, lambda: nc.scalar.activation(out=f[6][:, TB:2 * TB], in_=f[6][:, 0:TB], func=AF.Square), [f[6]], [f[6]])
                    fw.op(PE_, lambda: nc.tensor.matmul(ps[4][:, 0:TB], lhsT=blk64, rhs=f[6][:, TB:2 * TB], start=True, stop=True),
                          [cst, f[6]], [ps[4]])
                    rs(f[7][:, 0:TB], ps[4][:, 0:TB], [ps[4]], [f[7]])
                    fw.op(V, lambda cc=cc: nc.vector.scalar_tensor_tensor(out=mixed[:, 4 + cc, :], in0=f[6][:, 0:TB], scalar=subg[:, l:l + 1],
                                                                           in1=f[7][:, 0:TB], op0=ALU.mult, op1=ALU.mult),
                          [f[6], subg, f[7]], [mixed])

                if msub < 5:
                    fw.barrier()
                    return
                wu = seg_w(8)
                wv = seg_w(9)
                for cc in range(2):
                    fm(wu, cc, ps[0], 0)
                    fw.op(A_, lambda cc=cc: nc.scalar.activation(out=gu[:, cc, :], in_=ps[0][:, 0:TB], func=AF.Gelu_apprx_tanh),
                          [ps[0]], [gu])
                fw.op(V, lambda: nc.vector.memset(vpad[:, :, :, :], 0.0), [], [vpad])
                for tl in range(2):
                    P = ps[1]
                    tm(wv, tl, P, 0)
                    fw.op(A_, lambda: nc.scalar.activation(out=f[0][:, 0:256], in_=P[:, 0:256], func=AF.Gelu_apprx_tanh), [P], [f[0]])
                    fw.op(V, lambda: nc.vector.bn_stats(out=bnst[:, 0, 0:6], in_=f[0][:, 0:256]), [f[0]], [bnst])
                    fw.op(V, lambda: nc.vector.bn_aggr(out=bnst[:, 1, 0:2], in_=bnst[:, 0, 0:6]), [bnst], [bnst])
                    rs(bnst[:, 1, 2:3], bnst[:, 1, 1:2], [bnst], [bnst])
                    fw.op(V, lambda: nc.vector.tensor_scalar(out=f[0][:, 256:512], in0=f[0][:, 0:256], scalar1=bnst[:, 1, 0:1],
                                                             scalar2=bnst[:, 1, 2:3], op0=ALU.subtract, op1=ALU.mult), [f[0], bnst], [f[0]])
                    fw.op(V, lambda: nc.vector.tensor_tensor(out=f[0][:, 0:256], in0=f[0][:, 256:512], in1=rowt[:, 0:256], op=ALU.mult),
                          [f[0], rowt], [f[0]])
                    for par in range(2):
                        fw.op(V, lambda tl=tl, par=par: nc.vector.tensor_tensor(
                            out=vpad[:, tl, :, :].rearrange("p (c r) n -> p c r n", r=2)[:, :, par, par * 64:par * 64 + 64],
                            in0=f[0][:, 0:256].rearrange("p (c r n) -> p c r n", c=2, r=2)[:, :, par, :],
                            in1=rowt[:, 256:512].rearrange("p (c r n) -> p c r n", c=2, r=2)[:, :, par, :], op=ALU.add),
                            [f[0], rowt], [vpad])
                for cc in range(2):
                    P = ps[2]
                    for tl in range(2):
                        for par in range(2):
                            fw.op(PE_, lambda tl=tl, par=par: nc.tensor.matmul(
                                P[:, tl * 128:(tl + 1) * 128], lhsT=vpad[:, tl, 2 * cc + par, :], rhs=wsb[:, 2 * cc + par, :],
                                start=(par == 0), stop=(par == 1)), [vpad, wsb], [P], inc=(par == 1))
                    fw.op(V, lambda cc=cc: nc.vector.tensor_tensor(
                        out=f[1][:, 0:TB].rearrange("p (a b) -> p a b", a=2), in0=P[:, 0:TB].rearrange("p (a b) -> p a b", a=2),
                        in1=bst[:, cc, :].unsqueeze(1).to_broadcast([128, 2, 128]), op=ALU.add), [P, bst], [f[1]])
                    fw.op(V, lambda cc=cc: nc.vector.tensor_tensor(out=mixed[:, 6 + cc, :], in0=f[1][:, 0:TB], in1=gu[:, cc, :], op=ALU.mult),
                          [f[1], gu], [mixed])

                if debug and l == 0:
                    dbgt = sb("dbgt", [128, KC, TB], ctx=ph)
                    fw.op(A_, lambda: nc.scalar.copy(out=dbgt[:, :, :], in_=mixed[:, :, :]), [mixed], [dbgt])
                    fw.dma("sp", dbg[:, :, :], dbgt[:, :, :], reads=[dbgt])
                if msub < 6:
                    fw.barrier()
                    return
                for j in range(KC):
                    P = ps[3 + (j % 2)]
                    for k in range(KC):
                        fw.op(PE_, lambda k=k, j=j, P=P: nc.tensor.matmul(P[:, 0:TB], lhsT=wo[:, k, j * 128:(j + 1) * 128], rhs=mixed[:, k, :],
                                                                          start=(k == 0), stop=(k == KC - 1)), [wo, mixed], [P], inc=(k == KC - 1))
                    fw.op(V, lambda j=j, P=P: nc.vector.scalar_tensor_tensor(
                        out=xt[:, j, :], in0=P[:, 0:TB], scalar=modt[:, l, 16 + j, 0:1], in1=xt[:, j, :], op0=ALU.mult, op1=ALU.add),
                        [P, modt, xt], [xt])
                fw.barrier()

        def peer(l, s, cond=0):
            with ExitStack() as ph:
                h2 = sb("h2", [128, KC, TB], BF16, ctx=ph)
                wqs = [sb("wqs%d" % i, [128, KC, 256], BF16, ctx=ph) for i in range(2)]
                skb = sb("skb", [128, 2, 128], BF16, ctx=ph)
                qT = sb("qT", [128, 16, TB], BF16, ctx=ph)
                ssb = sb("ssb", [128, 16, 128], ctx=ph)
                swk = sb("swk", [128, 16, 128], ctx=ph)
                sv = sb("sv", [128, 16, 16], ctx=ph)
                cwk_v = swk[:, :, :].rearrange("p (h t) n -> p h (t n)", t=2)
                oh_v = ssb[:, :, :].rearrange("p (h t) (a b) -> p h (t a) b", t=2, a=8)
                si = sb("si", [128, 16, 16], U32, ctx=ph)
                sif = sb("sif", [128, 16, 16], ctx=ph)
                cs = sb("cs", [128, 8, 256], ctx=ph)
                tv = sb("tv", [128, 8, 16], ctx=ph)
                tpos = sb("tpos", [128, 8, 16], U32, ctx=ph)
                tu = sb("tu", [128, 2, 8, 16], U32, ctx=ph)
                tf = sb("tf", [128, 2, 8, 16], ctx=ph)
                trip = sb("trip", [128, 3, 8, 16], ctx=ph)
                tripT = sb("tripT", [128, 3, 128], ctx=ph)
                zz = sb("zz", [128, 8, 2], ctx=ph)
                NTS = 16
                Lhs = [sb("Lh%d" % i, [128, NTS, 128], BF16, ctx=ph) for i in range(2)]
                Rhs = [sb("Rh%d" % i, [128, NTS, 128], BF16, ctx=ph) for i in range(2)]
                Gall = sb("Gall", [128, TB, 128], BF16, ctx=ph)
                NSL = 4
                pus = [sb("pus%d" % i, [128, KC * 128], BF16, ctx=ph) for i in range(NSL)]
                pvs = [sb("pvs%d" % i, [128, D], BF16, ctx=ph) for i in range(NSL)]

                norm_mod(ph, l, 1, h2, 1, cond)
                load_w(PO, skb, skT[l])
                for jp in range(8):
                    w = wqs[jp % 2]
                    load_w(PO, w, wq[l].rearrange("(k p) n -> p k n", p=128)[:, :, jp * 256:(jp + 1) * 256])
                    P = ps[jp % 2]
                    for jj in range(2):
                        for k in range(KC):
                            fw.op(PE_, lambda k=k, jj=jj, w=w, P=P: nc.tensor.matmul(
                                P[:, jj * TB:(jj + 1) * TB], lhsT=w[:, k, jj * 128:(jj + 1) * 128], rhs=h2[:, k, :],
                                start=(k == 0), stop=(k == KC - 1)), [w, h2], [P], inc=(k == KC - 1))
                    fw.op(A_, lambda jp=jp, P=P: nc.scalar.copy(out=qT[:, 2 * jp:2 * jp + 2, :],
                                                                  in_=P[:, :].rearrange("p (a b) -> p a b", a=2)), [P], [qT])
                for tl in range(2):
                    tok = slice(tl * 128, (tl + 1) * 128)
                    for j in range(16):
                        P = ps[2 + j // 4]
                        fw.op(PE_, lambda j=j, P=P: nc.tensor.matmul(P[:, (j % 4) * 128:(j % 4 + 1) * 128], lhsT=qT[:, j, tok],
                                                                     rhs=skb[:, j % 2, :], start=True, stop=True), [qT, skb], [P],
                              inc=(j % 4 == 3))
                    for b4 in range(4):
                        fw.op(A_, lambda b4=b4: nc.scalar.copy(out=ssb[:, 4 * b4:4 * b4 + 4, :],
                                                                 in_=ps[2 + b4][:, :].rearrange("p (a b) -> p a b", a=4)), [ps[2 + b4]], [ssb])
                    for j in range(16):
                        fw.op(V, lambda j=j: nc.vector.max(out=sv[:, j, 0:8], in_=ssb[:, j, :]), [ssb], [sv])
                    for j in range(16):
                        fw.op(V, lambda j=j: nc.vector.max_index(out=si[:, j, 0:8], in_max=sv[:, j, 0:8], in_values=ssb[:, j, :]),
                              [sv, ssb], [si])
                    for j in range(16):
                        fw.op(V, lambda j=j: nc.vector.match_replace(out=swk[:, j, :], in_to_replace=sv[:, j, 0:8],
                                                                     in_values=ssb[:, j, :], imm_value=-1e30), [sv, ssb], [swk])
                    for j in range(16):
                        fw.op(V, lambda j=j: nc.vector.max(out=sv[:, j, 8:16], in_=swk[:, j, :]), [swk], [sv])
                    for j in range(16):
                        fw.op(V, lambda j=j: nc.vector.max_index(out=si[:, j, 8:16], in_max=sv[:, j, 8:16], in_values=swk[:, j, :]),
                              [sv, swk], [si])
                    fw.op(V, lambda: nc.vector.tensor_copy(out=sif[:, :, :], in_=si[:, :, :]), [si], [sif])
                    sv4 = sv[:, :, :].rearrange("p (h t) a -> p h t a", t=2)
                    fw.op(V, lambda: nc.vector.tensor_tensor(
                        out=cs[:, :, :].rearrange("p h (a b) -> p h a b", a=16),
                        in0=sv4[:, :, 0, :].unsqueeze(3).to_broadcast([128, 8, 16, 16]),
                        in1=sv4[:, :, 1, :].unsqueeze(2).to_broadcast([128, 8, 16, 16]), op=ALU.add), [sv], [cs])
                    for h in range(8):
                        fw.op(V, lambda h=h: nc.vector.max(out=tv[:, h, 0:8], in_=cs[:, h, :]), [cs], [tv])
                    for h in range(8):
                        fw.op(V, lambda h=h: nc.vector.max_index(out=tpos[:, h, 0:8], in_max=tv[:, h, 0:8], in_values=cs[:, h, :]),
                              [tv, cs], [tpos])
                    for h in range(8):
                        fw.op(V, lambda h=h: nc.vector.match_replace(out=cwk_v[:, h, :], in_to_replace=tv[:, h, 0:8],
                                                                     in_values=cs[:, h, :], imm_value=-1e30), [tv, cs], [swk])
                    for h in range(8):
                        fw.op(V, lambda h=h: nc.vector.max(out=tv[:, h, 8:16], in_=cwk_v[:, h, :]), [swk], [tv])
                    for h in range(8):
                        fw.op(V, lambda h=h: nc.vector.max_index(out=tpos[:, h, 8:16], in_max=tv[:, h, 8:16], in_values=cwk_v[:, h, :]),
                              [tv, swk], [tpos])
                    fw.op(V, lambda: nc.vector.tensor_tensor(out=trip[:, 2, :, :], in0=tv[:, :, :],
                                                             in1=tv[:, :, 0:1].to_broadcast([128, 8, 16]), op=ALU.subtract), [tv], [trip])
                    fw.op(A_, lambda: nc.scalar.activation(out=trip[:, 2, :, :], in_=trip[:, 2, :, :], func=AF.Exp), [trip], [trip])
                    fw.op(V, lambda: nc.vector.tensor_reduce(out=zz[:, :, 0], in_=trip[:, 2, :, :], axis=AX.X, op=ALU.add), [trip], [zz])
                    fw.op(V, lambda: nc.vector.reciprocal(out=zz[:, :, 1], in_=zz[:, :, 0]), [zz], [zz])
                    fw.op(V, lambda: nc.vector.tensor_tensor(out=trip[:, 2, :, :], in0=trip[:, 2, :, :],
                                                             in1=zz[:, :, 1:2].to_broadcast([128, 8, 16]), op=ALU.mult), [trip, zz], [trip])
                    fw.op(V, lambda: nc.vector.tensor_single_scalar(out=tu[:, 0, :, :], in_=tpos[:, :, :], scalar=4,
                                                                    op=ALU.logical_shift_right), [tpos], [tu])
                    fw.op(V, lambda: nc.vector.tensor_single_scalar(out=tu[:, 1, :, :], in_=tpos[:, :, :], scalar=15,
                                                                    op=ALU.bitwise_and), [tpos], [tu])
                    fw.op(V, lambda: nc.vector.tensor_copy(out=tf[:, :, :, :], in_=tu[:, :, :, :]), [tu], [tf])
                    sif4 = sif[:, :, :].rearrange("p (h t) a -> p h t a", t=2)
                    for w_ in range(2):
                        fw.op(V, lambda w_=w_: nc.vector.tensor_tensor(
                            out=oh_v[:, :, :, :], in0=iota16.unsqueeze(1).unsqueeze(1).to_broadcast([128, 8, 16, 16]),
                            in1=tf[:, w_, :, :].unsqueeze(3).to_broadcast([128, 8, 16, 16]), op=ALU.is_equal), [cst, tf], [ssb])
                        fw.op(V, lambda w_=w_: nc.vector.tensor_tensor(
                            out=oh_v[:, :, :, :], in0=oh_v[:, :, :, :],
                            in1=sif4[:, :, w_, :].unsqueeze(2).to_broadcast([128, 8, 16, 16]), op=ALU.mult), [ssb, sif], [ssb])
                        fw.op(V, lambda w_=w_: nc.vector.tensor_reduce(out=trip[:, w_, :, :], in_=oh_v[:, :, :, :], axis=AX.X, op=ALU.add),
                              [ssb], [trip])
                    P = ps[6]
                    for w_ in range(3):
                        fw.op(PE_, lambda w_=w_: nc.tensor.transpose(out=P[:, w_ * 128:(w_ + 1) * 128],
                                                                     in_=trip[:, w_, :, :].rearrange("p h k -> p (h k)"), identity=identF),
                              [trip, cst], [P], inc=(w_ == 2))
                    fw.op(A_, lambda: nc.scalar.copy(out=tripT[:, :, :], in_=P[:, 0:384].rearrange("p (a b) -> p a b", a=3)), [P], [tripT])
                    for t4 in range(128 // NTS):
                        tsl = slice(t4 * NTS, (t4 + 1) * NTS)
                        Lh = Lhs[t4 % 2]
                        Rh = Rhs[t4 % 2]
                        for tt_ in range(NTS):
                            tg = t4 * NTS + tt_
                            fw.op(V, lambda tt_=tt_, tg=tg, Lh=Lh: nc.vector.tensor_scalar(
                                out=Lh[:, tt_, :], in0=iotaN, scalar1=tripT[:, 0, tg:tg + 1], scalar2=None, op0=ALU.is_equal),
                                [cst, tripT], [Lh], disjoint=(tt_ > 0))
                            fw.op(V, lambda tt_=tt_, tg=tg, Rh=Rh: nc.vector.tensor_scalar(
                                out=Rh[:, tt_, :], in0=iotaN, scalar1=tripT[:, 1, tg:tg + 1], scalar2=tripT[:, 2, tg:tg + 1],
                                op0=ALU.is_equal, op1=ALU.mult), [cst, tripT], [Rh], disjoint=(tt_ > 0))
                        for g4 in range(NTS // 4):
                            P = ps[g4 % 2]
                            for u in range(4):
                                tt_ = g4 * 4 + u
                                fw.op(PE_, lambda tt_=tt_, u=u, P=P, Lh=Lh, Rh=Rh: nc.tensor.matmul(P[:, u * 128:(u + 1) * 128], lhsT=Lh[:, tt_, :],
                                                                                      rhs=Rh[:, tt_, :], start=True, stop=True),
                                      [Lh, Rh], [P], inc=(u == 3))
                            tb_ = tl * 128 + t4 * NTS + g4 * 4
                            fw.op(A_, lambda tb_=tb_, P=P: nc.scalar.copy(out=Gall[:, tb_:tb_ + 4, :],
                                                                          in_=P[:, :].rearrange("p (a b) -> p a b", a=4)), [P], [Gall])
                Aslot = [Buf("aslot%d" % i) for i in range(4)]
                ga = [sb("gax%d" % i, [128, TB], BF16, ctx=ph) for i in range(3)]
                wgt = [sb("wgx%d" % i, [128, TB], BF16, ctx=ph) for i in range(3)]

                def a_part(c):
                    pu_ = pus[c % NSL]
                    pv_ = pvs[c % NSL]
                    fw.dma("sp", pu_[:, :], pubf[l, c], reads=[WSC], writes=[pu_])
                    fw.dma("sp", pv_[:, :], pvbf[l, c], reads=[WSC], writes=[pv_])
                    P = ps[c % 4]
                    col = 0
                    wr = [Aslot[c % 4]] + ([P] if c < 4 else [])
                    for k in range(KC):
                        fw.op(PE_, lambda k=k, pu_=pu_, P=P, col=col: nc.tensor.matmul(
                            P[:, col:col + TB], lhsT=pu_[:, k * 128:(k + 1) * 128], rhs=h2[:, k, :], start=(k == 0), stop=(k == KC - 1)),
                            [pu_, h2], wr, inc=(k == KC - 1))

                def o_part(c):
                    pv_ = pvs[c % NSL]
                    P = ps[c % 4]
                    col = 0
                    g_ = ga[c % 3]
                    w_ = wgt[c % 3]
                    fw.op(A_, lambda g_=g_, P=P, col=col: nc.scalar.activation(out=g_[:, :], in_=P[:, col:col + TB], func=AF.Gelu_apprx_tanh),
                          [Aslot[c % 4]], [g_])
                    fw.op(V, lambda g_=g_, w_=w_, c=c: nc.vector.tensor_tensor(out=w_[:, :], in0=g_[:, :], in1=Gall[:, :, c], op=ALU.mult),
                          [g_, Gall], [w_])
                    for j in range(KC):
                        OPS = ps[4 + j // 2]
                        fw.op(PE_, lambda j=j, OPS=OPS, pv_=pv_, w_=w_, c=c: nc.tensor.matmul(
                            OPS[:, (j % 2) * TB:(j % 2 + 1) * TB], lhsT=pv_[:, j * 128:(j + 1) * 128], rhs=w_[:, :],
                            start=(c == 0 and j % 2 == 0), stop=(c == NEXP_C - 1), skip_group_check=True), [pv_, w_], [OPS], inc=(j == KC - 1))

                LOOK = int(os.environ.get("LOOK", "2"))
                for c in range(min(LOOK, NEXP_C)):
                    a_part(c)
                for c in range(NEXP_C):
                    if LOOK == 0:
                        a_part(c)
                    elif c + LOOK < NEXP_C:
                        a_part(c + LOOK)
                    o_part(c)
                for j in range(KC):
                    OPS = ps[4 + j // 2]
                    fw.op(V, lambda j=j, OPS=OPS: nc.vector.scalar_tensor_tensor(
                        out=xt[:, j, :], in0=OPS[:, (j % 2) * TB:(j % 2 + 1) * TB], scalar=modt[:, l, 40 + j, cond:cond + 1], in1=xt[:, j, :],
                        op0=ALU.mult, op1=ALU.add), [OPS, modt, xt], [xt])
                fw.barrier()


        def peer_pass(l, nblk, cond, xsrc_of, xdst, SBx):
            with ExitStack() as ph:
                xts = [sb("pxt%d" % i, [128, KC, TB], ctx=ph) for i in range(2)]
                h2s = [sb("ph2%d" % i, [128, KC, TB], BF16, ctx=ph) for i in range(2)]
                tTs = [[sb("ptT%d%d" % (i, t), [128, 3, 128], ctx=ph) for t in range(2)] for i in range(2)]
                ntm = norm_tmps(ph, 7)
                wqs = [sb("wqs%d" % i, [128, KC, 256], BF16, ctx=ph) for i in range(2)]
                skb = sb("skb", [128, 2, 128], BF16, ctx=ph)
                qT = sb("qT", [128, 16, TB], BF16, ctx=ph)
                ssb = sb("ssb", [128, 16, 128], ctx=ph)
                swk = sb("swk", [128, 16, 128], ctx=ph)
                sv = sb("sv", [128, 16, 16], ctx=ph)
                cwk_v = swk[:, :, :].rearrange("p (h t) n -> p h (t n)", t=2)
                oh_v = ssb[:, :, :].rearrange("p (h t) (a b) -> p h (t a) b", t=2, a=8)
                si = sb("si", [128, 16, 16], U32, ctx=ph)
                sif = sb("sif", [128, 16, 16], ctx=ph)
                cs = sb("cs", [128, 8, 256], ctx=ph)
                tv = sb("tv", [128, 8, 16], ctx=ph)
                tpos = sb("tpos", [128, 8, 16], U32, ctx=ph)
                tu = sb("tu", [128, 2, 8, 16], U32, ctx=ph)
                tf = sb("tf", [128, 2, 8, 16], ctx=ph)
                trip = sb("trip", [128, 3, 8, 16], ctx=ph)
                zz = sb("zz", [128, 8, 2], ctx=ph)
                NTS = 8
                Lhs = [sb("Lh%d" % i, [128, NTS, 128], BF16, ctx=ph) for i in range(2)]
                Rhs = [sb("Rh%d" % i, [128, NTS, 128], BF16, ctx=ph) for i in range(2)]
                Gall = sb("Gall", [128, TB, 128], BF16, ctx=ph)
                NSL = 4
                pus = [sb("pus%d" % i, [128, KC * 128], BF16, ctx=ph) for i in range(NSL)]
                pvs = [sb("pvs%d" % i, [128, D], BF16, ctx=ph) for i in range(NSL)]
                ga = [sb("gax%d" % i, [128, TB], BF16, ctx=ph) for i in range(3)]
                wgt = [sb("wgx%d" % i, [128, TB], BF16, ctx=ph) for i in range(3)]
                PB = ps[3]
                load_w(PO, skb, skT[l])
                sv4 = sv[:, :, :].rearrange("p (h t) a -> p h t a", t=2)
                sif4 = sif[:, :, :].rearrange("p (h t) a -> p h t a", t=2)

                def pre(s):
                    xt_ = xts[s % 2]
                    h2 = h2s[s % 2]
                    t0 = s * TB
                    fw.dma("sp", xt_[:, :, :], xsrc_of(s), reads=[SBx], writes=[xt_])
                    norm_mod(ph, l, 1, h2, 1, cond, xt=xt_, P=PB, tmps=ntm)
                    yield
                    for jp in range(8):
                        w = wqs[jp % 2]
                        load_w(PO, w, wq[l].rearrange("(k p) n -> p k n", p=128)[:, :, jp * 256:(jp + 1) * 256])
                        for jj in range(2):
                            for k in range(KC):
                                fw.op(PE_, lambda k=k, jj=jj, w=w: nc.tensor.matmul(
                                    PB[:, jj * TB:(jj + 1) * TB], lhsT=w[:, k, jj * 128:(jj + 1) * 128], rhs=h2[:, k, :],
                                    start=(k == 0), stop=(k == KC - 1)), [w, h2], [PB], inc=(k == KC - 1))
                        fw.op(A_, lambda jp=jp: nc.scalar.copy(out=qT[:, 2 * jp:2 * jp + 2, :],
                                                                 in_=PB[:, :].rearrange("p (a b) -> p a b", a=2)), [PB], [qT])
                        yield
                    for tl in range(2):
                        tok = slice(tl * 128, (tl + 1) * 128)
                        tripT = tTs[s % 2][tl]
                        for b4 in range(4):
                            for u in range(4):
                                j = b4 * 4 + u
                                fw.op(PE_, lambda j=j, u=u: nc.tensor.matmul(PB[:, u * 128:(u + 1) * 128], lhsT=qT[:, j, tok],
                                                                             rhs=skb[:, j % 2, :], start=True, stop=True), [qT, skb], [PB],
                                      inc=(u == 3))
                            fw.op(A_, lambda b4=b4: nc.scalar.copy(out=ssb[:, 4 * b4:4 * b4 + 4, :],
                                                                     in_=PB[:, :].rearrange("p (a b) -> p a b", a=4)), [PB], [ssb])
                            yield
                        for j in range(16):
                            fw.op(V, lambda j=j: nc.vector.max(out=sv[:, j, 0:8], in_=ssb[:, j, :]), [ssb], [sv], disjoint=(j > 0))
                        yield
                        for j in range(16):
                            fw.op(V, lambda j=j: nc.vector.max_index(out=si[:, j, 0:8], in_max=sv[:, j, 0:8], in_values=ssb[:, j, :]),
                                  [sv, ssb], [si], disjoint=(j > 0))
                        yield
                        for j in range(16):
                            fw.op(V, lambda j=j: nc.vector.match_replace(out=swk[:, j, :], in_to_replace=sv[:, j, 0:8],
                                                                         in_values=ssb[:, j, :], imm_value=-1e30), [sv, ssb], [swk], disjoint=(j > 0))
                        yield
                        for j in range(16):
                            fw.op(V, lambda j=j: nc.vector.max(out=sv[:, j, 8:16], in_=swk[:, j, :]), [swk], [sv], disjoint=(j > 0))
                        yield
                        for j in range(16):
                            fw.op(V, lambda j=j: nc.vector.max_index(out=si[:, j, 8:16], in_max=sv[:, j, 8:16], in_values=swk[:, j, :]),
                                  [sv, swk], [si], disjoint=(j > 0))
                        fw.op(V, lambda: nc.vector.tensor_copy(out=sif[:, :, :], in_=si[:, :, :]), [si], [sif])
                        fw.op(V, lambda: nc.vector.tensor_tensor(
                            out=cs[:, :, :].rearrange("p h (a b) -> p h a b", a=16),
                            in0=sv4[:, :, 0, :].unsqueeze(3).to_broadcast([128, 8, 16, 16]),
                            in1=sv4[:, :, 1, :].unsqueeze(2).to_broadcast([128, 8, 16, 16]), op=ALU.add), [sv], [cs])
                        yield
                        for h in range(8):
                            fw.op(V, lambda h=h: nc.vector.max(out=tv[:, h, 0:8], in_=cs[:, h, :]), [cs], [tv], disjoint=(h > 0))
                        for h in range(8):
                            fw.op(V, lambda h=h: nc.vector.max_index(out=tpos[:, h, 0:8], in_max=tv[:, h, 0:8], in_values=cs[:, h, :]),
                                  [tv, cs], [tpos], disjoint=(h > 0))
                        yield
                        for h in range(8):
                            fw.op(V, lambda h=h: nc.vector.match_replace(out=cwk_v[:, h, :], in_to_replace=tv[:, h, 0:8],
                                                                         in_values=cs[:, h, :], imm_value=-1e30), [tv, cs], [swk], disjoint=(h > 0))
                        for h in range(8):
                            fw.op(V, lambda h=h: nc.vector.max(out=tv[:, h, 8:16], in_=cwk_v[:, h, :]), [swk], [tv], disjoint=(h > 0))
                        yield
                        for h in range(8):
                            fw.op(V, lambda h=h: nc.vector.max_index(out=tpos[:, h, 8:16], in_max=tv[:, h, 8:16], in_values=cwk_v[:, h, :]),
                                  [tv, swk], [tpos], disjoint=(h > 0))
                        fw.op(V, lambda: nc.vector.tensor_tensor(out=trip[:, 2, :, :], in0=tv[:, :, :],
                                                                 in1=tv[:, :, 0:1].to_broadcast([128, 8, 16]), op=ALU.subtract), [tv], [trip])
                        fw.op(A_, lambda: nc.scalar.activation(out=trip[:, 2, :, :], in_=trip[:, 2, :, :], func=AF.Exp), [trip], [trip])
                        fw.op(V, lambda: nc.vector.tensor_reduce(out=zz[:, :, 0], in_=trip[:, 2, :, :], axis=AX.X, op=ALU.add), [trip], [zz])
                        fw.op(V, lambda: nc.vector.reciprocal(out=zz[:, :, 1], in_=zz[:, :, 0]), [zz], [zz])
                        fw.op(V, lambda: nc.vector.tensor_tensor(out=trip[:, 2, :, :], in0=trip[:, 2, :, :],
                                                                 in1=zz[:, :, 1:2].to_broadcast([128, 8, 16]), op=ALU.mult), [trip, zz], [trip])
                        yield
                        fw.op(V, lambda: nc.vector.tensor_single_scalar(out=tu[:, 0, :, :], in_=tpos[:, :, :], scalar=4,
                                                                        op=ALU.logical_shift_right), [tpos], [tu])
                        fw.op(V, lambda: nc.vector.tensor_single_scalar(out=tu[:, 1, :, :], in_=tpos[:, :, :], scalar=15,
                                                                        op=ALU.bitwise_and), [tpos], [tu])
                        fw.op(V, lambda: nc.vector.tensor_copy(out=tf[:, :, :, :], in_=tu[:, :, :, :]), [tu], [tf])
                        for w_ in range(2):
                            fw.op(V, lambda w_=w_: nc.vector.tensor_tensor(
                                out=oh_v[:, :, :, :], in0=iota16.unsqueeze(1).unsqueeze(1).to_broadcast([128, 8, 16, 16]),
                                in1=tf[:, w_, :, :].unsqueeze(3).to_broadcast([128, 8, 16, 16]), op=ALU.is_equal), [cst, tf], [ssb])
                            fw.op(V, lambda w_=w_: nc.vector.tensor_tensor(
                                out=oh_v[:, :, :, :], in0=oh_v[:, :, :, :],
                                in1=sif4[:, :, w_, :].unsqueeze(2).to_broadcast([128, 8, 16, 16]), op=ALU.mult), [ssb, sif], [ssb])
                            fw.op(V, lambda w_=w_: nc.vector.tensor_reduce(out=trip[:, w_, :, :], in_=oh_v[:, :, :, :], axis=AX.X, op=ALU.add),
                                  [ssb], [trip])
                            yield
                        for w_ in range(3):
                            fw.op(PE_, lambda w_=w_: nc.tensor.transpose(out=PB[:, w_ * 128:(w_ + 1) * 128],
                                                                         in_=trip[:, w_, :, :].rearrange("p h k -> p (h k)"), identity=identF),
                                  [trip, cst], [PB], inc=(w_ == 2))
                        fw.op(A_, lambda tripT=tripT: nc.scalar.copy(out=tripT[:, :, :], in_=PB[:, 0:384].rearrange("p (a b) -> p a b", a=3)),
                              [PB], [tripT])
                        yield

                def gbuild(s):
                    gi = 0
                    for tl in range(2):
                        tripT = tTs[s % 2][tl]
                        for t4 in range(128 // NTS):
                            Lh = Lhs[gi % 2]
                            Rh = Rhs[gi % 2]
                            for tt_ in range(NTS):
                                tg = t4 * NTS + tt_
                                fw.op(V, lambda tt_=tt_, tg=tg, Lh=Lh, tripT=tripT: nc.vector.tensor_scalar(
                                    out=Lh[:, tt_, :], in0=iotaN, scalar1=tripT[:, 0, tg:tg + 1], scalar2=None, op0=ALU.is_equal),
                                    [cst, tripT], [Lh], disjoint=(tt_ > 0))
                                fw.op(V, lambda tt_=tt_, tg=tg, Rh=Rh, tripT=tripT: nc.vector.tensor_scalar(
                                    out=Rh[:, tt_, :], in0=iotaN, scalar1=tripT[:, 1, tg:tg + 1], scalar2=tripT[:, 2, tg:tg + 1],
                                    op0=ALU.is_equal, op1=ALU.mult), [cst, tripT], [Rh], disjoint=(tt_ > 0))
                            for g4 in range(NTS // 4):
                                P = ps[gi % 2 * 2 + g4 % 2] if False else ps[(2 * gi + g4) % 3]
                                for u in range(4):
                                    tt_ = g4 * 4 + u
                                    fw.op(PE_, lambda tt_=tt_, u=u, P=P, Lh=Lh, Rh=Rh: nc.tensor.matmul(P[:, u * 128:(u + 1) * 128], lhsT=Lh[:, tt_, :],
                                                                                                          rhs=Rh[:, tt_, :], start=True, stop=True),
                                          [Lh, Rh], [P], inc=(u == 3))
                                tb_ = tl * 128 + t4 * NTS + g4 * 4
                                fw.op(A_, lambda tb_=tb_, P=P: nc.scalar.copy(out=Gall[:, tb_:tb_ + 4, :],
                                                                              in_=P[:, :].rearrange("p (a b) -> p a b", a=4)), [P], [Gall])
                            gi += 1

                def sweep(s, gen):
                    xt_ = xts[s % 2]
                    h2 = h2s[s % 2]
                    Aslot = [Buf("aslot%d" % i) for i in range(3)]

                    def a_part(c):
                        pu_ = pus[c % NSL]
                        pv_ = pvs[c % NSL]
                        fw.dma("sp", pu_[:, :], pubf[l, c], reads=[WSC], writes=[pu_])
                        fw.dma("sp", pv_[:, :], pvbf[l, c], reads=[WSC], writes=[pv_])
                        P = ps[c % 3]
                        wr = [Aslot[c % 3]] + ([P] if c < 3 else [])
                        for k in range(KC):
                            fw.op(PE_, lambda k=k, pu_=pu_, P=P: nc.tensor.matmul(
                                P[:, 0:TB], lhsT=pu_[:, k * 128:(k + 1) * 128], rhs=h2[:, k, :], start=(k == 0), stop=(k == KC - 1)),
                                [pu_, h2], wr, inc=(k == KC - 1))

                    def o_part(c):
                        pv_ = pvs[c % NSL]
                        P = ps[c % 3]
                        g_ = ga[c % 3]
                        w_ = wgt[c % 3]
                        fw.op(A_, lambda g_=g_, P=P: nc.scalar.activation(out=g_[:, :], in_=P[:, 0:TB], func=AF.Gelu_apprx_tanh),
                              [Aslot[c % 3]], [g_])
                        fw.op(V, lambda g_=g_, w_=w_, c=c: nc.vector.tensor_tensor(out=w_[:, :], in0=g_[:, :], in1=Gall[:, :, c], op=ALU.mult),
                              [g_, Gall], [w_])
                        for j in range(KC):
                            OPS = ps[4 + j // 2]
                            fw.op(PE_, lambda j=j, OPS=OPS, pv_=pv_, w_=w_, c=c: nc.tensor.matmul(
                                OPS[:, (j % 2) * TB:(j % 2 + 1) * TB], lhsT=pv_[:, j * 128:(j + 1) * 128], rhs=w_[:, :],
                                start=(c == 0 and j % 2 == 0), stop=(c == NEXP_C - 1), skip_group_check=True), [pv_, w_], [OPS], inc=(j == KC - 1))

                    NSTEP = 60
                    done = [0]

                    def advance(upto):
                        while gen is not None and done[0] < upto:
                            try:
                                next(gen)
                            except StopIteration:
                                done[0] = 10 ** 9
                                return
                            done[0] += 1
                    for c in range(2):
                        a_part(c)
                    for c in range(NEXP_C):
                        if c + 2 < NEXP_C:
                            a_part(c + 2)
                        o_part(c)
                        advance((c + 1) * NSTEP // NEXP_C)
                    advance(10 ** 8)
                    for j in range(KC):
                        OPS = ps[4 + j // 2]
                        fw.op(V, lambda j=j, OPS=OPS: nc.vector.scalar_tensor_tensor(
                            out=xt_[:, j, :], in0=OPS[:, (j % 2) * TB:(j % 2 + 1) * TB], scalar=modt[:, l, 40 + j, cond:cond + 1], in1=xt_[:, j, :],
                            op0=ALU.mult, op1=ALU.add), [OPS, modt, xt_], [xt_])
                    fw.dma("sp", xdst[:, :, s * TB:(s + 1) * TB], xt_[:, :, :], reads=[xt_], writes=[SBx])

                g0 = pre(0)
                for _ in g0:
                    pass
                for s in range(nblk):
                    gbuild(s)
                    sweep(s, pre(s + 1) if s + 1 < nblk else None)
                fw.barrier()
        def load_seg(l, seg, w):
            load_w(PO, w, w_in[l].rearrange("(k p) n -> p k n", p=128)[:, :, seg * 256:(seg + 1) * 256])
            return w

        def pfm(w, hb, cc, P, col=0):
            for k in range(KC):
                fw.op(PE_, lambda k=k: nc.tensor.matmul(P[:, col:col + TB], lhsT=w[:, k, cc * 128:(cc + 1) * 128],
                                                        rhs=hb[:, k, :], start=(k == 0), stop=(k == KC - 1)),
                      [w, hb], [P], inc=(k == KC - 1))

        def ptm(w, hb, tl, P, col=0):
            for k in range(KC):
                fw.op(PE_, lambda k=k: nc.tensor.matmul(P[:, col:col + 256], lhsT=hb[:, k, tl * 128:(tl + 1) * 128],
                                                        rhs=w[:, k, :], start=(k == 0), stop=(k == KC - 1)),
                      [w, hb], [P], inc=(k == KC - 1))

        def rope(src_t, src_ap, tA, tB, cs_, sn_, dst_ap, dst_t, P):
            fw.op(PE_, lambda: nc.tensor.matmul(P[:, 0:TB], lhsT=cst[:, CC_PSW:CC_PSW + 128], rhs=src_ap, start=True, stop=True),
                  [cst, src_t], [P])
            fw.op(V, lambda: nc.vector.tensor_tensor(out=tA[:, 0:TB], in0=src_ap, in1=cs_[:, :], op=ALU.mult), [src_t, cs_], [tA])
            fw.op(V, lambda: nc.vector.tensor_tensor(out=tB[:, 0:TB], in0=P[:, 0:TB], in1=sn_[:, :], op=ALU.mult), [P, sn_], [tB])
            fw.op(V, lambda: nc.vector.tensor_tensor(out=dst_ap, in0=tA[:, 0:TB], in1=tB[:, 0:TB], op=ALU.add), [tA, tB], [dst_t])

        def latA(l, s, src, dkg, dvg, SB):
            t0 = s * TB
            with ExitStack() as ph:
                hb = sb("hb", [128, KC, TB], BF16, ctx=ph)
                wseg = [sb("wseg%d" % i, [128, KC, 256], BF16, ctx=ph) for i in range(3)]
                f = [sb("f%d" % i, [128, 512], ctx=ph) for i in range(6)]
                kb = sb("kb", [128, 2, TB], BF16, ctx=ph)
                yb = sb("yb", [128, 2, TB], BF16, ctx=ph)
                vob = sb("vob", [128, 2, 256], BF16, ctx=ph)
                cs_ = sb("cs_", [128, TB], ctx=ph)
                sn_ = sb("sn_", [128, TB], ctx=ph)
                fw.dma("sp", xt[:, :, :], src[:, :, t0:t0 + TB], writes=[xt])
                fw.dma("sp", cs_[:, :], cosT[:, t0:t0 + TB], writes=[cs_])
                fw.dma("sp", sn_[:, :], sinT[:, t0:t0 + TB], writes=[sn_])
                norm_mod(ph, l, 0, hb, 0, 1)
                ws = [0]

                def seg(i):
                    w = wseg[ws[0] % 3]
                    ws[0] += 1
                    return load_seg(l, i, w)
                wa = seg(0)
                wg = seg(1)
                for cc in range(2):
                    pfm(wa, hb, cc, ps[2], 0)
                    pfm(wg, hb, cc, ps[2], TB)
                    fw.op(A_, lambda: nc.scalar.activation(out=f[0][:, 0:TB], in_=ps[2][:, TB:2 * TB], func=AF.Sigmoid), [ps[2]], [f[0]])
                    fw.op(V, lambda cc=cc: nc.vector.tensor_tensor(out=yb[:, cc, :], in0=ps[2][:, 0:TB], in1=f[0][:, 0:TB], op=ALU.mult),
                          [ps[2], f[0]], [yb])
                fw.dma("sp", ybuf[:, :, 15 + t0:15 + t0 + TB], yb[:, :, :], reads=[yb], writes=[SB["y"]])
                wk = seg(3)
                for cc in range(2):
                    P = ps[4]
                    pfm(wk, hb, cc, P, 0)
                    fw.op(A_, lambda: nc.scalar.activation(out=f[1][:, 0:TB], in_=P[:, 0:TB], func=AF.Square), [P], [f[1]])
                    fw.op(PE_, lambda: nc.tensor.matmul(P[:, TB:2 * TB], lhsT=blk64, rhs=f[1][:, 0:TB], start=True, stop=True), [cst, f[1]], [P])
                    rs(f[2][:, 0:TB], P[:, TB:2 * TB], [P], [f[2]])
                    fw.op(V, lambda cc=cc: nc.vector.scalar_tensor_tensor(out=kb[:, cc, :], in0=P[:, 0:TB], scalar=ppt[:, l, PC_NAK:PC_NAK + 1],
                                                                           in1=f[2][:, 0:TB], op0=ALU.mult, op1=ALU.mult), [P, ppt, f[2]], [kb])
                fw.dma("sp", nks[:, :, t0:t0 + TB], kb[:, :, :], reads=[kb], writes=[SB["k"]])
                wv = seg(4)
                for tl in range(2):
                    ptm(wv, hb, tl, ps[5], 0)
                    fw.op(A_, lambda tl=tl: nc.scalar.copy(out=vob[:, tl, :], in_=ps[5][:, 0:256]), [ps[5]], [vob])
                fw.dma("sp", nvs[:, 2 * s:2 * s + 2, :], vob[:, :, :], reads=[vob], writes=[SB["v"]])
                wk2 = seg(6)
                for cc in range(2):
                    P = ps[4]
                    pfm(wk2, hb, cc, P, 0)
                    fw.op(A_, lambda: nc.scalar.activation(out=f[1][:, 0:TB], in_=P[:, 0:TB], func=AF.Square), [P], [f[1]])
                    fw.op(PE_, lambda: nc.tensor.matmul(P[:, TB:2 * TB], lhsT=blk32, rhs=f[1][:, 0:TB], start=True, stop=True), [cst, f[1]], [P])
                    rs(f[2][:, 0:TB], P[:, TB:2 * TB], [P], [f[2]])
                    fw.op(V, lambda: nc.vector.scalar_tensor_tensor(out=f[3][:, 0:TB], in0=P[:, 0:TB], scalar=ppt[:, l, PC_DK:PC_DK + 1],
                                                                    in1=f[2][:, 0:TB], op0=ALU.mult, op1=ALU.mult), [P, ppt, f[2]], [f[3]])
                    rope(f[3], f[3][:, 0:TB], f[4], f[5], cs_, sn_, dkg[:, cc, t0:t0 + TB], dkg, ps[3])
                wv2 = seg(7)
                for tl in range(2):
                    ptm(wv2, hb, tl, ps[5], 0)
                    fw.op(A_, lambda tl=tl: nc.scalar.copy(out=dvg[:, 2 * s + tl, :], in_=ps[5][:, 0:256]), [ps[5]], [dvg])
                fw.barrier()

        def latB(l, s, src, dkg, dvg, SB, CA):
            t0 = s * TB
            nck, ncvt, dck, dcvt, biast, qsc = CA
            ones64 = onesE[:, 0:64]
            with ExitStack() as ph:
                hb = sb("hb", [128, KC, TB], BF16, ctx=ph)
                mixed = sb("mixed", [128, KC, TB], BF16, ctx=ph)
                mixh = sb("mixh", [64, 8, TB], BF16, ctx=ph)
                wseg = [sb("wseg%d" % i, [128, KC, 256], BF16, ctx=ph) for i in range(2)]
                wo = sb("wo", [128, 4, D], BF16, ctx=ph)
                woh = sb("woh", [64, 8, D], BF16, ctx=ph)
                wsb = sb("wsb", [128, 4, 128], BF16, ctx=ph)
                rowt = sb("rowt", [128, 512], ctx=ph)
                bst = sb("bst", [128, 2, 128], ctx=ph)
                diagw = sb("diagw", [128, 2, 31, 128], BF16, ctx=ph)
                ywin = sb("ywin", [128, 2, TB + 30], BF16, ctx=ph)
                f = [sb("f%d" % i, [128, 512], ctx=ph) for i in range(6)]
                qb = sb("qb", [128, 2, 2, TB], BF16, ctx=ph)
                dqv = sb("dqv", [128, 4, 2, TB], BF16, ctx=ph)
                nkw = sb("nkw", [128, 2, 768], BF16, ctx=ph)
                nvw = sb("nvw", [128, 6, 256], BF16, ctx=ph)
                pt = [sb("pt%d" % i, [128, 512], BF16, ctx=ph) for i in range(4)]
                pt2 = sb("pt2", [128, 1024], BF16, ctx=ph)
                gu = sb("gu", [128, 2, TB], ctx=ph)
                vpad = sb("vpad", [128, 2, 4, 128], BF16, ctx=ph)
                bnst = sb("bnst", [128, 2, 8], ctx=ph)
                cs_ = sb("cs_", [128, TB], ctx=ph)
                sn_ = sb("sn_", [128, TB], ctx=ph)
                fw.dma("sp", xt[:, :, :], src[:, :, t0:t0 + TB], writes=[xt])
                fw.dma("sp", cs_[:, :], cosT[:, t0:t0 + TB], writes=[cs_])
                fw.dma("sp", sn_[:, :], sinT[:, t0:t0 + TB], writes=[sn_])
                fw.dma("sp", ywin[:, :, :], ybuf[:, :, t0:t0 + TB + 30], reads=[SB["y"]], writes=[ywin])
                jlo, jhi = max(0, 2 * s - 2), min(SEQL // 128 - 1, 2 * s + 3)
                nt = jhi - jlo + 1
                fw.dma("sp", nkw[:, :, 0:nt * 128], nks[:, :, jlo * 128:(jhi + 1) * 128], reads=[SB["k"]], writes=[nkw])
                fw.dma("sp", nvw[:, 0:nt, :], nvs[:, jlo:jhi + 1, :], reads=[SB["v"]], writes=[nvw])
                norm_mod(ph, l, 0, hb, 0, 1)
                wrows = w_out[l].rearrange("(k p) n -> p k n", p=128)
                fw.dma(PO, wo[:, 0:2, :], wrows[:, 0:2, :], writes=[wo])
                fw.dma(PO, wo[:, 2:4, :], wrows[:, 6:8, :], writes=[wo])
                fw.dma(PO, woh[:, :, :], w_out[l][256:768, :].rearrange("(h p) n -> p h n", p=64), writes=[woh])
                load_w(PO, wsb, wsT[l])
                fw.dma("sp", rowt[:, :], rowp[l].partition_broadcast(128), writes=[rowt])
                fw.dma("sp", bst[:, :, :], bsT[l], writes=[bst])
                fw.op(V, lambda: nc.vector.memset(vpad[:, :, :, :], 0.0), [], [vpad])
                for cc in range(2):
                    for tap in range(31):
                        fw.op(V, lambda cc=cc, tap=tap, e=nc.vector: e.tensor_scalar(
                            out=diagw[:, cc, tap, :], in0=identB, scalar1=ppt[:, l, PC_CONVW + cc * 31 + tap:PC_CONVW + cc * 31 + tap + 1],
                            scalar2=None, op0=ALU.mult), [cstb, ppt], [diagw])
                ws = [0]

                def seg(i):
                    w = wseg[ws[0] % 2]
                    ws[0] += 1
                    return load_seg(l, i, w)

                for cc in range(2):
                    for tap in range(31):
                        fw.op(PE_, lambda cc=cc, tap=tap: nc.tensor.matmul(
                            ps[3][:, cc * TB:(cc + 1) * TB], lhsT=diagw[:, cc, tap, :], rhs=ywin[:, cc, tap:tap + TB],
                            start=(tap == 0), stop=(tap == 30)), [diagw, ywin], [ps[3]], inc=(tap == 30))
                    fw.op(A_, lambda cc=cc: nc.scalar.activation(
                        out=f[1][:, cc * TB:(cc + 1) * TB], in_=ps[3][:, cc * TB:(cc + 1) * TB], func=AF.Identity,
                        bias=ppt[:, l, PC_CONVB + cc:PC_CONVB + cc + 1], scale=1.0), [ps[3], ppt], [f[1]])
                for cc in range(2):
                    fw.op(PE_, lambda cc=cc: nc.tensor.matmul(ps[2][:, 0:TB], lhsT=o256, rhs=f[1][:, cc * TB:(cc + 1) * TB],
                                                              start=(cc == 0), stop=(cc == 1)), [cst, f[1]], [ps[2]], inc=(cc == 1))
                for cc in range(2):
                    fw.op(V, lambda cc=cc: nc.vector.tensor_tensor(out=f[2][:, cc * TB:(cc + 1) * TB], in0=f[1][:, cc * TB:(cc + 1) * TB],
                                                                    in1=ps[2][:, 0:TB], op=ALU.subtract), [f[1], ps[2]], [f[2]])
                fw.op(A_, lambda: nc.scalar.activation(out=f[3][:, :], in_=f[2][:, :], func=AF.Square), [f[2]], [f[3]])
                for cc in range(2):
                    fw.op(PE_, lambda cc=cc: nc.tensor.matmul(ps[3][:, 0:TB], lhsT=o256, rhs=f[3][:, cc * TB:(cc + 1) * TB],
                                                              start=(cc == 0), stop=(cc == 1)), [cst, f[3]], [ps[3]], inc=(cc == 1))
                rs(f[4][:, 0:TB], ps[3][:, 0:TB], [ps[3]], [f[4]])
                for cc in range(2):
                    fw.op(V, lambda cc=cc: nc.vector.tensor_tensor(out=f[3][:, cc * TB:(cc + 1) * TB], in0=f[2][:, cc * TB:(cc + 1) * TB],
                                                                    in1=f[4][:, 0:TB], op=ALU.mult), [f[2], f[4]], [f[3]])
                    fw.op(A_, lambda cc=cc: nc.scalar.activation(
                        out=mixed[:, cc, :], in_=f[3][:, cc * TB:(cc + 1) * TB], func=AF.Silu,
                        scale=ppt[:, l, PC_CLNG + cc:PC_CLNG + cc + 1], bias=ppt[:, l, PC_CLNB + cc:PC_CLNB + cc + 1]),
                        [f[3], ppt], [mixed])
                if lat_stage < 2:
                    fw.barrier()
                    return

                wq_ = seg(2)
                for cc in range(2):
                    P = ps[2]
                    pfm(wq_, hb, cc, P, 0)
                    fw.op(A_, lambda: nc.scalar.activation(out=f[1][:, 0:TB], in_=P[:, 0:TB], func=AF.Square), [P], [f[1]])
                    fw.op(PE_, lambda: nc.tensor.matmul(P[:, TB:2 * TB], lhsT=blk64, rhs=f[1][:, 0:TB], start=True, stop=True), [cst, f[1]], [P])
                    rs(f[2][:, 0:TB], P[:, TB:2 * TB], [P], [f[2]])
                    for par in range(2):
                        fw.op(V, lambda cc=cc, par=par: nc.vector.scalar_tensor_tensor(
                            out=qb[:, par, cc, :], in0=P[:, 0:TB], scalar=qsc[:, l, par:par + 1], in1=f[2][:, 0:TB],
                            op0=ALU.mult, op1=ALU.mult), [P, qsc, f[2]], [qb])
                first = [True] * 4
                items = [(rr, j, d_, v0, v1) for rr in range(4) for (j, d_, v0, v1) in na_tiles(4 * s + rr)]

                def na_s(i):
                    rr, j, d_, v0, v1 = items[i]
                    SP = ps[i % 2]
                    ci = COMBOS.index((d_, v0, v1))
                    for h in range(4):
                        cc, par = h // 2, h % 2
                        fw.op(PE_, lambda h=h, cc=cc, par=par, j=j, SP=SP, rr=rr: nc.tensor.matmul(
                            SP[:, h * 64:(h + 1) * 64], lhsT=nkw[:, cc, (j - jlo) * 128:(j - jlo + 1) * 128],
                            rhs=qb[:, par, cc, rr * 64:(rr + 1) * 64], start=True, stop=False, skip_group_check=True),
                            [nkw, qb], [SP], inc=False)
                        fw.op(PE_, lambda h=h, ci=ci, SP=SP: nc.tensor.matmul(
                            SP[:, h * 64:(h + 1) * 64], lhsT=identB, rhs=biast[:, ci * 4 + h, :], start=False, stop=True,
                            skip_group_check=True), [cstb, biast], [SP], inc=(h == 3))

                def na_od(i):
                    rr, j, d_, v0, v1 = items[i]
                    SP = ps[i % 2]
                    p_ = pt[i % 2]
                    fw.op(A_, lambda SP=SP, p_=p_: nc.scalar.activation(out=p_[:, 0:256], in_=SP[:, 0:256], func=AF.Exp), [SP], [p_])
                    for h in range(4):
                        OD = ps[4 + h]
                        fw.op(PE_, lambda h=h, j=j, OD=OD, p_=p_, rr=rr, st=first[h]: nc.tensor.matmul(
                            OD[0:64, rr * 64:(rr + 1) * 64], lhsT=nvw[:, j - jlo, h * 64:(h + 1) * 64], rhs=p_[:, h * 64:(h + 1) * 64],
                            start=st, stop=False, skip_group_check=True), [nvw, p_], [OD], inc=False)
                        first[h] = False
                        fw.op(PE_, lambda h=h, OD=OD, p_=p_, rr=rr: nc.tensor.matmul(
                            OD[0:64, TB + rr * 64:TB + (rr + 1) * 64], lhsT=ones64, rhs=p_[:, h * 64:(h + 1) * 64],
                            start=False, stop=False, skip_group_check=True), [cstb, p_], [OD], inc=(h == 3))

                na_s(0)
                for i in range(len(items)):
                    if i + 1 < len(items):
                        na_s(i + 1)
                    na_od(i)
                for kt in range(2):
                    for h in range(4):
                        cc, par = h // 2, h % 2
                        SP = ps[h // 2]
                        fw.op(PE_, lambda h=h, cc=cc, par=par, SP=SP, kt=kt: nc.tensor.matmul(
                            SP[:, (h % 2) * 256:(h % 2 + 1) * 256], lhsT=nck[:, l, cc, kt * 128:(kt + 1) * 128], rhs=qb[:, par, cc, :],
                            start=True, stop=True), [nck, qb], [SP], inc=(h % 2 == 1))
                    for hh in range(2):
                        fw.op(A_, lambda hh=hh: nc.scalar.activation(out=pt2[:, hh * 512:(hh + 1) * 512], in_=ps[hh][:, :], func=AF.Exp),
                              [ps[hh]], [pt2])
                    for h in range(4):
                        OD = ps[4 + h]
                        fw.op(PE_, lambda h=h, OD=OD, kt=kt: nc.tensor.matmul(
                            OD[0:64, 0:TB], lhsT=ncvt[:, l, kt, h * 64:(h + 1) * 64], rhs=pt2[:, h * 256:(h + 1) * 256],
                            start=False, stop=(kt == 1), skip_group_check=True), [ncvt, pt2], [OD], inc=False)
                        fw.op(PE_, lambda h=h, OD=OD, kt=kt: nc.tensor.matmul(
                            OD[0:64, TB:2 * TB], lhsT=ones64, rhs=pt2[:, h * 256:(h + 1) * 256],
                            start=False, stop=(kt == 1), skip_group_check=True), [cstb, pt2], [OD], inc=True)
                for h in range(4):
                    OD = ps[4 + h]
                    fw.op(V, lambda OD=OD: nc.vector.reciprocal(out=f[5][0:64, 0:TB], in_=OD[0:64, TB:2 * TB]), [OD], [f[5]])
                    fw.op(V, lambda OD=OD, h=h: nc.vector.tensor_tensor(out=mixh[0:64, h, :], in0=OD[0:64, 0:TB], in1=f[5][0:64, 0:TB],
                                                                         op=ALU.mult), [OD, f[5]], [mixh])
                if lat_stage < 3:
                    fw.barrier()
                    return

                wq_ = seg(5)
                for cc in range(2):
                    P = ps[2]
                    pfm(wq_, hb, cc, P, 0)
                    fw.op(A_, lambda: nc.scalar.activation(out=f[1][:, 0:TB], in_=P[:, 0:TB], func=AF.Square), [P], [f[1]])
                    fw.op(PE_, lambda: nc.tensor.matmul(P[:, TB:2 * TB], lhsT=blk32, rhs=f[1][:, 0:TB], start=True, stop=True), [cst, f[1]], [P])
                    rs(f[2][:, 0:TB], P[:, TB:2 * TB], [P], [f[2]])
                    fw.op(V, lambda: nc.vector.scalar_tensor_tensor(out=f[3][:, 0:TB], in0=P[:, 0:TB], scalar=ppt[:, l, PC_DQ:PC_DQ + 1],
                                                                    in1=f[2][:, 0:TB], op0=ALU.mult, op1=ALU.mult), [P, ppt, f[2]], [f[3]])
                    rope(f[3], f[3][:, 0:TB], f[4], f[5], cs_, sn_, f[0][:, 0:TB], f[0], ps[3])
                    for v in range(4):
                        fw.op(V, lambda v=v, cc=cc: nc.vector.tensor_scalar(out=dqv[:, v, cc, :], in0=f[0][:, 0:TB],
                                                                             scalar1=cst[:, CC_MASK + v:CC_MASK + v + 1], scalar2=None,
                                                                             op0=ALU.mult), [f[0], cst], [dqv])
                sc32 = 32 ** -0.5
                for cc in range(2):
                    firstb = [True] * 4
                    ntile = SEQL // 128 + 2
                    def tile_src(ti):
                        if ti < SEQL // 128:
                            return (dkg[:, cc, ti * 128:(ti + 1) * 128], dkg, dvg, (lambda h, ti=ti: dvg[:, ti, h * 64:(h + 1) * 64]))
                        kt = ti - SEQL // 128
                        return (dck[:, l, cc, kt * 128:(kt + 1) * 128], dck, dcvt, (lambda h, kt=kt: dcvt[:, l, kt, h * 64:(h + 1) * 64]))

                    def s_part(ti):
                        Kap, Kt, Vt, vsl = tile_src(ti)
                        st_ = ti % 2
                        SPs = [ps[2 * st_], ps[2 * st_ + 1]]
                        for sub in range(2):
                            for par in range(2):
                                fw.op(PE_, lambda sub=sub, par=par, Kap=Kap, SPs=SPs: nc.tensor.matmul(
                                    SPs[sub][:, par * 256:(par + 1) * 256], lhsT=Kap, rhs=dqv[:, sub * 2 + par, cc, :],
                                    start=True, stop=True), [Kt, dqv], [SPs[sub]], inc=(par == 1))

                    def od_part(ti):
                        Kap, Kt, Vt, vsl = tile_src(ti)
                        st_ = ti % 2
                        SPs = [ps[2 * st_], ps[2 * st_ + 1]]
                        pts = [pt[2 * st_], pt[2 * st_ + 1]]
                        for sub in range(2):
                            fw.op(A_, lambda sub=sub, SPs=SPs, pts=pts: nc.scalar.activation(out=pts[sub][:, :], in_=SPs[sub][:, :],
                                                                                              func=AF.Exp, scale=sc32), [SPs[sub]], [pts[sub]])
                        for sub in range(2):
                            for par in range(2):
                                bi = sub * 2 + par
                                OD = ps[4 + bi]
                                h = 2 * cc + par
                                fw.op(PE_, lambda OD=OD, h=h, sub=sub, par=par, pts=pts, vsl=vsl, st=firstb[bi]: nc.tensor.matmul(
                                    OD[0:64, 0:TB], lhsT=vsl(h), rhs=pts[sub][:, par * 256:(par + 1) * 256],
                                    start=st, stop=False, skip_group_check=True), [Vt, pts[sub]], [OD], inc=False)
                                firstb[bi] = False
                                fw.op(PE_, lambda OD=OD, sub=sub, par=par, pts=pts: nc.tensor.matmul(
                                    OD[0:64, TB:2 * TB], lhsT=ones64, rhs=pts[sub][:, par * 256:(par + 1) * 256],
                                    start=False, stop=False, skip_group_check=True), [cstb, pts[sub]], [OD], inc=True)

                    s_part(0)
                    for ti in range(ntile):
                        if ti + 1 < ntile:
                            s_part(ti + 1)
                        od_part(ti)
                    for par in range(2):
                        h = 2 * cc + par
                        O1, O2 = ps[4 + par], ps[6 + par]
                        fw.op(V, lambda O1=O1: nc.vector.reciprocal(out=f[5][0:64, 0:TB], in_=O1[0:64, TB:2 * TB]), [O1], [f[5]])
                        fw.op(V, lambda O2=O2: nc.vector.reciprocal(out=f[5][0:64, TB:2 * TB], in_=O2[0:64, TB:2 * TB]), [O2], [f[5]])
                        fw.op(V, lambda O1=O1: nc.vector.tensor_tensor(out=f[0][0:64, 0:TB], in0=O1[0:64, 0:TB], in1=f[5][0:64, 0:TB], op=ALU.mult),
                              [O1, f[5]], [f[0]])
                        fw.op(V, lambda O2=O2: nc.vector.tensor_tensor(out=f[0][0:64, TB:2 * TB], in0=O2[0:64, 0:TB], in1=f[5][0:64, TB:2 * TB],
                                                                        op=ALU.mult), [O2, f[5]], [f[0]])
                        fw.op(V, lambda: nc.vector.scalar_tensor_tensor(out=f[1][0:64, 0:TB], in0=f[0][0:64, TB:2 * TB], scalar=lamt[0:64, l, 0:1],
                                                                        in1=f[0][0:64, 0:TB], op0=ALU.mult, op1=ALU.add), [f[0], lamt], [f[1]])
                        fw.op(A_, lambda: nc.scalar.activation(out=f[1][0:64, TB:2 * TB], in_=f[1][0:64, 0:TB], func=AF.Square), [f[1]], [f[1]])
                        fw.op(PE_, lambda: nc.tensor.matmul(ps[2][0:64, 0:TB], lhsT=cst[0:64, CC_B64:CC_B64 + 64], rhs=f[1][0:64, TB:2 * TB],
                                                            start=True, stop=True), [cst, f[1]], [ps[2]])
                        fw.op(A_, lambda: nc.scalar.activation(out=f[2][0:64, 0:TB], in_=ps[2][0:64, 0:TB], func=AF.Sqrt, bias=epst[0:64, 0:1],
                                                               scale=1.0), [ps[2], epst], [f[2]])
                        fw.op(V, lambda: nc.vector.reciprocal(out=f[2][0:64, 0:TB], in_=f[2][0:64, 0:TB]), [f[2]], [f[2]])
                        fw.op(V, lambda h=h: nc.vector.scalar_tensor_tensor(out=mixh[0:64, 4 + h, :], in0=f[1][0:64, 0:TB], scalar=subg[0:64, l:l + 1],
                                                                             in1=f[2][0:64, 0:TB], op0=ALU.mult, op1=ALU.mult),
                              [f[1], subg, f[2]], [mixh])
                if lat_stage < 4:
                    fw.barrier()
                    return

                wu = seg(8)
                wv = seg(9)
                for cc in range(2):
                    pfm(wu, hb, cc, ps[0], 0)
                    fw.op(A_, lambda cc=cc: nc.scalar.activation(out=gu[:, cc, :], in_=ps[0][:, 0:TB], func=AF.Gelu_apprx_tanh),
                          [ps[0]], [gu])
                for tl in range(2):
                    P = ps[1]
                    ptm(wv, hb, tl, P, 0)
                    fw.op(A_, lambda: nc.scalar.activation(out=f[0][:, 0:256], in_=P[:, 0:256], func=AF.Gelu_apprx_tanh), [P], [f[0]])
                    fw.op(V, lambda: nc.vector.bn_stats(out=bnst[:, 0, 0:6], in_=f[0][:, 0:256]), [f[0]], [bnst])
                    fw.op(V, lambda: nc.vector.bn_aggr(out=bnst[:, 1, 0:2], in_=bnst[:, 0, 0:6]), [bnst], [bnst])
                    rs(bnst[:, 1, 2:3], bnst[:, 1, 1:2], [bnst], [bnst])
                    fw.op(V, lambda: nc.vector.tensor_scalar(out=f[0][:, 256:512], in0=f[0][:, 0:256], scalar1=bnst[:, 1, 0:1],
                                                             scalar2=bnst[:, 1, 2:3], op0=ALU.subtract, op1=ALU.mult), [f[0], bnst], [f[0]])
                    fw.op(V, lambda: nc.vector.tensor_tensor(out=f[0][:, 0:256], in0=f[0][:, 256:512], in1=rowt[:, 0:256], op=ALU.mult),
                          [f[0], rowt], [f[0]])
                    for par in range(2):
                        fw.op(V, lambda tl=tl, par=par: nc.vector.tensor_tensor(
                            out=vpad[:, tl, :, :].rearrange("p (c r) n -> p c r n", r=2)[:, :, par, par * 64:par * 64 + 64],
                            in0=f[0][:, 0:256].rearrange("p (c r n) -> p c r n", c=2, r=2)[:, :, par, :],
                            in1=rowt[:, 256:512].rearrange("p (c r n) -> p c r n", c=2, r=2)[:, :, par, :], op=ALU.add),
                            [f[0], rowt], [vpad])
                for cc in range(2):
                    P = ps[2]
                    for tl in range(2):
                        for par in range(2):
                            fw.op(PE_, lambda tl=tl, par=par, cc=cc: nc.tensor.matmul(
                                P[:, tl * 128:(tl + 1) * 128], lhsT=vpad[:, tl, 2 * cc + par, :], rhs=wsb[:, 2 * cc + par, :],
                                start=(par == 0), stop=(par == 1)), [vpad, wsb], [P], inc=(par == 1))
                    fw.op(V, lambda cc=cc: nc.vector.tensor_tensor(
                        out=f[1][:, 0:TB].rearrange("p (a b) -> p a b", a=2), in0=P[:, 0:TB].rearrange("p (a b) -> p a b", a=2),
                        in1=bst[:, cc, :].unsqueeze(1).to_broadcast([128, 2, 128]), op=ALU.add), [P, bst], [f[1]])
                    fw.op(V, lambda cc=cc: nc.vector.tensor_tensor(out=mixed[:, 6 + cc, :], in0=f[1][:, 0:TB], in1=gu[:, cc, :], op=ALU.mult),
                          [f[1], gu], [mixed])

                for j in range(KC):
                    P = ps[3 + (j % 2)]
                    jj = slice(j * 128, (j + 1) * 128)
                    n_mm = 12
                    mi = 0
                    for q_ in range(2):
                        fw.op(PE_, lambda q_=q_, P=P, jj=jj, mi=mi: nc.tensor.matmul(P[:, 0:TB], lhsT=wo[:, q_, jj], rhs=mixed[:, q_, :],
                                                                                       start=(mi == 0), stop=False), [wo, mixed], [P], inc=False)
                        mi += 1
                    for h in range(8):
                        fw.op(PE_, lambda h=h, P=P, jj=jj: nc.tensor.matmul(P[:, 0:TB], lhsT=woh[0:64, h, jj], rhs=mixh[0:64, h, :],
                                                                             start=False, stop=False), [woh, mixh], [P], inc=False)
                    for q_ in range(2):
                        fw.op(PE_, lambda q_=q_, P=P, jj=jj: nc.tensor.matmul(P[:, 0:TB], lhsT=wo[:, 2 + q_, jj], rhs=mixed[:, 6 + q_, :],
                                                                               start=False, stop=(q_ == 1)), [wo, mixed], [P], inc=(q_ == 1))
                    fw.op(V, lambda j=j, P=P: nc.vector.scalar_tensor_tensor(
                        out=xt[:, j, :], in0=P[:, 0:TB], scalar=modt[:, l, 16 + j, 1:2], in1=xt[:, j, :], op0=ALU.mult, op1=ALU.add),
                        [P, modt, xt], [xt])
                fw.dma("sp", yl[:, :, t0:t0 + TB], xt[:, :, :], reads=[xt], writes=[SB["x"]])
                fw.barrier()

        def latent():
            SB = {k: Buf("scr_" + k) for k in ["y", "k", "v", "x"]}
            with ExitStack() as gl0:
                nck = sb("nck", [128, L, 2, 256], BF16, ctx=gl0)
                ncvt = sb("ncvt", [128, L, 2, 256], BF16, ctx=gl0)
                dck = sb("dck", [128, L, 2, 256], BF16, ctx=gl0)
                dcvt = sb("dcvt", [128, L, 2, 256], BF16, ctx=gl0)
                qsc = sb("qsc", [128, L, 2], ctx=gl0)
                zt = sb("zt", [128, 2, 15], BF16, ctx=gl0)
                for (t_, d_) in [(nck, nckT), (ncvt, ncv), (dck, dckT), (dcvt, dcv)]:
                    fw.dma(PO, t_[:, :, :, :], d_.rearrange("l p a b -> p l a b"), writes=[t_])
                fw.op(V, lambda: nc.vector.tensor_scalar(out=qsc[:, :, :], in0=ppt[:, :, PC_NAQE:PC_NAQE + 2], scalar1=0.125, scalar2=None,
                                                         op0=ALU.mult), [ppt], [qsc])
                fw.op(V, lambda: nc.vector.memset(zt[:, :, :], 0.0), [], [zt])
                fw.dma("sp", ybuf[:, :, 0:15], zt[:, :, :], reads=[zt], writes=[SB["y"]])
                fw.dma("sp", ybuf[:, :, 15 + SEQL:30 + SEQL], zt[:, :, :], reads=[zt], writes=[SB["y"]])
                for l in range(nl):
                    src = xl if l == 0 else yl
                    with ExitStack() as gl:
                        dkg = sb("dkg", [128, 2, SEQL], BF16, ctx=gl)
                        dvg = sb("dvg", [128, SEQL // 128, 256], BF16, ctx=gl)
                        biast = sb("biast", [128, NCOMBO * 4, 64], BF16, ctx=gl)
                        fw.dma(PO, biast[:, :, :], biasT[l], writes=[biast])
                        for s in range(n_lat):
                            latA(l, s, src, dkg, dvg, SB)
                        if lat_stage >= 1:
                            for s in range(n_lat):
                                latB(l, s, src, dkg, dvg, SB, (nck, ncvt, dck, dcvt, biast, qsc))
                        fw.barrier()
                    if lat_stage >= 5:
                        fw.barrier()
                        peer_pass(l, n_lat, 1, (lambda s: yl[:, :, s * TB:(s + 1) * TB]), yl, SB["x"])

        SBc = Buf("scr_xc")
        for l in range(nl if n_seq else 0):
            for s in range(n_seq):
                srcx = xc if l == 0 else yc
                fw.dma("sp", xt[:, :, :], srcx[:, :, s * TB:(s + 1) * TB], reads=[SBc], writes=[xt])
                if stage >= 1:
                    mixers(l, s)
                fw.dma("sp", yc[:, :, s * TB:(s + 1) * TB], xt[:, :, :], reads=[xt], writes=[SBc])
                fw.barrier()
            if stage >= 2:
                peer_pass(l, n_seq, 0, (lambda s: yc[:, :, s * TB:(s + 1) * TB]), yc, SBc)
        if n_lat:
            latent()
        fw.finish()
        print("instructions:", fw.nins, fw.cnt)
        import collections
        agg = collections.defaultdict(int)
        for k, cx in ALLOC.items():
            agg[cx._nm.rsplit("_", 1)[0]] = max(agg[cx._nm.rsplit("_", 1)[0]], cx._bytes)
        print("alloc KB by ctx(first tile):", {k: round(v / 1024, 1) for k, v in agg.items()})
    return nc


def _consts():
    c = np.zeros((128, NCC), np.float32)
    p = np.arange(128)
    c[:, CC_ID:CC_ID + 128] = np.eye(128)
    c[:, CC_B64:CC_B64 + 128] = (p[:, None] // 64 == p[None, :] // 64) / 64.0
    c[:, CC_B32:CC_B32 + 128] = (p[:, None] // 32 == p[None, :] // 32) / 32.0
    c[:, CC_O1024:CC_O1024 + 128] = 1.0 / 1024
    c[:, CC_O256:CC_O256 + 128] = 1.0 / 256
    c[:, CC_OE:CC_OE + 128] = (p[None, :] < 64)
    c[:, CC_OO:CC_OO + 128] = (p[None, :] >= 64)
    c[:, CC_IOTA:CC_IOTA + 128] = p[None, :]
    c[:, CC_IOTA16:CC_IOTA16 + 16] = np.arange(16)[None, :]
    c[:, CC_MASK + 0] = ((p % 64) < 32) & (p < 64)
    c[:, CC_MASK + 1] = ((p % 64) < 32) & (p >= 64)
    c[:, CC_MASK + 2] = ((p % 64) >= 32) & (p < 64)
    c[:, CC_MASK + 3] = ((p % 64) >= 32) & (p >= 64)
    for m in range(128):
        if (m % 16) < 8:
            c[m + 8, CC_PSW + m] = -1.0
        else:
            c[m - 8, CC_PSW + m] = 1.0
    return c


def na_tiles(r):
    r0 = min(max(r - 4, 0), 56)
    out = []
    for j in range(r0 // 2, (r0 + 7) // 2 + 1):
        v0 = r0 <= 2 * j < r0 + 8
        v1 = r0 <= 2 * j + 1 < r0 + 8
        out.append((j, 2 * j - r, bool(v0), bool(v1)))
    return out


COMBOS = sorted(set((d, v0, v1) for r in range(64) for (_, d, v0, v1) in na_tiles(r)))
NCOMBO = len(COMBOS)
SEQL = 4096


def _rope_tables():
    f = np.arange(128)
    w = f % 32
    i = np.where(w < 16, w, w - 16)
    inv = (10000.0 ** (-(np.arange(8, dtype=np.float32)) / 8)).astype(np.float32)
    t = np.arange(SEQL)
    rows = (t // 64).astype(np.float32)
    cols = (t % 64).astype(np.float32)
    pos = np.where((w < 16)[:, None], rows[None, :], cols[None, :]).astype(np.float32)
    ang = (pos * inv[i % 8][:, None]).astype(np.float32)
    return np.cos(ang).astype(np.float32), np.sin(ang).astype(np.float32)


def _bias_tables(rel_bias):
    out = np.full((L, 128, NCOMBO * 4, 64), -1e30, np.float32)
    key = np.arange(128)
    wr = key // 64
    kc = key % 64
    qc = np.arange(64)
    c0 = np.clip(qc - 8, 0, 48)
    inwin = (kc[:, None] >= c0[None, :]) & (kc[:, None] < c0[None, :] + 16)
    dci = np.clip(kc[:, None] - qc[None, :] + 15, 0, 30)
    for ci, (d, v0, v1) in enumerate(COMBOS):
        valid = np.where(wr == 0, v0, v1)[:, None] & inwin
        dri = np.clip(d + wr + 7, 0, 14)
        for h in range(4):
            g = rel_bias[:, h][:, dri[:, None], dci]
            out[:, :, ci * 4 + h, :] = np.where(valid[None], g, -1e30)
    return out


def _fm(v):
    return np.ascontiguousarray(v.reshape(-1, 128).T)


def _pack_weights(inp):
    f = lambda a: np.asarray(a, np.float32)
    pp = np.zeros((L, 128, NPC), np.float32)
    p = np.arange(128)
    for l in range(L):
        pp[l, :, PC_N1G:PC_N1G + 8] = _fm(f(inp["norm1_g"][l]))
        pp[l, :, PC_N2G:PC_N2G + 8] = _fm(f(inp["norm2_g"][l]))
        pp[l, :, PC_BMOD:PC_BMOD + 48] = _fm(f(inp["b_mod"][l]))
        cw = f(inp["conv_w"][l])
        for cc in range(2):
            pp[l, :, PC_CONVW + cc * 31:PC_CONVW + (cc + 1) * 31] = cw[:, cc * 128:(cc + 1) * 128].T
        pp[l, :, PC_CONVB:PC_CONVB + 2] = _fm(f(inp["conv_b"][l]))
        pp[l, :, PC_CLNG:PC_CLNG + 2] = _fm(f(inp["conv_ln_g"][l]))
        pp[l, :, PC_CLNB:PC_CLNB + 2] = _fm(f(inp["conv_ln_b"][l]))
        pp[l, :, PC_NAQ] = f(inp["na_qn_g"][l])[p % 64]
        pp[l, :, PC_NAK] = f(inp["na_kn_g"][l])[p % 64]
        dq = f(inp["diff_qn_g"][l])[p % 32]
        pp[l, :, PC_DQ1] = np.where((p % 64) < 32, dq, 0.0)
        pp[l, :, PC_DQ2] = np.where((p % 64) >= 32, dq, 0.0)
        pp[l, :, PC_DK] = f(inp["diff_kn_g"][l])[p % 32]
        pp[l, :, PC_DQ] = dq
        pp[l, :, PC_NAQE] = np.where(p < 64, pp[l, :, PC_NAQ], 0.0)
        pp[l, :, PC_NAQO] = np.where(p >= 64, pp[l, :, PC_NAQ], 0.0)
        pp[l, :, PC_D1E] = np.where(p < 64, pp[l, :, PC_DQ1], 0.0)
        pp[l, :, PC_D1O] = np.where(p >= 64, pp[l, :, PC_DQ1], 0.0)
        pp[l, :, PC_D2E] = np.where(p < 64, pp[l, :, PC_DQ2], 0.0)
        pp[l, :, PC_D2O] = np.where(p >= 64, pp[l, :, PC_DQ2], 0.0)
        pp[l, :, PC_SUB] = f(inp["diff_subln_g"][l])[p % 64]
    rowp = np.concatenate([f(inp["gmlp_ln_g"]), f(inp["gmlp_ln_b"])], axis=1)
    bs = f(inp["gmlp_bs"])
    bsT = np.zeros((L, 128, 2, 128), np.float32)
    for cc in range(2):
        bsT[:, 0:64, cc, :] = bs[:, 2 * cc, None, :]
        bsT[:, 64:128, cc, :] = bs[:, 2 * cc + 1, None, :]
    wsT = np.ascontiguousarray(f(inp["gmlp_ws"]).transpose(0, 3, 1, 2))
    dlam = f(inp["diff_lambda"]).reshape(L, 128)
    skT = np.ascontiguousarray(f(inp["peer_sub_keys"]).transpose(0, 3, 1, 2))
    pu = f(inp["peer_u"])
    puT = np.ascontiguousarray(pu.reshape(L, 128, 128, KC, 128).transpose(0, 2, 4, 3, 1)).reshape(L, 128, 128, KC * 128)
    cosT, sinT = _rope_tables()
    return dict(cosT=cosT, sinT=sinT, biasT=_bias_tables(f(inp["na_rel_bias"])), consts=_consts(), pp=pp, rowp=rowp, bsT=bsT, wsT=wsT, dlam=dlam, skT=skT, puT=puT,
                w_mod=f(inp["w_mod"]), w_in=f(inp["w_in"]), w_out=f(inp["w_out"]), wq=f(inp["peer_wq"]), pv=f(inp["peer_v"]))


def _pack_latent(inp, b):
    f = lambda a: np.asarray(a, np.float32)
    xs = f(inp["x_sample"][b]).reshape(SEQL, KC, 128)
    m = {"xl": np.ascontiguousarray(xs.transpose(2, 1, 0))}
    for nm, key in [("n", "cache_na_kv"), ("d", "cache_diff_kv")]:
        c = f(inp[key][b])
        k = c[:, 0].transpose(0, 1, 3, 2).reshape(L, 2, 128, 256).transpose(0, 2, 1, 3)
        v = c[:, 1].transpose(0, 2, 1, 3).reshape(L, 2, 128, 256).transpose(0, 2, 1, 3)
        m[nm + "ckT"] = np.ascontiguousarray(k)
        m[nm + "cv"] = np.ascontiguousarray(v)
    return m


_NC_CACHE = {}


def kernel(**inputs):
    f = lambda a: np.asarray(a, np.float32)
    W = _pack_weights(inputs)
    xp = f(inputs["x_prompt"])
    c_ctx = f(inputs["c_ctx"])
    cvec = f(inputs["c"])
    if "nc" not in _NC_CACHE:
        _NC_CACHE["nc"] = build_program(4)
    nc = _NC_CACHE["nc"]
    in_maps = []
    lat = [_pack_latent(inputs, b) for b in range(2)]
    for c in range(NCORES):
        xs = xp[4 * c:4 * c + 4].reshape(1024, KC, 128)
        m = dict(W)
        m.update(lat[c // 4])
        m["xc"] = np.ascontiguousarray(xs.transpose(2, 1, 0))
        cond = np.stack([c_ctx, cvec[c // 4]], axis=1)
        m["condT"] = np.ascontiguousarray(cond.reshape(KC, 128, 2).transpose(1, 0, 2))
        in_maps.append(m)
    res = run_bass_kernel_spmd(nc, in_maps, core_ids=list(range(NCORES)))
    y_prompt = np.zeros((32, 256, 1024), np.float32)
    na_kv = np.zeros((32, L, 2, 4, 256, 64), np.float32)
    diff_kv = np.zeros((32, L, 2, 4, 256, 64), np.float32)
    for c in range(NCORES):
        r = res.results[c]
        y_prompt[4 * c:4 * c + 4] = r["yc"].transpose(2, 1, 0).reshape(4, 256, 1024)
        for (kT, vo, dst) in [("nkT", "nvo", na_kv), ("dkT", "dvo", diff_kv)]:
            k = r[kT]
            k = k.transpose(0, 2, 1, 3).reshape(L, 4, 64, 4, 256)
            dst[4 * c:4 * c + 4, :, 0] = k.transpose(3, 0, 1, 4, 2)
            v = r[vo]
            v = v.transpose(0, 2, 1, 3).reshape(L, 4, 256, 4, 64)
            dst[4 * c:4 * c + 4, :, 1] = v.transpose(1, 0, 3, 2, 4)
    y_sample = np.stack([res.results[4 * b]["yl"].transpose(2, 1, 0).reshape(SEQL, 1024) for b in range(2)], axis=0)
    return (y_prompt, y_sample, na_kv, diff_kv)
```

```python
import math
import numpy as np
from contextlib import ExitStack
import concourse.bass as bass
import concourse.mybir as mybir
from concourse.bass_utils import run_bass_kernel_spmd

F32 = mybir.dt.float32
BF16 = mybir.dt.bfloat16
U32 = mybir.dt.uint32
AF = mybir.ActivationFunctionType
ALU = mybir.AluOpType
AX = mybir.AxisListType

NCORES = 8
D = 1024
KC = 8
TB = 256
L = 2
EPS = 1e-6
NEXP_C = 128
ND = 12


class Buf:
    __slots__ = ("name", "w", "r")

    def __init__(self, name):
        self.name = name
        self.w = None
        self.r = {}


class Tl:
    def __init__(self, t, name):
        self.t = t
        self.b = Buf(name)

    def __getitem__(self, idx):
        return self.t[idx]


class FW:
    def __init__(self, nc, es):
        self.nc = nc
        self.es = es
        self.eng = {"pe": nc.tensor, "dve": nc.vector, "act": nc.scalar, "pool": nc.gpsimd, "sp": nc.sync}
        self.sem = {k: es.enter_context(nc.semaphore("sem_" + k)) for k in ["pe", "dve", "act", "pool"]}
        self.cnt = {k: 0 for k in self.sem}
        self.seen = {k: {} for k in self.eng}
        self.dsem = {q: [es.enter_context(nc.semaphore("d_%s_%d" % (q, i))) for i in range(ND)] for q in ["sp", "pool"]}
        self.dcnt = {q: [0] * ND for q in self.dsem}
        self.dnext = {q: 0 for q in self.dsem}
        self.nins = 0

    def _wait(self, E, tok):
        if tok is None:
            return
        if tok[0] == "c":
            _, Dn, n = tok
            if Dn == E and E == "pe":
                return
            key = ("c", Dn)
            if self.seen[E].get(key, 0) >= n:
                return
            self.eng[E].wait_ge(self.sem[Dn], n)
            self.seen[E][key] = n
        else:
            _, q, i, n = tok
            key = ("d", q, i)
            if self.seen[E].get(key, 0) >= n:
                return
            self.eng[E].wait_ge(self.dsem[q][i], n)
            self.seen[E][key] = n

    def _deps(self, E, reads, writes, disjoint=False):
        for b in reads:
            self._wait(E, b.w)
        for b in writes:
            if not (disjoint and b.w is not None and b.w[0] == "c" and b.w[1] == E):
                self._wait(E, b.w)
            for t in list(b.r.values()):
                if disjoint and t[0] == "c" and t[1] == E:
                    continue
                self._wait(E, t)

    def _upd(self, E, tok, reads, writes):
        for b in reads:
            b.r[E] = tok
        for b in writes:
            b.w = tok
            b.r = {}

    def op(self, E, fn, reads=(), writes=(), inc=True, disjoint=False):
        reads = [x.b if isinstance(x, Tl) else x for x in reads]
        writes = [x.b if isinstance(x, Tl) else x for x in writes]
        self._deps(E, reads, writes, disjoint)
        ins = fn()
        self.nins += 1
        if inc:
            ins.then_inc(self.sem[E], 1)
            self.cnt[E] += 1
            tok = ("c", E, self.cnt[E])
        else:
            tok = ("c", E, self.cnt[E] + 1)
        self._upd(E, tok, reads, writes)

    def dma(self, q, out, in_, reads=(), writes=(), **kw):
        reads = [x.b if isinstance(x, Tl) else x for x in reads]
        writes = [x.b if isinstance(x, Tl) else x for x in writes]
        i = self.dnext[q]
        self.dnext[q] = (i + 1) % ND
        if self.dcnt[q][i] > 0:
            self._wait(q, ("d", q, i, self.dcnt[q][i]))
        self._deps(q, reads, writes)
        ins = self.eng[q].dma_start(out=out, in_=in_, **kw)
        self.nins += 1
        ins.then_inc(self.dsem[q][i], 16)
        self.dcnt[q][i] += 16
        tok = ("d", q, i, self.dcnt[q][i])
        self._upd(q, tok, reads, writes)

    def barrier(self):
        for E in ["pe", "dve", "act", "pool", "sp"]:
            for Dn in self.sem:
                if Dn != E and self.cnt[Dn] > 0:
                    self._wait(E, ("c", Dn, self.cnt[Dn]))
            for q in self.dsem:
                for i in range(ND):
                    if self.dcnt[q][i] > 0:
                        self._wait(E, ("d", q, i, self.dcnt[q][i]))

    def finish(self):
        for q in self.dsem:
            for i in range(ND):
                if self.dcnt[q][i] > 0:
                    self._wait("sp", ("d", q, i, self.dcnt[q][i]))
        for Dn in self.sem:
            if self.cnt[Dn] > 0:
                self._wait("sp", ("c", Dn, self.cnt[Dn]))


PC_N1G, PC_N2G, PC_BMOD, PC_CONVW, PC_CONVB, PC_CLNG, PC_CLNB = 0, 8, 16, 64, 126, 128, 130
PC_NAQ, PC_NAK, PC_DQ1, PC_DQ2, PC_DK, PC_SUB = 132, 133, 134, 135, 136, 137
PC_NAQE, PC_NAQO, PC_D1E, PC_D1O, PC_D2E, PC_D2O = 138, 139, 140, 141, 142, 143
PC_DQ = 144
NPC = 145
CC_ID, CC_B64, CC_B32, CC_O1024, CC_O256, CC_OE, CC_OO, CC_IOTA = 0, 128, 256, 384, 512, 640, 768, 896
CC_IOTA16 = 1024
CC_MASK = 1040
CC_PSW = 1044
NCC = 1044 + 128


def build_program(n_seq=4, debug=False, stage=9, nl=L, n_lat=16, lat_stage=9):
    nc = bass.Bass("TRN2", target_bir_lowering=False)
    NT = max(n_seq, 1) * TB
    dr = {}

    def din(name, shape, dt=F32):
        dr[name] = nc.dram_tensor(name, list(shape), dt, kind="ExternalInput").ap()
        return dr[name]

    def dout(name, shape, dt=F32):
        dr[name] = nc.dram_tensor(name, list(shape), dt, kind="ExternalOutput").ap()
        return dr[name]

    xc = din("xc", [128, KC, NT])
    condT = din("condT", [128, KC, 2])
    consts = din("consts", [128, NCC])
    pp = din("pp", [L, 128, NPC])
    rowp = din("rowp", [L, 512])
    bsT = din("bsT", [L, 128, 2, 128])
    wsT = din("wsT", [L, 128, 4, 128])
    dlam = din("dlam", [L, 128])
    w_mod = din("w_mod", [L, D, 6 * D])
    w_in = din("w_in", [L, D, 2560])
    w_out = din("w_out", [L, D, D])
    wq = din("wq", [L, D, 2048])
    skT = din("skT", [L, 128, 2, 128])
    puT = din("puT", [L, NEXP_C, 128, KC * 128])
    pv = din("pv", [L, 128 * 128, D])
    yc = dout("yc", [128, KC, NT])
    nkT = dout("nkT", [L, 128, 2, NT])
    dkT = dout("dkT", [L, 128, 2, NT])
    nvo = dout("nvo", [L, 128, NT // 128, 256])
    dvo = dout("dvo", [L, 128, NT // 128, 256])
    if debug:
        dbg = dout("dbg", [128, 8, TB])
    pubf = nc.dram_tensor("pubf", [L, NEXP_C, 128, KC * 128], BF16, kind="Internal").ap()
    pvbf = nc.dram_tensor("pvbf", [L, NEXP_C, 128, D], BF16, kind="Internal").ap()
    if n_lat:
        xl = din("xl", [128, KC, SEQL])
        nckT = din("nckT", [L, 128, 2, 256])
        ncv = din("ncv", [L, 128, 2, 256])
        dckT = din("dckT", [L, 128, 2, 256])
        dcv = din("dcv", [L, 128, 2, 256])
        biasT = din("biasT", [L, 128, NCOMBO * 4, 64])
        cosT = din("cosT", [128, SEQL])
        sinT = din("sinT", [128, SEQL])
        yl = dout("yl", [128, KC, SEQL])
        ybuf = nc.dram_tensor("ybuf", [128, 2, SEQL + 30], BF16, kind="Internal").ap()
        nks = nc.dram_tensor("nks", [128, 2, SEQL], BF16, kind="Internal").ap()
        nvs = nc.dram_tensor("nvs", [128, SEQL // 128, 256], BF16, kind="Internal").ap()

    with ExitStack() as es:
        fw = FW(nc, es)

        uid = [0]
        ALLOC = {}
        ALLOCN = {}

        def sb(name, shape, dt=F32, ctx=None):
            uid[0] += 1
            name = "%s_%d" % (name, uid[0])
            nb = int(np.prod(shape[1:])) * (2 if dt == BF16 else 4)
            cx = ctx or es
            if not hasattr(cx, "_bytes"):
                cx._bytes = 0
                cx._nm = name
                ALLOC[len(ALLOC)] = cx
            cx._bytes += nb
            return Tl((ctx or es).enter_context(nc.sbuf_tensor(name, list(shape), dt)), name)

        V, A_, PE_, PO = "dve", "act", "pe", "pool"

        def rs(out_ap, in_ap, rt, wt):
            fw.op(A_, lambda: nc.scalar.activation(out=out_ap, in_=in_ap, func=AF.Sqrt, bias=epst[:, 0:1], scale=1.0), list(rt) + [epst], wt)
            fw.op(V, lambda: nc.vector.reciprocal(out=out_ap, in_=out_ap), wt, wt)

        cst = sb("cst", [128, NCC])
        identF = cst[:, CC_ID:CC_ID + 128]
        blk64 = cst[:, CC_B64:CC_B64 + 128]
        blk32 = cst[:, CC_B32:CC_B32 + 128]
        o1024 = cst[:, CC_O1024:CC_O1024 + 128]
        o256 = cst[:, CC_O256:CC_O256 + 128]
        iotaN = cst[:, CC_IOTA:CC_IOTA + 128]
        iota16 = cst[:, CC_IOTA16:CC_IOTA16 + 16]
        cstb = sb("cstb", [128, 4, 128], BF16)
        ppt = sb("ppt", [128, L, NPC])
        modt = sb("modt", [128, L, 48, 2])
        sc1 = sb("sc1", [128, L, 2, 2, KC])
        lamt = sb("lamt", [128, L, 4])
        subg = sb("subg", [128, L])
        xt = sb("xt", [128, KC, TB])
        ps = [Tl(es.enter_context(nc.psum_tensor("ps%d" % i, [128, 512], F32)), "ps%d" % i) for i in range(8)]

        epst = sb("epst", [128, 1])
        fw.op(V, lambda: nc.vector.memset(epst[:, :], EPS), [], [epst])
        fw.dma("sp", cst[:, :], consts[:, :], writes=[cst])
        fw.dma("sp", ppt[:, :, :], pp.rearrange("l p n -> p l n"), writes=[ppt])
        fw.op(V, lambda: nc.vector.tensor_copy(out=cstb[:, 0, :], in_=identF), [cst], [cstb])
        fw.op(V, lambda: nc.vector.tensor_copy(out=cstb[:, 1, :], in_=cst[:, CC_OE:CC_OE + 128]), [cst], [cstb])
        fw.op(V, lambda: nc.vector.tensor_copy(out=cstb[:, 2, :], in_=cst[:, CC_OO:CC_OO + 128]), [cst], [cstb])
        fw.op(V, lambda: nc.vector.tensor_copy(out=cstb[:, 3, :], in_=iotaN), [cst], [cstb])
        identB, onesE, onesO, iotaB = cstb[:, 0, :], cstb[:, 1, :], cstb[:, 2, :], cstb[:, 3, :]

        import os
        sub = int(os.environ.get("SUB", "99"))
        with ExitStack() as ph:
            cdt = sb("cdt", [128, KC, 2], ctx=ph)
            sct = sb("sct", [128, KC, 2], ctx=ph)
            wm = [sb("wm%d" % i, [128, KC, 512], ctx=ph) for i in range(2)]
            dl = sb("dl", [128, L, 128], ctx=ph)
            dl2 = sb("dl2", [128, L, 2, 32], ctx=ph)
            dl3 = sb("dl3", [128, L, 2], ctx=ph)
            fw.dma("sp", cdt[:, :, :], condT[:, :, :], writes=[cdt])
            fw.op(A_, lambda: nc.scalar.activation(out=sct[:, :, :], in_=cdt[:, :, :], func=AF.Silu), [cdt], [sct])
            fw.dma("sp", dl[:, :, :], dlam.partition_broadcast(128), writes=[dl])
            for l in range(L if sub >= 1 else 0):
                for sl in range(12 if sub >= 2 else 0):
                    w = wm[sl % 2]
                    fw.dma("sp", w[:, :, :], w_mod[l].rearrange("(k p) n -> p k n", p=128)[:, :, sl * 512:(sl + 1) * 512],
                           writes=[w])
                    for oc4 in range(4):
                        oc = sl * 4 + oc4
                        for k in range(KC):
                            fw.op(PE_, lambda k=k, oc=oc, oc4=oc4, w=w: nc.tensor.matmul(
                                ps[0][:, oc * 2:oc * 2 + 2], lhsT=w[:, k, oc4 * 128:(oc4 + 1) * 128], rhs=sct[:, k, :],
                                start=(k == 0), stop=(k == KC - 1)), [w, sct], [ps[0]], inc=(k == KC - 1))
                if sub < 3:
                    continue
                fw.op(V, lambda l=l: nc.vector.tensor_tensor(
                    out=modt[:, l, :, :], in0=ps[0][:, 0:96].rearrange("p (a b) -> p a b", b=2),
                    in1=ppt[:, l, PC_BMOD:PC_BMOD + 48].unsqueeze(2).to_broadcast([128, 48, 2]), op=ALU.add),
                    [ps[0], ppt], [modt])
                for j, (vec, pc) in enumerate([(1, PC_N1G), (4, PC_N2G)]):
                    for cd in range(2):
                        fw.op(V, lambda l=l, j=j, vec=vec, pc=pc, cd=cd: nc.vector.scalar_tensor_tensor(
                            out=sc1[:, l, j, cd, :], in0=modt[:, l, vec * 8:(vec + 1) * 8, cd], scalar=1.0,
                            in1=ppt[:, l, pc:pc + 8], op0=ALU.add, op1=ALU.mult), [modt, ppt], [sc1])
                if sub < 4:
                    continue
                fw.op(V, lambda l=l: nc.vector.tensor_tensor(
                    out=dl2[:, l, :, :], in0=dl[:, l, :].rearrange("p (a b c) -> p a b c", a=2, b=2)[:, :, 0, :],
                    in1=dl[:, l, :].rearrange("p (a b c) -> p a b c", a=2, b=2)[:, :, 1, :], op=ALU.mult), [dl], [dl2])
                fw.op(V, lambda l=l: nc.vector.tensor_reduce(out=dl3[:, l, :], in_=dl2[:, l, :, :], axis=AX.X, op=ALU.add),
                      [dl2], [dl3])
                fw.op(A_, lambda l=l: nc.scalar.activation(out=dl3[:, l, :], in_=dl3[:, l, :], func=AF.Exp), [dl3], [dl3])
                lam_init = 0.8 - 0.6 * math.exp(-0.3 * l)
                fw.op(V, lambda l=l, li=lam_init: nc.vector.scalar_tensor_tensor(
                    out=lamt[:, l, 0:1], in0=dl3[:, l, 1:2], scalar=-li, in1=dl3[:, l, 0:1],
                    op0=ALU.add, op1=ALU.subtract), [dl3], [lamt])
                fw.op(V, lambda l=l, li=lam_init: nc.vector.tensor_scalar(
                    out=subg[:, l:l + 1], in0=ppt[:, l, PC_SUB:PC_SUB + 1], scalar1=(1.0 - li), scalar2=None,
                    op0=ALU.mult), [ppt], [subg])
            fw.barrier()

        WSC = Buf("wscratch")
        with ExitStack() as ph:
            cvt = [sb("cvt%d" % i, [128, 8, D], BF16, ctx=ph) for i in range(3)]
            ci_ = 0
            for l in range(nl):
                pvv = pv[l].rearrange("(n c) d -> c n d", c=128)
                for c0 in range(0, NEXP_C, 8):
                    for (src_, dst_) in [(puT[l, c0:c0 + 8], pubf[l, c0:c0 + 8]), (pvv[c0:c0 + 8], pvbf[l, c0:c0 + 8])]:
                        t_ = cvt[ci_ % 3]
                        ci_ += 1
                        fw.dma(PO, t_[:, :, :], src_.rearrange("c p n -> p c n"), writes=[t_])
                        fw.dma("sp", dst_.rearrange("c p n -> p c n"), t_[:, :, :], reads=[t_], writes=[WSC])
            fw.barrier()

        def norm_tmps(ctx, nb=0):
            return ([sb("nsq%d_%d" % (nb, i), [128, TB], ctx=ctx) for i in range(2)], sb("nrstd%d" % nb, [128, TB], ctx=ctx),
                    [sb("ntmp%d_%d" % (nb, i), [128, TB], ctx=ctx) for i in range(2)])

        def norm_mod(ctx, l, which, hb, nb, cond=0, xt=xt, P=None, tmps=None):
            sq, rstd, tmp = tmps if tmps is not None else norm_tmps(ctx, nb)
            P = P if P is not None else ps[1]
            for k in range(KC):
                s = sq[k % 2]
                fw.op(A_, lambda k=k, s=s: nc.scalar.activation(out=s[:, :], in_=xt[:, k, :], func=AF.Square), [xt], [s])
                fw.op(PE_, lambda k=k, s=s: nc.tensor.matmul(P[:, 0:TB], lhsT=o1024, rhs=s[:, :], start=(k == 0),
                                                             stop=(k == KC - 1)), [cst, s], [P], inc=True)
            rs(rstd[:, :], P[:, 0:TB], [P], [rstd])
            shv = 0 if which == 0 else 3
            for k in range(KC):
                t = tmp[k % 2]
                fw.op(V, lambda k=k, t=t: nc.vector.tensor_tensor(out=t[:, :], in0=xt[:, k, :], in1=rstd[:, :], op=ALU.mult),
                      [xt, rstd], [t])
                fw.op(A_, lambda k=k, t=t: nc.scalar.activation(
                    out=hb[:, k, :], in_=t[:, :], func=AF.Identity, scale=sc1[:, l, which, cond, k:k + 1],
                    bias=modt[:, l, shv * 8 + k, cond:cond + 1]), [t, sc1, modt], [hb])

        def load_w(q, tile_, src, **kw):
            fw.dma(q, tile_[:], src, writes=[tile_], **kw)

        def mixers(l, s):
            t0 = s * TB
            with ExitStack() as ph:
                hb = sb("hb", [128, KC, TB], BF16, ctx=ph)
                mixed = sb("mixed", [128, KC, TB], BF16, ctx=ph)
                wseg = [sb("wseg%d" % i, [128, KC, 256], BF16, ctx=ph) for i in range(3)]
                wo = sb("wo", [128, KC, D], BF16, ctx=ph)
                wsb = sb("wsb", [128, 4, 128], BF16, ctx=ph)
                rowt = sb("rowt", [128, 512], ctx=ph)
                bst = sb("bst", [128, 2, 128], ctx=ph)
                diagw = sb("diagw", [128, 2, 31, 128], BF16, ctx=ph)
                ypad = sb("ypad", [128, 2, TB + 30], BF16, ctx=ph)
                f = [sb("f%d" % i, [128, 512], ctx=ph) for i in range(8)]
                qb = sb("qb", [128, 2, 2, TB], BF16, ctx=ph)
                q2b = sb("q2b", [128, 2, 2, TB], BF16, ctx=ph)
                kb = sb("kb", [128, 2, TB], BF16, ctx=ph)
                kout = sb("kout", [128, 2, TB], ctx=ph)
                vout = sb("vout", [128, 2, 256], ctx=ph)
                vpad = sb("vpad", [128, 2, 4, 128], BF16, ctx=ph)
                pt = [sb("pt%d" % i, [128, 512], BF16, ctx=ph) for i in range(2)]
                gu = sb("gu", [128, 2, TB], ctx=ph)
                bnst = sb("bnst", [128, 2, 8], ctx=ph)

                norm_mod(ph, l, 0, hb, 0)
                load_w(PO, wo, w_out[l].rearrange("(k p) n -> p k n", p=128))
                load_w(PO, wsb, wsT[l])
                fw.dma("sp", rowt[:, :], rowp[l].partition_broadcast(128), writes=[rowt])
                fw.dma("sp", bst[:, :, :], bsT[l], writes=[bst])
                fw.op(V, lambda: nc.vector.memset(ypad[:, :, :], 0.0), [], [ypad])
                fw.op(V, lambda: nc.vector.memset(vpad[:, :, :, :], 0.0), [], [vpad])
                for cc in range(2):
                    for tap in range(31):
                        fw.op(V, lambda cc=cc, tap=tap, e=nc.vector: e.tensor_scalar(
                            out=diagw[:, cc, tap, :], in0=identB, scalar1=ppt[:, l, PC_CONVW + cc * 31 + tap:PC_CONVW + cc * 31 + tap + 1],
                            scalar2=None, op0=ALU.mult), [cstb, ppt], [diagw], disjoint=(cc + tap > 0))

                msub = float(os.environ.get("MSUB", "99"))
                if msub < 2:
                    fw.barrier()
                    return
                wslot = [0]

                def seg_w(seg):
                    w = wseg[wslot[0] % 3]
                    wslot[0] += 1
                    load_w(PO, w, w_in[l].rearrange("(k p) n -> p k n", p=128)[:, :, seg * 256:(seg + 1) * 256])
                    return w

                def fm(w, cc, P, col=0):
                    for k in range(KC):
                        fw.op(PE_, lambda k=k: nc.tensor.matmul(P[:, col:col + TB], lhsT=w[:, k, cc * 128:(cc + 1) * 128],
                                                                rhs=hb[:, k, :], start=(k == 0), stop=(k == KC - 1)),
                              [w, hb], [P], inc=(k == KC - 1))

                def tm(w, tl, P, col=0):
                    for k in range(KC):
                        fw.op(PE_, lambda k=k: nc.tensor.matmul(P[:, col:col + 256], lhsT=hb[:, k, tl * 128:(tl + 1) * 128],
                                                                rhs=w[:, k, :], start=(k == 0), stop=(k == KC - 1)),
                              [w, hb], [P], inc=(k == KC - 1))

                wa = seg_w(0)
                wg = seg_w(1)
                for cc in range(2):
                    fm(wa, cc, ps[2], 0)
                    fm(wg, cc, ps[2], TB)
                    fw.op(A_, lambda: nc.scalar.activation(out=f[0][:, 0:TB], in_=ps[2][:, TB:2 * TB], func=AF.Sigmoid),
                          [ps[2]], [f[0]])
                    fw.op(V, lambda cc=cc: nc.vector.tensor_tensor(out=ypad[:, cc, 15:15 + TB], in0=ps[2][:, 0:TB],
                                                                    in1=f[0][:, 0:TB], op=ALU.mult), [ps[2], f[0]], [ypad])
                for cc in range(2):
                    for tap in range(31):
                        fw.op(PE_, lambda cc=cc, tap=tap: nc.tensor.matmul(
                            ps[3][:, cc * TB:(cc + 1) * TB], lhsT=diagw[:, cc, tap, :], rhs=ypad[:, cc, tap:tap + TB],
                            start=(tap == 0), stop=(tap == 30)), [diagw, ypad], [ps[3]], inc=(tap == 30))
                    fw.op(A_, lambda cc=cc: nc.scalar.activation(
                        out=f[1][:, cc * TB:(cc + 1) * TB], in_=ps[3][:, cc * TB:(cc + 1) * TB], func=AF.Identity,
                        bias=ppt[:, l, PC_CONVB + cc:PC_CONVB + cc + 1], scale=1.0), [ps[3], ppt], [f[1]])
                for cc in range(2):
                    fw.op(PE_, lambda cc=cc: nc.tensor.matmul(ps[2][:, 0:TB], lhsT=o256, rhs=f[1][:, cc * TB:(cc + 1) * TB],
                                                              start=(cc == 0), stop=(cc == 1)), [cst, f[1]], [ps[2]], inc=(cc == 1))
                for cc in range(2):
                    fw.op(V, lambda cc=cc: nc.vector.tensor_tensor(out=f[2][:, cc * TB:(cc + 1) * TB], in0=f[1][:, cc * TB:(cc + 1) * TB],
                                                                    in1=ps[2][:, 0:TB], op=ALU.subtract), [f[1], ps[2]], [f[2]])
                fw.op(A_, lambda: nc.scalar.activation(out=f[3][:, :], in_=f[2][:, :], func=AF.Square), [f[2]], [f[3]])
                for cc in range(2):
                    fw.op(PE_, lambda cc=cc: nc.tensor.matmul(ps[3][:, 0:TB], lhsT=o256, rhs=f[3][:, cc * TB:(cc + 1) * TB],
                                                              start=(cc == 0), stop=(cc == 1)), [cst, f[3]], [ps[3]], inc=(cc == 1))
                rs(f[4][:, 0:TB], ps[3][:, 0:TB], [ps[3]], [f[4]])
                for cc in range(2):
                    fw.op(V, lambda cc=cc: nc.vector.tensor_tensor(out=f[3][:, cc * TB:(cc + 1) * TB], in0=f[2][:, cc * TB:(cc + 1) * TB],
                                                                    in1=f[4][:, 0:TB], op=ALU.mult), [f[2], f[4]], [f[3]])
                    fw.op(A_, lambda cc=cc: nc.scalar.activation(
                        out=mixed[:, cc, :], in_=f[3][:, cc * TB:(cc + 1) * TB], func=AF.Silu,
                        scale=ppt[:, l, PC_CLNG + cc:PC_CLNG + cc + 1], bias=ppt[:, l, PC_CLNB + cc:PC_CLNB + cc + 1]),
                        [f[3], ppt], [mixed])

                if msub < 3:
                    fw.barrier()
                    return
                def qk_norm(w, cc, blk, gcol, dst_list, fout=None):
                    P = ps[4]
                    fm(w, cc, P, 0)
                    fw.op(A_, lambda: nc.scalar.activation(out=f[5][:, 0:TB], in_=P[:, 0:TB], func=AF.Square), [P], [f[5]])
                    fw.op(PE_, lambda: nc.tensor.matmul(P[:, TB:2 * TB], lhsT=blk, rhs=f[5][:, 0:TB], start=True, stop=True),
                          [cst, f[5]], [P])
                    rs(f[6][:, 0:TB], P[:, TB:2 * TB], [P], [f[6]])
                    for (gc, dst, tl_) in dst_list:
                        fw.op(V, lambda gc=gc, dst=dst: nc.vector.scalar_tensor_tensor(
                            out=dst, in0=P[:, 0:TB], scalar=ppt[:, l, gc:gc + 1], in1=f[6][:, 0:TB],
                            op0=ALU.mult, op1=ALU.mult), [P, ppt, f[6]], [tl_])

                def v_proj(w, outd):
                    for tl in range(2):
                        P = ps[5]
                        tm(w, tl, P, 0)
                        fw.op(A_, lambda tl=tl: nc.scalar.copy(out=vout[:, tl, :], in_=P[:, 0:256]), [P], [vout])
                        for par in range(2 if os.environ.get("VP", "1") == "1" else 0):
                            fw.op(A_, lambda tl=tl, par=par: nc.scalar.activation(
                                out=vpad[:, tl, :, :].rearrange("p (c r) n -> p c r n", r=2)[:, :, par, par * 64:par * 64 + 64],
                                in_=P[:, 0:256].rearrange("p (c r n) -> p c r n", c=2, r=2)[:, :, par, :], func=AF.Identity), [P], [vpad])
                    fw.dma("sp", outd[l, :, 2 * s:2 * s + 2, :], vout[:, :, :], reads=[vout])

                def attend(cc, qlist, scale):
                    outs = []
                    att = int(os.environ.get("ATT", "9"))
                    for qi, qt_ in enumerate(qlist):
                        OP = ps[6 + qi]
                        for kt in range(2):
                            SP = ps[(kt + 2 * qi) % 4]
                            p_ = pt[(kt + qi) % 2]
                            for par in range(2):
                                fw.op(PE_, lambda par=par, kt=kt, qt_=qt_, SP=SP: nc.tensor.matmul(
                                    SP[:, par * TB:(par + 1) * TB], lhsT=kb[:, cc, kt * 128:(kt + 1) * 128],
                                    rhs=qt_[:, par, cc, :], start=True, stop=True), [kb, qt_], [SP], inc=(par == 1))
                            fw.op(A_, lambda SP=SP, p_=p_: nc.scalar.activation(out=p_[:, :], in_=SP[:, :], func=AF.Exp, scale=scale),
                                  [SP], [p_])
                            if att < 2:
                                continue
                            for par in range(2):
                                fw.op(PE_, lambda par=par, kt=kt, p_=p_, OP=OP: nc.tensor.matmul(
                                    OP[:, 0:TB], lhsT=vpad[:, kt, 2 * cc + par, :], rhs=p_[:, par * TB:(par + 1) * TB],
                                    start=(kt == 0 and par == 0), stop=(kt == 1 and par == 1), skip_group_check=True),
                                    [vpad, p_], [OP], inc=False)
                            for par in range(2):
                                fw.op(PE_, lambda par=par, kt=kt, p_=p_, OP=OP: nc.tensor.matmul(
                                    OP[:, TB:2 * TB], lhsT=(onesE if par == 0 else onesO), rhs=p_[:, par * TB:(par + 1) * TB],
                                    start=False, stop=(kt == 1 and par == 1), skip_group_check=True),
                                    [cstb, p_], [OP], inc=(par == 1))
                        outs.append(OP)
                    return outs

                wqn = seg_w(2)
                wkn = seg_w(3)
                wvn = seg_w(4)
                for cc in range(2):
                    qk_norm(wqn, cc, blk64, None, [(PC_NAQE, qb[:, 0, cc, :], qb), (PC_NAQO, qb[:, 1, cc, :], qb)])
                    qk_norm(wkn, cc, blk64, None, [(PC_NAK, kout[:, cc, :], kout)])
                    fw.op(A_, lambda cc=cc: nc.scalar.copy(out=kb[:, cc, :], in_=kout[:, cc, :]), [kout], [kb])
                fw.dma("sp", nkT[l, :, :, t0:t0 + TB], kout[:, :, :], reads=[kout])
                if msub < 3.3:
                    fw.barrier()
                    return
                v_proj(wvn, nvo)
                if msub < 3.6:
                    fw.barrier()
                    return
                for cc in range(2):
                    (OP,) = attend(cc, [qb], 0.125)
                    if int(os.environ.get("ATT", "9")) < 3:
                        continue
                    fw.op(V, lambda OP=OP: nc.vector.reciprocal(out=f[7][:, 0:TB], in_=OP[:, TB:2 * TB]), [OP], [f[7]])
                    fw.op(V, lambda cc=cc, OP=OP: nc.vector.tensor_tensor(out=mixed[:, 2 + cc, :], in0=OP[:, 0:TB], in1=f[7][:, 0:TB],
                                                                           op=ALU.mult), [OP, f[7]], [mixed])

                if msub < 4:
                    fw.barrier()
                    return
                wqd = seg_w(5)
                wkd = seg_w(6)
                wvd = seg_w(7)
                for cc in range(2):
                    qk_norm(wqd, cc, blk32, None, [(PC_D1E, qb[:, 0, cc, :], qb), (PC_D1O, qb[:, 1, cc, :], qb),
                                                   (PC_D2E, q2b[:, 0, cc, :], q2b), (PC_D2O, q2b[:, 1, cc, :], q2b)])
                    qk_norm(wkd, cc, blk32, None, [(PC_DK, kout[:, cc, :], kout)])
                    fw.op(A_, lambda cc=cc: nc.scalar.copy(out=kb[:, cc, :], in_=kout[:, cc, :]), [kout], [kb])
                fw.dma("sp", dkT[l, :, :, t0:t0 + TB], kout[:, :, :], reads=[kout])
                v_proj(wvd, dvo)
                for cc in range(2):
                    O1, O2 = attend(cc, [qb, q2b], 32 ** -0.5)
                    fw.op(V, lambda O1=O1: nc.vector.reciprocal(out=f[7][:, 0:TB], in_=O1[:, TB:2 * TB]), [O1], [f[7]])
                    fw.op(V, lambda O2=O2: nc.vector.reciprocal(out=f[7][:, TB:2 * TB], in_=O2[:, TB:2 * TB]), [O2], [f[7]])
                    fw.op(V, lambda O1=O1: nc.vector.tensor_tensor(out=f[5][:, 0:TB], in0=O1[:, 0:TB], in1=f[7][:, 0:TB], op=ALU.mult),
                          [O1, f[7]], [f[5]])
                    fw.op(V, lambda O2=O2: nc.vector.tensor_tensor(out=f[5][:, TB:2 * TB], in0=O2[:, 0:TB], in1=f[7][:, TB:2 * TB],
                                                                    op=ALU.mult), [O2, f[7]], [f[5]])
                    fw.op(V, lambda: nc.vector.scalar_tensor_tensor(out=f[6][:, 0:TB], in0=f[5][:, TB:2 * TB], scalar=lamt[:, l, 0:1],
                                                                    in1=f[5][:, 0:TB], op0=ALU.mult, op1=ALU.add), [f[5], lamt], [f[6]])
                    fw.op(A_, lambda: nc.scalar.activation(out=f[6][:, TB:2 * TB], in_=f[6][:, 0:TB], func=AF.Square), [f[6]], [f[6]])
                    fw.op(PE_, lambda: nc.tensor.matmul(ps[4][:, 0:TB], lhsT=blk64, rhs=f[6][:, TB:2 * TB], start=True, stop=True),
                          [cst, f[6]], [ps[4]])
                    rs(f[7][:, 0:TB], ps[4][:, 0:TB], [ps[4]], [f[7]])
                    fw.op(V, lambda cc=cc: nc.vector.scalar_tensor_tensor(out=mixed[:, 4 + cc, :], in0=f[6][:, 0:TB], scalar=subg[:, l:l + 1],
                                                                           in1=f[7][:, 0:TB], op0=ALU.mult, op1=ALU.mult),
                          [f[6], subg, f[7]], [mixed])

                if msub < 5:
                    fw.barrier()
                    return
                wu = seg_w(8)
                wv = seg_w(9)
                for cc in range(2):
                    fm(wu, cc, ps[0], 0)
                    fw.op(A_, lambda cc=cc: nc.scalar.activation(out=gu[:, cc, :], in_=ps[0][:, 0:TB], func=AF.Gelu_apprx_tanh),
                          [ps[0]], [gu])
                fw.op(V, lambda: nc.vector.memset(vpad[:, :, :, :], 0.0), [], [vpad])
                for tl in range(2):
                    P = ps[1]
                    tm(wv, tl, P, 0)
                    fw.op(A_, lambda: nc.scalar.activation(out=f[0][:, 0:256], in_=P[:, 0:256], func=AF.Gelu_apprx_tanh), [P], [f[0]])
                    fw.op(V, lambda: nc.vector.bn_stats(out=bnst[:, 0, 0:6], in_=f[0][:, 0:256]), [f[0]], [bnst])
                    fw.op(V, lambda: nc.vector.bn_aggr(out=bnst[:, 1, 0:2], in_=bnst[:, 0, 0:6]), [bnst], [bnst])
                    rs(bnst[:, 1, 2:3], bnst[:, 1, 1:2], [bnst], [bnst])
                    fw.op(V, lambda: nc.vector.tensor_scalar(out=f[0][:, 256:512], in0=f[0][:, 0:256], scalar1=bnst[:, 1, 0:1],
                                                             scalar2=bnst[:, 1, 2:3], op0=ALU.subtract, op1=ALU.mult), [f[0], bnst], [f[0]])
                    fw.op(V, lambda: nc.vector.tensor_tensor(out=f[0][:, 0:256], in0=f[0][:, 256:512], in1=rowt[:, 0:256], op=ALU.mult),
                          [f[0], rowt], [f[0]])
                    for par in range(2):
                        fw.op(V, lambda tl=tl, par=par: nc.vector.tensor_tensor(
                            out=vpad[:, tl, :, :].rearrange("p (c r) n -> p c r n", r=2)[:, :, par, par * 64:par * 64 + 64],
                            in0=f[0][:, 0:256].rearrange("p (c r n) -> p c r n", c=2, r=2)[:, :, par, :],
                            in1=rowt[:, 256:512].rearrange("p (c r n) -> p c r n", c=2, r=2)[:, :, par, :], op=ALU.add),
                            [f[0], rowt], [vpad])
                for cc in range(2):
                    P = ps[2]
                    for tl in range(2):
                        for par in range(2):
                            fw.op(PE_, lambda tl=tl, par=par: nc.tensor.matmul(
                                P[:, tl * 128:(tl + 1) * 128], lhsT=vpad[:, tl, 2 * cc + par, :], rhs=wsb[:, 2 * cc + par, :],
                                start=(par == 0), stop=(par == 1)), [vpad, wsb], [P], inc=(par == 1))
                    fw.op(V, lambda cc=cc: nc.vector.tensor_tensor(
                        out=f[1][:, 0:TB].rearrange("p (a b) -> p a b", a=2), in0=P[:, 0:TB].rearrange("p (a b) -> p a b", a=2),
                        in1=bst[:, cc, :].unsqueeze(1).to_broadcast([128, 2, 128]), op=ALU.add), [P, bst], [f[1]])
                    fw.op(V, lambda cc=cc: nc.vector.tensor_tensor(out=mixed[:, 6 + cc, :], in0=f[1][:, 0:TB], in1=gu[:, cc, :], op=ALU.mult),
                          [f[1], gu], [mixed])

                if debug and l == 0:
                    dbgt = sb("dbgt", [128, KC, TB], ctx=ph)
                    fw.op(A_, lambda: nc.scalar.copy(out=dbgt[:, :, :], in_=mixed[:, :, :]), [mixed], [dbgt])
                    fw.dma("sp", dbg[:, :, :], dbgt[:, :, :], reads=[dbgt])
                if msub < 6:
                    fw.barrier()
                    return
                for j in range(KC):
                    P = ps[3 + (j % 2)]
                    for k in range(KC):
                        fw.op(PE_, lambda k=k, j=j, P=P: nc.tensor.matmul(P[:, 0:TB], lhsT=wo[:, k, j * 128:(j + 1) * 128], rhs=mixed[:, k, :],
                                                                          start=(k == 0), stop=(k == KC - 1)), [wo, mixed], [P], inc=(k == KC - 1))
                    fw.op(V, lambda j=j, P=P: nc.vector.scalar_tensor_tensor(
                        out=xt[:, j, :], in0=P[:, 0:TB], scalar=modt[:, l, 16 + j, 0:1], in1=xt[:, j, :], op0=ALU.mult, op1=ALU.add),
                        [P, modt, xt], [xt])
                fw.barrier()

        def peer(l, s, cond=0):
            with ExitStack() as ph:
                h2 = sb("h2", [128, KC, TB], BF16, ctx=ph)
                wqs = [sb("wqs%d" % i, [128, KC, 256], BF16, ctx=ph) for i in range(2)]
                skb = sb("skb", [128, 2, 128], BF16, ctx=ph)
                qT = sb("qT", [128, 16, TB], BF16, ctx=ph)
                ssb = sb("ssb", [128, 16, 128], ctx=ph)
                swk = sb("swk", [128, 16, 128], ctx=ph)
                sv = sb("sv", [128, 16, 16], ctx=ph)
                cwk_v = swk[:, :, :].rearrange("p (h t) n -> p h (t n)", t=2)
                oh_v = ssb[:, :, :].rearrange("p (h t) (a b) -> p h (t a) b", t=2, a=8)
                si = sb("si", [128, 16, 16], U32, ctx=ph)
                sif = sb("sif", [128, 16, 16], ctx=ph)
                cs = sb("cs", [128, 8, 256], ctx=ph)
                tv = sb("tv", [128, 8, 16], ctx=ph)
                tpos = sb("tpos", [128, 8, 16], U32, ctx=ph)
                tu = sb("tu", [128, 2, 8, 16], U32, ctx=ph)
                tf = sb("tf", [128, 2, 8, 16], ctx=ph)
                trip = sb("trip", [128, 3, 8, 16], ctx=ph)
                tripT = sb("tripT", [128, 3, 128], ctx=ph)
                zz = sb("zz", [128, 8, 2], ctx=ph)
                NTS = 16
                Lhs = [sb("Lh%d" % i, [128, NTS, 128], BF16, ctx=ph) for i in range(2)]
                Rhs = [sb("Rh%d" % i, [128, NTS, 128], BF16, ctx=ph) for i in range(2)]
                Gall = sb("Gall", [128, TB, 128], BF16, ctx=ph)
                NSL = 4
                pus = [sb("pus%d" % i, [128, KC * 128], BF16, ctx=ph) for i in range(NSL)]
                pvs = [sb("pvs%d" % i, [128, D], BF16, ctx=ph) for i in range(NSL)]

                norm_mod(ph, l, 1, h2, 1, cond)
                load_w(PO, skb, skT[l])
                for jp in range(8):
                    w = wqs[jp % 2]
                    load_w(PO, w, wq[l].rearrange("(k p) n -> p k n", p=128)[:, :, jp * 256:(jp + 1) * 256])
                    P = ps[jp % 2]
                    for jj in range(2):
                        for k in range(KC):
                            fw.op(PE_, lambda k=k, jj=jj, w=w, P=P: nc.tensor.matmul(
                                P[:, jj * TB:(jj + 1) * TB], lhsT=w[:, k, jj * 128:(jj + 1) * 128], rhs=h2[:, k, :],
                                start=(k == 0), stop=(k == KC - 1)), [w, h2], [P], inc=(k == KC - 1))
                    fw.op(A_, lambda jp=jp, P=P: nc.scalar.copy(out=qT[:, 2 * jp:2 * jp + 2, :],
                                                                  in_=P[:, :].rearrange("p (a b) -> p a b", a=2)), [P], [qT])
                for tl in range(2):
                    tok = slice(tl * 128, (tl + 1) * 128)
                    for j in range(16):
                        P = ps[2 + j // 4]
                        fw.op(PE_, lambda j=j, P=P: nc.tensor.matmul(P[:, (j % 4) * 128:(j % 4 + 1) * 128], lhsT=qT[:, j, tok],
                                                                     rhs=skb[:, j % 2, :], start=True, stop=True), [qT, skb], [P],
                              inc=(j % 4 == 3))
                    for b4 in range(4):
                        fw.op(A_, lambda b4=b4: nc.scalar.copy(out=ssb[:, 4 * b4:4 * b4 + 4, :],
                                                                 in_=ps[2 + b4][:, :].rearrange("p (a b) -> p a b", a=4)), [ps[2 + b4]], [ssb])
                    for j in range(16):
                        fw.op(V, lambda j=j: nc.vector.max(out=sv[:, j, 0:8], in_=ssb[:, j, :]), [ssb], [sv])
                    for j in range(16):
                        fw.op(V, lambda j=j: nc.vector.max_index(out=si[:, j, 0:8], in_max=sv[:, j, 0:8], in_values=ssb[:, j, :]),
                              [sv, ssb], [si])
                    for j in range(16):
                        fw.op(V, lambda j=j: nc.vector.match_replace(out=swk[:, j, :], in_to_replace=sv[:, j, 0:8],
                                                                     in_values=ssb[:, j, :], imm_value=-1e30), [sv, ssb], [swk])
                    for j in range(16):
                        fw.op(V, lambda j=j: nc.vector.max(out=sv[:, j, 8:16], in_=swk[:, j, :]), [swk], [sv])
                    for j in range(16):
                        fw.op(V, lambda j=j: nc.vector.max_index(out=si[:, j, 8:16], in_max=sv[:, j, 8:16], in_values=swk[:, j, :]),
                              [sv, swk], [si])
                    fw.op(V, lambda: nc.vector.tensor_copy(out=sif[:, :, :], in_=si[:, :, :]), [si], [sif])
                    sv4 = sv[:, :, :].rearrange("p (h t) a -> p h t a", t=2)
                    fw.op(V, lambda: nc.vector.tensor_tensor(
                        out=cs[:, :, :].rearrange("p h (a b) -> p h a b", a=16),
                        in0=sv4[:, :, 0, :].unsqueeze(3).to_broadcast([128, 8, 16, 16]),
                        in1=sv4[:, :, 1, :].unsqueeze(2).to_broadcast([128, 8, 16, 16]), op=ALU.add), [sv], [cs])
                    for h in range(8):
                        fw.op(V, lambda h=h: nc.vector.max(out=tv[:, h, 0:8], in_=cs[:, h, :]), [cs], [tv])
                    for h in range(8):
                        fw.op(V, lambda h=h: nc.vector.max_index(out=tpos[:, h, 0:8], in_max=tv[:, h, 0:8], in_values=cs[:, h, :]),
                              [tv, cs], [tpos])
                    for h in range(8):
                        fw.op(V, lambda h=h: nc.vector.match_replace(out=cwk_v[:, h, :], in_to_replace=tv[:, h, 0:8],
                                                                     in_values=cs[:, h, :], imm_value=-1e30), [tv, cs], [swk])
                    for h in range(8):
                        fw.op(V, lambda h=h: nc.vector.max(out=tv[:, h, 8:16], in_=cwk_v[:, h, :]), [swk], [tv])
                    for h in range(8):
                        fw.op(V, lambda h=h: nc.vector.max_index(out=tpos[:, h, 8:16], in_max=tv[:, h, 8:16], in_values=cwk_v[:, h, :]),
                              [tv, swk], [tpos])
                    fw.op(V, lambda: nc.vector.tensor_tensor(out=trip[:, 2, :, :], in0=tv[:, :, :],
                                                             in1=tv[:, :, 0:1].to_broadcast([128, 8, 16]), op=ALU.subtract), [tv], [trip])
                    fw.op(A_, lambda: nc.scalar.activation(out=trip[:, 2, :, :], in_=trip[:, 2, :, :], func=AF.Exp), [trip], [trip])
                    fw.op(V, lambda: nc.vector.tensor_reduce(out=zz[:, :, 0], in_=trip[:, 2, :, :], axis=AX.X, op=ALU.add), [trip], [zz])
                    fw.op(V, lambda: nc.vector.reciprocal(out=zz[:, :, 1], in_=zz[:, :, 0]), [zz], [zz])
                    fw.op(V, lambda: nc.vector.tensor_tensor(out=trip[:, 2, :, :], in0=trip[:, 2, :, :],
                                                             in1=zz[:, :, 1:2].to_broadcast([128, 8, 16]), op=ALU.mult), [trip, zz], [trip])
                    fw.op(V, lambda: nc.vector.tensor_single_scalar(out=tu[:, 0, :, :], in_=tpos[:, :, :], scalar=4,
                                                                    op=ALU.logical_shift_right), [tpos], [tu])
                    fw.op(V, lambda: nc.vector.tensor_single_scalar(out=tu[:, 1, :, :], in_=tpos[:, :, :], scalar=15,
                                                                    op=ALU.bitwise_and), [tpos], [tu])
                    fw.op(V, lambda: nc.vector.tensor_copy(out=tf[:, :, :, :], in_=tu[:, :, :, :]), [tu], [tf])
                    sif4 = sif[:, :, :].rearrange("p (h t) a -> p h t a", t=2)
                    for w_ in range(2):
                        fw.op(V, lambda w_=w_: nc.vector.tensor_tensor(
                            out=oh_v[:, :, :, :], in0=iota16.unsqueeze(1).unsqueeze(1).to_broadcast([128, 8, 16, 16]),
                            in1=tf[:, w_, :, :].unsqueeze(3).to_broadcast([128, 8, 16, 16]), op=ALU.is_equal), [cst, tf], [ssb])
                        fw.op(V, lambda w_=w_: nc.vector.tensor_tensor(
                            out=oh_v[:, :, :, :], in0=oh_v[:, :, :, :],
                            in1=sif4[:, :, w_, :].unsqueeze(2).to_broadcast([128, 8, 16, 16]), op=ALU.mult), [ssb, sif], [ssb])
                        fw.op(V, lambda w_=w_: nc.vector.tensor_reduce(out=trip[:, w_, :, :], in_=oh_v[:, :, :, :], axis=AX.X, op=ALU.add),
                              [ssb], [trip])
                    P = ps[6]
                    for w_ in range(3):
                        fw.op(PE_, lambda w_=w_: nc.tensor.transpose(out=P[:, w_ * 128:(w_ + 1) * 128],
                                                                     in_=trip[:, w_, :, :].rearrange("p h k -> p (h k)"), identity=identF),
                              [trip, cst], [P], inc=(w_ == 2))
                    fw.op(A_, lambda: nc.scalar.copy(out=tripT[:, :, :], in_=P[:, 0:384].rearrange("p (a b) -> p a b", a=3)), [P], [tripT])
                    for t4 in range(128 // NTS):
                        tsl = slice(t4 * NTS, (t4 + 1) * NTS)
                        Lh = Lhs[t4 % 2]
                        Rh = Rhs[t4 % 2]
                        for tt_ in range(NTS):
                            tg = t4 * NTS + tt_
                            fw.op(V, lambda tt_=tt_, tg=tg, Lh=Lh: nc.vector.tensor_scalar(
                                out=Lh[:, tt_, :], in0=iotaN, scalar1=tripT[:, 0, tg:tg + 1], scalar2=None, op0=ALU.is_equal),
                                [cst, tripT], [Lh], disjoint=(tt_ > 0))
                            fw.op(V, lambda tt_=tt_, tg=tg, Rh=Rh: nc.vector.tensor_scalar(
                                out=Rh[:, tt_, :], in0=iotaN, scalar1=tripT[:, 1, tg:tg + 1], scalar2=tripT[:, 2, tg:tg + 1],
                                op0=ALU.is_equal, op1=ALU.mult), [cst, tripT], [Rh], disjoint=(tt_ > 0))
                        for g4 in range(NTS // 4):
                            P = ps[g4 % 2]
                            for u in range(4):
                                tt_ = g4 * 4 + u
                                fw.op(PE_, lambda tt_=tt_, u=u, P=P, Lh=Lh, Rh=Rh: nc.tensor.matmul(P[:, u * 128:(u + 1) * 128], lhsT=Lh[:, tt_, :],
                                                                                      rhs=Rh[:, tt_, :], start=True, stop=True),
                                      [Lh, Rh], [P], inc=(u == 3))
                            tb_ = tl * 128 + t4 * NTS + g4 * 4
                            fw.op(A_, lambda tb_=tb_, P=P: nc.scalar.copy(out=Gall[:, tb_:tb_ + 4, :],
                                                                          in_=P[:, :].rearrange("p (a b) -> p a b", a=4)), [P], [Gall])
                Aslot = [Buf("aslot%d" % i) for i in range(4)]
                ga = [sb("gax%d" % i, [128, TB], BF16, ctx=ph) for i in range(3)]
                wgt = [sb("wgx%d" % i, [128, TB], BF16, ctx=ph) for i in range(3)]

                def a_part(c):
                    pu_ = pus[c % NSL]
                    pv_ = pvs[c % NSL]
                    fw.dma("sp", pu_[:, :], pubf[l, c], reads=[WSC], writes=[pu_])
                    fw.dma("sp", pv_[:, :], pvbf[l, c], reads=[WSC], writes=[pv_])
                    P = ps[c % 4]
                    col = 0
                    wr = [Aslot[c % 4]] + ([P] if c < 4 else [])
                    for k in range(KC):
                        fw.op(PE_, lambda k=k, pu_=pu_, P=P, col=col: nc.tensor.matmul(
                            P[:, col:col + TB], lhsT=pu_[:, k * 128:(k + 1) * 128], rhs=h2[:, k, :], start=(k == 0), stop=(k == KC - 1)),
                            [pu_, h2], wr, inc=(k == KC - 1))

                def o_part(c):
                    pv_ = pvs[c % NSL]
                    P = ps[c % 4]
                    col = 0
                    g_ = ga[c % 3]
                    w_ = wgt[c % 3]
                    fw.op(A_, lambda g_=g_, P=P, col=col: nc.scalar.activation(out=g_[:, :], in_=P[:, col:col + TB], func=AF.Gelu_apprx_tanh),
                          [Aslot[c % 4]], [g_])
                    fw.op(V, lambda g_=g_, w_=w_, c=c: nc.vector.tensor_tensor(out=w_[:, :], in0=g_[:, :], in1=Gall[:, :, c], op=ALU.mult),
                          [g_, Gall], [w_])
                    for j in range(KC):
                        OPS = ps[4 + j // 2]
                        fw.op(PE_, lambda j=j, OPS=OPS, pv_=pv_, w_=w_, c=c: nc.tensor.matmul(
                            OPS[:, (j % 2) * TB:(j % 2 + 1) * TB], lhsT=pv_[:, j * 128:(j + 1) * 128], rhs=w_[:, :],
                            start=(c == 0 and j % 2 == 0), stop=(c == NEXP_C - 1), skip_group_check=True), [pv_, w_], [OPS], inc=(j == KC - 1))

                LOOK = int(os.environ.get("LOOK", "2"))
                for c in range(min(LOOK, NEXP_C)):
                    a_part(c)
                for c in range(NEXP_C):
                    if LOOK == 0:
                        a_part(c)
                    elif c + LOOK < NEXP_C:
                        a_part(c + LOOK)
                    o_part(c)
                for j in range(KC):
                    OPS = ps[4 + j // 2]
                    fw.op(V, lambda j=j, OPS=OPS: nc.vector.scalar_tensor_tensor(
                        out=xt[:, j, :], in0=OPS[:, (j % 2) * TB:(j % 2 + 1) * TB], scalar=modt[:, l, 40 + j, cond:cond + 1], in1=xt[:, j, :],
                        op0=ALU.mult, op1=ALU.add), [OPS, modt, xt], [xt])
                fw.barrier()


        def peer_pass(l, nblk, cond, xsrc_of, xdst, SBx):
            with ExitStack() as ph:
                xts = [sb("pxt%d" % i, [128, KC, TB], ctx=ph) for i in range(2)]
                h2s = [sb("ph2%d" % i, [128, KC, TB], BF16, ctx=ph) for i in range(2)]
                tTs = [[sb("ptT%d%d" % (i, t), [128, 3, 128], ctx=ph) for t in range(2)] for i in range(2)]
                ntm = norm_tmps(ph, 7)
                wqs = [sb("wqs%d" % i, [128, KC, 256], BF16, ctx=ph) for i in range(2)]
                skb = sb("skb", [128, 2, 128], BF16, ctx=ph)
                qT = sb("qT", [128, 16, TB], BF16, ctx=ph)
                ssb = sb("ssb", [128, 16, 128], ctx=ph)
                swk = sb("swk", [128, 16, 128], ctx=ph)
                sv = sb("sv", [128, 16, 16], ctx=ph)
                cwk_v = swk[:, :, :].rearrange("p (h t) n -> p h (t n)", t=2)
                oh_v = ssb[:, :, :].rearrange("p (h t) (a b) -> p h (t a) b", t=2, a=8)
                si = sb("si", [128, 16, 16], U32, ctx=ph)
                sif = sb("sif", [128, 16, 16], ctx=ph)
                cs = sb("cs", [128, 8, 256], ctx=ph)
                tv = sb("tv", [128, 8, 16], ctx=ph)
                tpos = sb("tpos", [128, 8, 16], U32, ctx=ph)
                tu = sb("tu", [128, 2, 8, 16], U32, ctx=ph)
                tf = sb("tf", [128, 2, 8, 16], ctx=ph)
                trip = sb("trip", [128, 3, 8, 16], ctx=ph)
                zz = sb("zz", [128, 8, 2], ctx=ph)
                NTS = 8
                Lhs = [sb("Lh%d" % i, [128, NTS, 128], BF16, ctx=ph) for i in range(2)]
                Rhs = [sb("Rh%d" % i, [128, NTS, 128], BF16, ctx=ph) for i in range(2)]
                Gall = sb("Gall", [128, TB, 128], BF16, ctx=ph)
                NSL = 4
                pus = [sb("pus%d" % i, [128, KC * 128], BF16, ctx=ph) for i in range(NSL)]
                pvs = [sb("pvs%d" % i, [128, D], BF16, ctx=ph) for i in range(NSL)]
                ga = [sb("gax%d" % i, [128, TB], BF16, ctx=ph) for i in range(3)]
                wgt = [sb("wgx%d" % i, [128, TB], BF16, ctx=ph) for i in range(3)]
                PB = ps[3]
                load_w(PO, skb, skT[l])
                sv4 = sv[:, :, :].rearrange("p (h t) a -> p h t a", t=2)
                sif4 = sif[:, :, :].rearrange("p (h t) a -> p h t a", t=2)

                def pre(s):
                    xt_ = xts[s % 2]
                    h2 = h2s[s % 2]
                    t0 = s * TB
                    fw.dma("sp", xt_[:, :, :], xsrc_of(s), reads=[SBx], writes=[xt_])
                    norm_mod(ph, l, 1, h2, 1, cond, xt=xt_, P=PB, tmps=ntm)
                    yield
                    for jp in range(8):
                        w = wqs[jp % 2]
                        load_w(PO, w, wq[l].rearrange("(k p) n -> p k n", p=128)[:, :, jp * 256:(jp + 1) * 256])
                        for jj in range(2):
                            for k in range(KC):
                                fw.op(PE_, lambda k=k, jj=jj, w=w: nc.tensor.matmul(
                                    PB[:, jj * TB:(jj + 1) * TB], lhsT=w[:, k, jj * 128:(jj + 1) * 128], rhs=h2[:, k, :],
                                    start=(k == 0), stop=(k == KC - 1)), [w, h2], [PB], inc=(k == KC - 1))
                        fw.op(A_, lambda jp=jp: nc.scalar.copy(out=qT[:, 2 * jp:2 * jp + 2, :],
                                                                 in_=PB[:, :].rearrange("p (a b) -> p a b", a=2)), [PB], [qT])
                        yield
                    for tl in range(2):
                        tok = slice(tl * 128, (tl + 1) * 128)
                        tripT = tTs[s % 2][tl]
                        for b4 in range(4):
                            for u in range(4):
                                j = b4 * 4 + u
                                fw.op(PE_, lambda j=j, u=u: nc.tensor.matmul(PB[:, u * 128:(u + 1) * 128], lhsT=qT[:, j, tok],
                                                                             rhs=skb[:, j % 2, :], start=True, stop=True), [qT, skb], [PB],
                                      inc=(u == 3))
                            fw.op(A_, lambda b4=b4: nc.scalar.copy(out=ssb[:, 4 * b4:4 * b4 + 4, :],
                                                                     in_=PB[:, :].rearrange("p (a b) -> p a b", a=4)), [PB], [ssb])
                            yield
                        for j in range(16):
                            fw.op(V, lambda j=j: nc.vector.max(out=sv[:, j, 0:8], in_=ssb[:, j, :]), [ssb], [sv], disjoint=(j > 0))
                        yield
                        for j in range(16):
                            fw.op(V, lambda j=j: nc.vector.max_index(out=si[:, j, 0:8], in_max=sv[:, j, 0:8], in_values=ssb[:, j, :]),
                                  [sv, ssb], [si], disjoint=(j > 0))
                        yield
                        for j in range(16):
                            fw.op(V, lambda j=j: nc.vector.match_replace(out=swk[:, j, :], in_to_replace=sv[:, j, 0:8],
                                                                         in_values=ssb[:, j, :], imm_value=-1e30), [sv, ssb], [swk], disjoint=(j > 0))
                        yield
                        for j in range(16):
                            fw.op(V, lambda j=j: nc.vector.max(out=sv[:, j, 8:16], in_=swk[:, j, :]), [swk], [sv], disjoint=(j > 0))
                        yield
                        for j in range(16):
                            fw.op(V, lambda j=j: nc.vector.max_index(out=si[:, j, 8:16], in_max=sv[:, j, 8:16], in_values=swk[:, j, :]),
                                  [sv, swk], [si], disjoint=(j > 0))
                        fw.op(V, lambda: nc.vector.tensor_copy(out=sif[:, :, :], in_=si[:, :, :]), [si], [sif])
                        fw.op(V, lambda: nc.vector.tensor_tensor(
                            out=cs[:, :, :].rearrange("p h (a b) -> p h a b", a=16),
                            in0=sv4[:, :, 0, :].unsqueeze(3).to_broadcast([128, 8, 16, 16]),
                            in1=sv4[:, :, 1, :].unsqueeze(2).to_broadcast([128, 8, 16, 16]), op=ALU.add), [sv], [cs])
                        yield
                        for h in range(8):
                            fw.op(V, lambda h=h: nc.vector.max(out=tv[:, h, 0:8], in_=cs[:, h, :]), [cs], [tv], disjoint=(h > 0))
                        for h in range(8):
                            fw.op(V, lambda h=h: nc.vector.max_index(out=tpos[:, h, 0:8], in_max=tv[:, h, 0:8], in_values=cs[:, h, :]),
                                  [tv, cs], [tpos], disjoint=(h > 0))
                        yield
                        for h in range(8):
                            fw.op(V, lambda h=h: nc.vector.match_replace(out=cwk_v[:, h, :], in_to_replace=tv[:, h, 0:8],
                                                                         in_values=cs[:, h, :], imm_value=-1e30), [tv, cs], [swk], disjoint=(h > 0))
                        for h in range(8):
                            fw.op(V, lambda h=h: nc.vector.max(out=tv[:, h, 8:16], in_=cwk_v[:, h, :]), [swk], [tv], disjoint=(h > 0))
                        yield
                        for h in range(8):
                            fw.op(V, lambda h=h: nc.vector.max_index(out=tpos[:, h, 8:16], in_max=tv[:, h, 8:16], in_values=cwk_v[:, h, :]),
                                  [tv, swk], [tpos], disjoint=(h > 0))
                        fw.op(V, lambda: nc.vector.tensor_tensor(out=trip[:, 2, :, :], in0=tv[:, :, :],
                                                                 in1=tv[:, :, 0:1].to_broadcast([128, 8, 16]), op=ALU.subtract), [tv], [trip])
                        fw.op(A_, lambda: nc.scalar.activation(out=trip[:, 2, :, :], in_=trip[:, 2, :, :], func=AF.Exp), [trip], [trip])
                        fw.op(V, lambda: nc.vector.tensor_reduce(out=zz[:, :, 0], in_=trip[:, 2, :, :], axis=AX.X, op=ALU.add), [trip], [zz])
                        fw.op(V, lambda: nc.vector.reciprocal(out=zz[:, :, 1], in_=zz[:, :, 0]), [zz], [zz])
                        fw.op(V, lambda: nc.vector.tensor_tensor(out=trip[:, 2, :, :], in0=trip[:, 2, :, :],
                                                                 in1=zz[:, :, 1:2].to_broadcast([128, 8, 16]), op=ALU.mult), [trip, zz], [trip])
                        yield
                        fw.op(V, lambda: nc.vector.tensor_single_scalar(out=tu[:, 0, :, :], in_=tpos[:, :, :], scalar=4,
                                                                        op=ALU.logical_shift_right), [tpos], [tu])
                        fw.op(V, lambda: nc.vector.tensor_single_scalar(out=tu[:, 1, :, :], in_=tpos[:, :, :], scalar=15,
                                                                        op=ALU.bitwise_and), [tpos], [tu])
                        fw.op(V, lambda: nc.vector.tensor_copy(out=tf[:, :, :, :], in_=tu[:, :, :, :]), [tu], [tf])
                        for w_ in range(2):
                            fw.op(V, lambda w_=w_: nc.vector.tensor_tensor(
                                out=oh_v[:, :, :, :], in0=iota16.unsqueeze(1).unsqueeze(1).to_broadcast([128, 8, 16, 16]),
                                in1=tf[:, w_, :, :].unsqueeze(3).to_broadcast([128, 8, 16, 16]), op=ALU.is_equal), [cst, tf], [ssb])
                            fw.op(V, lambda w_=w_: nc.vector.tensor_tensor(
                                out=oh_v[:, :, :, :], in0=oh_v[:, :, :, :],
                                in1=sif4[:, :, w_, :].unsqueeze(2).to_broadcast([128, 8, 16, 16]), op=ALU.mult), [ssb, sif], [ssb])
                            fw.op(V, lambda w_=w_: nc.vector.tensor_reduce(out=trip[:, w_, :, :], in_=oh_v[:, :, :, :], axis=AX.X, op=ALU.add),
                                  [ssb], [trip])
                            yield
                        for w_ in range(3):
                            fw.op(PE_, lambda w_=w_: nc.tensor.transpose(out=PB[:, w_ * 128:(w_ + 1) * 128],
                                                                         in_=trip[:, w_, :, :].rearrange("p h k -> p (h k)"), identity=identF),
                                  [trip, cst], [PB], inc=(w_ == 2))
                        fw.op(A_, lambda tripT=tripT: nc.scalar.copy(out=tripT[:, :, :], in_=PB[:, 0:384].rearrange("p (a b) -> p a b", a=3)),
                              [PB], [tripT])
                        yield

                def gbuild(s):
                    gi = 0
                    for tl in range(2):
                        tripT = tTs[s % 2][tl]
                        for t4 in range(128 // NTS):
                            Lh = Lhs[gi % 2]
                            Rh = Rhs[gi % 2]
                            for tt_ in range(NTS):
                                tg = t4 * NTS + tt_
                                fw.op(V, lambda tt_=tt_, tg=tg, Lh=Lh, tripT=tripT: nc.vector.tensor_scalar(
                                    out=Lh[:, tt_, :], in0=iotaN, scalar1=tripT[:, 0, tg:tg + 1], scalar2=None, op0=ALU.is_equal),
                                    [cst, tripT], [Lh], disjoint=(tt_ > 0))
                                fw.op(V, lambda tt_=tt_, tg=tg, Rh=Rh, tripT=tripT: nc.vector.tensor_scalar(
                                    out=Rh[:, tt_, :], in0=iotaN, scalar1=tripT[:, 1, tg:tg + 1], scalar2=tripT[:, 2, tg:tg + 1],
                                    op0=ALU.is_equal, op1=ALU.mult), [cst, tripT], [Rh], disjoint=(tt_ > 0))
                            for g4 in range(NTS // 4):
                                P = ps[gi % 2 * 2 + g4 % 2] if False else ps[(2 * gi + g4) % 3]
                                for u in range(4):
                                    tt_ = g4 * 4 + u
                                    fw.op(PE_, lambda tt_=tt_, u=u, P=P, Lh=Lh, Rh=Rh: nc.tensor.matmul(P[:, u * 128:(u + 1) * 128], lhsT=Lh[:, tt_, :],
                                                                                                          rhs=Rh[:, tt_, :], start=True, stop=True),
                                          [Lh, Rh], [P], inc=(u == 3))
                                tb_ = tl * 128 + t4 * NTS + g4 * 4
                                fw.op(A_, lambda tb_=tb_, P=P: nc.scalar.copy(out=Gall[:, tb_:tb_ + 4, :],
                                                                              in_=P[:, :].rearrange("p (a b) -> p a b", a=4)), [P], [Gall],
                                      disjoint=(tb_ > 0))
                            gi += 1

                def sweep(s, gen):
                    xt_ = xts[s % 2]
                    h2 = h2s[s % 2]
                    Aslot = [Buf("aslot%d" % i) for i in range(3)]

                    def a_part(c):
                        pu_ = pus[c % NSL]
                        pv_ = pvs[c % NSL]
                        fw.dma("sp", pu_[:, :], pubf[l, c], reads=[WSC], writes=[pu_])
                        fw.dma("sp", pv_[:, :], pvbf[l, c], reads=[WSC], writes=[pv_])
                        P = ps[c % 3]
                        wr = [Aslot[c % 3]] + ([P] if c < 3 else [])
                        for k in range(KC):
                            fw.op(PE_, lambda k=k, pu_=pu_, P=P: nc.tensor.matmul(
                                P[:, 0:TB], lhsT=pu_[:, k * 128:(k + 1) * 128], rhs=h2[:, k, :], start=(k == 0), stop=(k == KC - 1)),
                                [pu_, h2], wr, inc=(k == KC - 1))

                    def o_part(c):
                        pv_ = pvs[c % NSL]
                        P = ps[c % 3]
                        g_ = ga[c % 3]
                        w_ = wgt[c % 3]
                        fw.op(A_, lambda g_=g_, P=P: nc.scalar.activation(out=g_[:, :], in_=P[:, 0:TB], func=AF.Gelu_apprx_tanh),
                              [Aslot[c % 3]], [g_])
                        fw.op(V, lambda g_=g_, w_=w_, c=c: nc.vector.tensor_tensor(out=w_[:, :], in0=g_[:, :], in1=Gall[:, :, c], op=ALU.mult),
                              [g_, Gall], [w_])
                        for j in range(KC):
                            OPS = ps[4 + j // 2]
                            fw.op(PE_, lambda j=j, OPS=OPS, pv_=pv_, w_=w_, c=c: nc.tensor.matmul(
                                OPS[:, (j % 2) * TB:(j % 2 + 1) * TB], lhsT=pv_[:, j * 128:(j + 1) * 128], rhs=w_[:, :],
                                start=(c == 0 and j % 2 == 0), stop=(c == NEXP_C - 1), skip_group_check=True), [pv_, w_], [OPS], inc=(j == KC - 1))

                    NSTEP = 60
                    done = [0]

                    def advance(upto):
                        while gen is not None and done[0] < upto:
                            try:
                                next(gen)
                            except StopIteration:
                                done[0] = 10 ** 9
                                return
                            done[0] += 1
                    for c in range(2):
                        a_part(c)
                    for c in range(NEXP_C):
                        if c + 2 < NEXP_C:
                            a_part(c + 2)
                        o_part(c)
                        advance((c + 1) * NSTEP // NEXP_C)
                    advance(10 ** 8)
                    for j in range(KC):
                        OPS = ps[4 + j // 2]
                        fw.op(V, lambda j=j, OPS=OPS: nc.vector.scalar_tensor_tensor(
                            out=xt_[:, j, :], in0=OPS[:, (j % 2) * TB:(j % 2 + 1) * TB], scalar=modt[:, l, 40 + j, cond:cond + 1], in1=xt_[:, j, :],
                            op0=ALU.mult, op1=ALU.add), [OPS, modt, xt_], [xt_])
                    fw.dma("sp", xdst[:, :, s * TB:(s + 1) * TB], xt_[:, :, :], reads=[xt_], writes=[SBx])

                g0 = pre(0)
                for _ in g0:
                    pass
                for s in range(nblk):
                    gbuild(s)
                    sweep(s, pre(s + 1) if s + 1 < nblk else None)
                fw.barrier()
        def load_seg(l, seg, w):
            load_w(PO, w, w_in[l].rearrange("(k p) n -> p k n", p=128)[:, :, seg * 256:(seg + 1) * 256])
            return w

        def pfm(w, hb, cc, P, col=0):
            for k in range(KC):
                fw.op(PE_, lambda k=k: nc.tensor.matmul(P[:, col:col + TB], lhsT=w[:, k, cc * 128:(cc + 1) * 128],
                                                        rhs=hb[:, k, :], start=(k == 0), stop=(k == KC - 1)),
                      [w, hb], [P], inc=(k == KC - 1))

        def ptm(w, hb, tl, P, col=0):
            for k in range(KC):
                fw.op(PE_, lambda k=k: nc.tensor.matmul(P[:, col:col + 256], lhsT=hb[:, k, tl * 128:(tl + 1) * 128],
                                                        rhs=w[:, k, :], start=(k == 0), stop=(k == KC - 1)),
                      [w, hb], [P], inc=(k == KC - 1))

        def rope(src_t, src_ap, tA, tB, cs_, sn_, dst_ap, dst_t, P):
            fw.op(PE_, lambda: nc.tensor.matmul(P[:, 0:TB], lhsT=cst[:, CC_PSW:CC_PSW + 128], rhs=src_ap, start=True, stop=True),
                  [cst, src_t], [P])
            fw.op(V, lambda: nc.vector.tensor_tensor(out=tA[:, 0:TB], in0=src_ap, in1=cs_[:, :], op=ALU.mult), [src_t, cs_], [tA])
            fw.op(V, lambda: nc.vector.tensor_tensor(out=tB[:, 0:TB], in0=P[:, 0:TB], in1=sn_[:, :], op=ALU.mult), [P, sn_], [tB])
            fw.op(V, lambda: nc.vector.tensor_tensor(out=dst_ap, in0=tA[:, 0:TB], in1=tB[:, 0:TB], op=ALU.add), [tA, tB], [dst_t])

        def latA(l, s, src, dkg, dvg, SB):
            t0 = s * TB
            with ExitStack() as ph:
                hb = sb("hb", [128, KC, TB], BF16, ctx=ph)
                wseg = [sb("wseg%d" % i, [128, KC, 256], BF16, ctx=ph) for i in range(3)]
                f = [sb("f%d" % i, [128, 512], ctx=ph) for i in range(6)]
                kb = sb("kb", [128, 2, TB], BF16, ctx=ph)
                yb = sb("yb", [128, 2, TB], BF16, ctx=ph)
                vob = sb("vob", [128, 2, 256], BF16, ctx=ph)
                cs_ = sb("cs_", [128, TB], ctx=ph)
                sn_ = sb("sn_", [128, TB], ctx=ph)
                fw.dma("sp", xt[:, :, :], src[:, :, t0:t0 + TB], writes=[xt])
                fw.dma("sp", cs_[:, :], cosT[:, t0:t0 + TB], writes=[cs_])
                fw.dma("sp", sn_[:, :], sinT[:, t0:t0 + TB], writes=[sn_])
                norm_mod(ph, l, 0, hb, 0, 1)
                ws = [0]

                def seg(i):
                    w = wseg[ws[0] % 3]
                    ws[0] += 1
                    return load_seg(l, i, w)
                wa = seg(0)
                wg = seg(1)
                for cc in range(2):
                    pfm(wa, hb, cc, ps[2], 0)
                    pfm(wg, hb, cc, ps[2], TB)
                    fw.op(A_, lambda: nc.scalar.activation(out=f[0][:, 0:TB], in_=ps[2][:, TB:2 * TB], func=AF.Sigmoid), [ps[2]], [f[0]])
                    fw.op(V, lambda cc=cc: nc.vector.tensor_tensor(out=yb[:, cc, :], in0=ps[2][:, 0:TB], in1=f[0][:, 0:TB], op=ALU.mult),
                          [ps[2], f[0]], [yb])
                fw.dma("sp", ybuf[:, :, 15 + t0:15 + t0 + TB], yb[:, :, :], reads=[yb], writes=[SB["y"]])
                wk = seg(3)
                for cc in range(2):
                    P = ps[4]
                    pfm(wk, hb, cc, P, 0)
                    fw.op(A_, lambda: nc.scalar.activation(out=f[1][:, 0:TB], in_=P[:, 0:TB], func=AF.Square), [P], [f[1]])
                    fw.op(PE_, lambda: nc.tensor.matmul(P[:, TB:2 * TB], lhsT=blk64, rhs=f[1][:, 0:TB], start=True, stop=True), [cst, f[1]], [P])
                    rs(f[2][:, 0:TB], P[:, TB:2 * TB], [P], [f[2]])
                    fw.op(V, lambda cc=cc: nc.vector.scalar_tensor_tensor(out=kb[:, cc, :], in0=P[:, 0:TB], scalar=ppt[:, l, PC_NAK:PC_NAK + 1],
                                                                           in1=f[2][:, 0:TB], op0=ALU.mult, op1=ALU.mult), [P, ppt, f[2]], [kb])
                fw.dma("sp", nks[:, :, t0:t0 + TB], kb[:, :, :], reads=[kb], writes=[SB["k"]])
                wv = seg(4)
                for tl in range(2):
                    ptm(wv, hb, tl, ps[5], 0)
                    fw.op(A_, lambda tl=tl: nc.scalar.copy(out=vob[:, tl, :], in_=ps[5][:, 0:256]), [ps[5]], [vob])
                fw.dma("sp", nvs[:, 2 * s:2 * s + 2, :], vob[:, :, :], reads=[vob], writes=[SB["v"]])
                wk2 = seg(6)
                for cc in range(2):
                    P = ps[4]
                    pfm(wk2, hb, cc, P, 0)
                    fw.op(A_, lambda: nc.scalar.activation(out=f[1][:, 0:TB], in_=P[:, 0:TB], func=AF.Square), [P], [f[1]])
                    fw.op(PE_, lambda: nc.tensor.matmul(P[:, TB:2 * TB], lhsT=blk32, rhs=f[1][:, 0:TB], start=True, stop=True), [cst, f[1]], [P])
                    rs(f[2][:, 0:TB], P[:, TB:2 * TB], [P], [f[2]])
                    fw.op(V, lambda: nc.vector.scalar_tensor_tensor(out=f[3][:, 0:TB], in0=P[:, 0:TB], scalar=ppt[:, l, PC_DK:PC_DK + 1],
                                                                    in1=f[2][:, 0:TB], op0=ALU.mult, op1=ALU.mult), [P, ppt, f[2]], [f[3]])
                    rope(f[3], f[3][:, 0:TB], f[4], f[5], cs_, sn_, dkg[:, cc, t0:t0 + TB], dkg, ps[3])
                wv2 = seg(7)
                for tl in range(2):
                    ptm(wv2, hb, tl, ps[5], 0)
                    fw.op(A_, lambda tl=tl: nc.scalar.copy(out=dvg[:, 2 * s + tl, :], in_=ps[5][:, 0:256]), [ps[5]], [dvg])
                fw.barrier()

        def latB(l, s, src, dkg, dvg, SB, CA):
            t0 = s * TB
            nck, ncvt, dck, dcvt, biast, qsc = CA
            ones64 = onesE[:, 0:64]
            with ExitStack() as ph:
                hb = sb("hb", [128, KC, TB], BF16, ctx=ph)
                mixed = sb("mixed", [128, KC, TB], BF16, ctx=ph)
                mixh = sb("mixh", [64, 8, TB], BF16, ctx=ph)
                wseg = [sb("wseg%d" % i, [128, KC, 256], BF16, ctx=ph) for i in range(2)]
                wo = sb("wo", [128, 4, D], BF16, ctx=ph)
                woh = sb("woh", [64, 8, D], BF16, ctx=ph)
                wsb = sb("wsb", [128, 4, 128], BF16, ctx=ph)
                rowt = sb("rowt", [128, 512], ctx=ph)
                bst = sb("bst", [128, 2, 128], ctx=ph)
                diagw = sb("diagw", [128, 2, 31, 128], BF16, ctx=ph)
                ywin = sb("ywin", [128, 2, TB + 30], BF16, ctx=ph)
                f = [sb("f%d" % i, [128, 512], ctx=ph) for i in range(6)]
                qb = sb("qb", [128, 2, 2, TB], BF16, ctx=ph)
                dqv = sb("dqv", [128, 4, 2, TB], BF16, ctx=ph)
                nkw = sb("nkw", [128, 2, 768], BF16, ctx=ph)
                nvw = sb("nvw", [128, 6, 256], BF16, ctx=ph)
                pt = [sb("pt%d" % i, [128, 512], BF16, ctx=ph) for i in range(4)]
                pt2 = sb("pt2", [128, 1024], BF16, ctx=ph)
                gu = sb("gu", [128, 2, TB], ctx=ph)
                vpad = sb("vpad", [128, 2, 4, 128], BF16, ctx=ph)
                bnst = sb("bnst", [128, 2, 8], ctx=ph)
                cs_ = sb("cs_", [128, TB], ctx=ph)
                sn_ = sb("sn_", [128, TB], ctx=ph)
                fw.dma("sp", xt[:, :, :], src[:, :, t0:t0 + TB], writes=[xt])
                fw.dma("sp", cs_[:, :], cosT[:, t0:t0 + TB], writes=[cs_])
                fw.dma("sp", sn_[:, :], sinT[:, t0:t0 + TB], writes=[sn_])
                fw.dma("sp", ywin[:, :, :], ybuf[:, :, t0:t0 + TB + 30], reads=[SB["y"]], writes=[ywin])
                jlo, jhi = max(0, 2 * s - 2), min(SEQL // 128 - 1, 2 * s + 3)
                nt = jhi - jlo + 1
                fw.dma("sp", nkw[:, :, 0:nt * 128], nks[:, :, jlo * 128:(jhi + 1) * 128], reads=[SB["k"]], writes=[nkw])
                fw.dma("sp", nvw[:, 0:nt, :], nvs[:, jlo:jhi + 1, :], reads=[SB["v"]], writes=[nvw])
                norm_mod(ph, l, 0, hb, 0, 1)
                wrows = w_out[l].rearrange("(k p) n -> p k n", p=128)
                fw.dma(PO, wo[:, 0:2, :], wrows[:, 0:2, :], writes=[wo])
                fw.dma(PO, wo[:, 2:4, :], wrows[:, 6:8, :], writes=[wo])
                fw.dma(PO, woh[:, :, :], w_out[l][256:768, :].rearrange("(h p) n -> p h n", p=64), writes=[woh])
                load_w(PO, wsb, wsT[l])
                fw.dma("sp", rowt[:, :], rowp[l].partition_broadcast(128), writes=[rowt])
                fw.dma("sp", bst[:, :, :], bsT[l], writes=[bst])
                fw.op(V, lambda: nc.vector.memset(vpad[:, :, :, :], 0.0), [], [vpad])
                for cc in range(2):
                    for tap in range(31):
                        fw.op(V, lambda cc=cc, tap=tap, e=nc.vector: e.tensor_scalar(
                            out=diagw[:, cc, tap, :], in0=identB, scalar1=ppt[:, l, PC_CONVW + cc * 31 + tap:PC_CONVW + cc * 31 + tap + 1],
                            scalar2=None, op0=ALU.mult), [cstb, ppt], [diagw], disjoint=(cc + tap > 0))
                ws = [0]

                def seg(i):
                    w = wseg[ws[0] % 2]
                    ws[0] += 1
                    return load_seg(l, i, w)

                for cc in range(2):
                    for tap in range(31):
                        fw.op(PE_, lambda cc=cc, tap=tap: nc.tensor.matmul(
                            ps[3][:, cc * TB:(cc + 1) * TB], lhsT=diagw[:, cc, tap, :], rhs=ywin[:, cc, tap:tap + TB],
                            start=(tap == 0), stop=(tap == 30)), [diagw, ywin], [ps[3]], inc=(tap == 30))
                    fw.op(A_, lambda cc=cc: nc.scalar.activation(
                        out=f[1][:, cc * TB:(cc + 1) * TB], in_=ps[3][:, cc * TB:(cc + 1) * TB], func=AF.Identity,
                        bias=ppt[:, l, PC_CONVB + cc:PC_CONVB + cc + 1], scale=1.0), [ps[3], ppt], [f[1]])
                for cc in range(2):
                    fw.op(PE_, lambda cc=cc: nc.tensor.matmul(ps[2][:, 0:TB], lhsT=o256, rhs=f[1][:, cc * TB:(cc + 1) * TB],
                                                              start=(cc == 0), stop=(cc == 1)), [cst, f[1]], [ps[2]], inc=(cc == 1))
                for cc in range(2):
                    fw.op(V, lambda cc=cc: nc.vector.tensor_tensor(out=f[2][:, cc * TB:(cc + 1) * TB], in0=f[1][:, cc * TB:(cc + 1) * TB],
                                                                    in1=ps[2][:, 0:TB], op=ALU.subtract), [f[1], ps[2]], [f[2]])
                fw.op(A_, lambda: nc.scalar.activation(out=f[3][:, :], in_=f[2][:, :], func=AF.Square), [f[2]], [f[3]])
                for cc in range(2):
                    fw.op(PE_, lambda cc=cc: nc.tensor.matmul(ps[3][:, 0:TB], lhsT=o256, rhs=f[3][:, cc * TB:(cc + 1) * TB],
                                                              start=(cc == 0), stop=(cc == 1)), [cst, f[3]], [ps[3]], inc=(cc == 1))
                rs(f[4][:, 0:TB], ps[3][:, 0:TB], [ps[3]], [f[4]])
                for cc in range(2):
                    fw.op(V, lambda cc=cc: nc.vector.tensor_tensor(out=f[3][:, cc * TB:(cc + 1) * TB], in0=f[2][:, cc * TB:(cc + 1) * TB],
                                                                    in1=f[4][:, 0:TB], op=ALU.mult), [f[2], f[4]], [f[3]])
                    fw.op(A_, lambda cc=cc: nc.scalar.activation(
                        out=mixed[:, cc, :], in_=f[3][:, cc * TB:(cc + 1) * TB], func=AF.Silu,
                        scale=ppt[:, l, PC_CLNG + cc:PC_CLNG + cc + 1], bias=ppt[:, l, PC_CLNB + cc:PC_CLNB + cc + 1]),
                        [f[3], ppt], [mixed])
                if lat_stage < 2:
                    fw.barrier()
                    return

                wq_ = seg(2)
                for cc in range(2):
                    P = ps[2]
                    pfm(wq_, hb, cc, P, 0)
                    fw.op(A_, lambda: nc.scalar.activation(out=f[1][:, 0:TB], in_=P[:, 0:TB], func=AF.Square), [P], [f[1]])
                    fw.op(PE_, lambda: nc.tensor.matmul(P[:, TB:2 * TB], lhsT=blk64, rhs=f[1][:, 0:TB], start=True, stop=True), [cst, f[1]], [P])
                    rs(f[2][:, 0:TB], P[:, TB:2 * TB], [P], [f[2]])
                    for par in range(2):
                        fw.op(V, lambda cc=cc, par=par: nc.vector.scalar_tensor_tensor(
                            out=qb[:, par, cc, :], in0=P[:, 0:TB], scalar=qsc[:, l, par:par + 1], in1=f[2][:, 0:TB],
                            op0=ALU.mult, op1=ALU.mult), [P, qsc, f[2]], [qb])
                first = [True] * 4
                items = [(rr, j, d_, v0, v1) for rr in range(4) for (j, d_, v0, v1) in na_tiles(4 * s + rr)]

                def na_s(i):
                    rr, j, d_, v0, v1 = items[i]
                    SP = ps[i % 2]
                    ci = COMBOS.index((d_, v0, v1))
                    for h in range(4):
                        cc, par = h // 2, h % 2
                        fw.op(PE_, lambda h=h, cc=cc, par=par, j=j, SP=SP, rr=rr: nc.tensor.matmul(
                            SP[:, h * 64:(h + 1) * 64], lhsT=nkw[:, cc, (j - jlo) * 128:(j - jlo + 1) * 128],
                            rhs=qb[:, par, cc, rr * 64:(rr + 1) * 64], start=True, stop=False, skip_group_check=True),
                            [nkw, qb], [SP], inc=False)
                        fw.op(PE_, lambda h=h, ci=ci, SP=SP: nc.tensor.matmul(
                            SP[:, h * 64:(h + 1) * 64], lhsT=identB, rhs=biast[:, ci * 4 + h, :], start=False, stop=True,
                            skip_group_check=True), [cstb, biast], [SP], inc=(h == 3))

                def na_od(i):
                    rr, j, d_, v0, v1 = items[i]
                    SP = ps[i % 2]
                    p_ = pt[i % 2]
                    fw.op(A_, lambda SP=SP, p_=p_: nc.scalar.activation(out=p_[:, 0:256], in_=SP[:, 0:256], func=AF.Exp), [SP], [p_])
                    for h in range(4):
                        OD = ps[4 + h]
                        fw.op(PE_, lambda h=h, j=j, OD=OD, p_=p_, rr=rr, st=first[h]: nc.tensor.matmul(
                            OD[0:64, rr * 64:(rr + 1) * 64], lhsT=nvw[:, j - jlo, h * 64:(h + 1) * 64], rhs=p_[:, h * 64:(h + 1) * 64],
                            start=st, stop=False, skip_group_check=True), [nvw, p_], [OD], inc=False)
                        first[h] = False
                        fw.op(PE_, lambda h=h, OD=OD, p_=p_, rr=rr: nc.tensor.matmul(
                            OD[0:64, TB + rr * 64:TB + (rr + 1) * 64], lhsT=ones64, rhs=p_[:, h * 64:(h + 1) * 64],
                            start=False, stop=False, skip_group_check=True), [cstb, p_], [OD], inc=(h == 3))

                na_s(0)
                for i in range(len(items)):
                    if i + 1 < len(items):
                        na_s(i + 1)
                    na_od(i)
                for kt in range(2):
                    for h in range(4):
                        cc, par = h // 2, h % 2
                        SP = ps[h // 2]
                        fw.op(PE_, lambda h=h, cc=cc, par=par, SP=SP, kt=kt: nc.tensor.matmul(
                            SP[:, (h % 2) * 256:(h % 2 + 1) * 256], lhsT=nck[:, l, cc, kt * 128:(kt + 1) * 128], rhs=qb[:, par, cc, :],
                            start=True, stop=True), [nck, qb], [SP], inc=(h % 2 == 1))
                    for hh in range(2):
                        fw.op(A_, lambda hh=hh: nc.scalar.activation(out=pt2[:, hh * 512:(hh + 1) * 512], in_=ps[hh][:, :], func=AF.Exp),
                              [ps[hh]], [pt2])
                    for h in range(4):
                        OD = ps[4 + h]
                        fw.op(PE_, lambda h=h, OD=OD, kt=kt: nc.tensor.matmul(
                            OD[0:64, 0:TB], lhsT=ncvt[:, l, kt, h * 64:(h + 1) * 64], rhs=pt2[:, h * 256:(h + 1) * 256],
                            start=False, stop=(kt == 1), skip_group_check=True), [ncvt, pt2], [OD], inc=False)
                        fw.op(PE_, lambda h=h, OD=OD, kt=kt: nc.tensor.matmul(
                            OD[0:64, TB:2 * TB], lhsT=ones64, rhs=pt2[:, h * 256:(h + 1) * 256],
                            start=False, stop=(kt == 1), skip_group_check=True), [cstb, pt2], [OD], inc=True)
                for h in range(4):
                    OD = ps[4 + h]
                    fw.op(V, lambda OD=OD: nc.vector.reciprocal(out=f[5][0:64, 0:TB], in_=OD[0:64, TB:2 * TB]), [OD], [f[5]])
                    fw.op(V, lambda OD=OD, h=h: nc.vector.tensor_tensor(out=mixh[0:64, h, :], in0=OD[0:64, 0:TB], in1=f[5][0:64, 0:TB],
                                                                         op=ALU.mult), [OD, f[5]], [mixh])
                if lat_stage < 3:
                    fw.barrier()
                    return

                wq_ = seg(5)
                for cc in range(2):
                    P = ps[2]
                    pfm(wq_, hb, cc, P, 0)
                    fw.op(A_, lambda: nc.scalar.activation(out=f[1][:, 0:TB], in_=P[:, 0:TB], func=AF.Square), [P], [f[1]])
                    fw.op(PE_, lambda: nc.tensor.matmul(P[:, TB:2 * TB], lhsT=blk32, rhs=f[1][:, 0:TB], start=True, stop=True), [cst, f[1]], [P])
                    rs(f[2][:, 0:TB], P[:, TB:2 * TB], [P], [f[2]])
                    fw.op(V, lambda: nc.vector.scalar_tensor_tensor(out=f[3][:, 0:TB], in0=P[:, 0:TB], scalar=ppt[:, l, PC_DQ:PC_DQ + 1],
                                                                    in1=f[2][:, 0:TB], op0=ALU.mult, op1=ALU.mult), [P, ppt, f[2]], [f[3]])
                    rope(f[3], f[3][:, 0:TB], f[4], f[5], cs_, sn_, f[0][:, 0:TB], f[0], ps[3])
                    for v in range(4):
                        fw.op(V, lambda v=v, cc=cc: nc.vector.tensor_scalar(out=dqv[:, v, cc, :], in0=f[0][:, 0:TB],
                                                                             scalar1=cst[:, CC_MASK + v:CC_MASK + v + 1], scalar2=None,
                                                                             op0=ALU.mult), [f[0], cst], [dqv])
                sc32 = 32 ** -0.5
                for cc in range(2):
                    firstb = [True] * 4
                    ntile = SEQL // 128 + 2
                    def tile_src(ti):
                        if ti < SEQL // 128:
                            return (dkg[:, cc, ti * 128:(ti + 1) * 128], dkg, dvg, (lambda h, ti=ti: dvg[:, ti, h * 64:(h + 1) * 64]))
                        kt = ti - SEQL // 128
                        return (dck[:, l, cc, kt * 128:(kt + 1) * 128], dck, dcvt, (lambda h, kt=kt: dcvt[:, l, kt, h * 64:(h + 1) * 64]))

                    def s_part(ti):
                        Kap, Kt, Vt, vsl = tile_src(ti)
                        st_ = ti % 2
                        SPs = [ps[2 * st_], ps[2 * st_ + 1]]
                        for sub in range(2):
                            for par in range(2):
                                fw.op(PE_, lambda sub=sub, par=par, Kap=Kap, SPs=SPs: nc.tensor.matmul(
                                    SPs[sub][:, par * 256:(par + 1) * 256], lhsT=Kap, rhs=dqv[:, sub * 2 + par, cc, :],
                                    start=True, stop=True), [Kt, dqv], [SPs[sub]], inc=(par == 1))

                    def od_part(ti):
                        Kap, Kt, Vt, vsl = tile_src(ti)
                        st_ = ti % 2
                        SPs = [ps[2 * st_], ps[2 * st_ + 1]]
                        pts = [pt[2 * st_], pt[2 * st_ + 1]]
                        for sub in range(2):
                            fw.op(A_, lambda sub=sub, SPs=SPs, pts=pts: nc.scalar.activation(out=pts[sub][:, :], in_=SPs[sub][:, :],
                                                                                              func=AF.Exp, scale=sc32), [SPs[sub]], [pts[sub]])
                        for sub in range(2):
                            for par in range(2):
                                bi = sub * 2 + par
                                OD = ps[4 + bi]
                                h = 2 * cc + par
                                fw.op(PE_, lambda OD=OD, h=h, sub=sub, par=par, pts=pts, vsl=vsl, st=firstb[bi]: nc.tensor.matmul(
                                    OD[0:64, 0:TB], lhsT=vsl(h), rhs=pts[sub][:, par * 256:(par + 1) * 256],
                                    start=st, stop=False, skip_group_check=True), [Vt, pts[sub]], [OD], inc=False)
                                firstb[bi] = False
                                fw.op(PE_, lambda OD=OD, sub=sub, par=par, pts=pts: nc.tensor.matmul(
                                    OD[0:64, TB:2 * TB], lhsT=ones64, rhs=pts[sub][:, par * 256:(par + 1) * 256],
                                    start=False, stop=False, skip_group_check=True), [cstb, pts[sub]], [OD], inc=True)

                    s_part(0)
                    for ti in range(ntile):
                        if ti + 1 < ntile:
                            s_part(ti + 1)
                        od_part(ti)
                    for par in range(2):
                        h = 2 * cc + par
                        O1, O2 = ps[4 + par], ps[6 + par]
                        fw.op(V, lambda O1=O1: nc.vector.reciprocal(out=f[5][0:64, 0:TB], in_=O1[0:64, TB:2 * TB]), [O1], [f[5]])
                        fw.op(V, lambda O2=O2: nc.vector.reciprocal(out=f[5][0:64, TB:2 * TB], in_=O2[0:64, TB:2 * TB]), [O2], [f[5]])
                        fw.op(V, lambda O1=O1: nc.vector.tensor_tensor(out=f[0][0:64, 0:TB], in0=O1[0:64, 0:TB], in1=f[5][0:64, 0:TB], op=ALU.mult),
                              [O1, f[5]], [f[0]])
                        fw.op(V, lambda O2=O2: nc.vector.tensor_tensor(out=f[0][0:64, TB:2 * TB], in0=O2[0:64, 0:TB], in1=f[5][0:64, TB:2 * TB],
                                                                        op=ALU.mult), [O2, f[5]], [f[0]])
                        fw.op(V, lambda: nc.vector.scalar_tensor_tensor(out=f[1][0:64, 0:TB], in0=f[0][0:64, TB:2 * TB], scalar=lamt[0:64, l, 0:1],
                                                                        in1=f[0][0:64, 0:TB], op0=ALU.mult, op1=ALU.add), [f[0], lamt], [f[1]])
                        fw.op(A_, lambda: nc.scalar.activation(out=f[1][0:64, TB:2 * TB], in_=f[1][0:64, 0:TB], func=AF.Square), [f[1]], [f[1]])
                        fw.op(PE_, lambda: nc.tensor.matmul(ps[2][0:64, 0:TB], lhsT=cst[0:64, CC_B64:CC_B64 + 64], rhs=f[1][0:64, TB:2 * TB],
                                                            start=True, stop=True), [cst, f[1]], [ps[2]])
                        fw.op(A_, lambda: nc.scalar.activation(out=f[2][0:64, 0:TB], in_=ps[2][0:64, 0:TB], func=AF.Sqrt, bias=epst[0:64, 0:1],
                                                               scale=1.0), [ps[2], epst], [f[2]])
                        fw.op(V, lambda: nc.vector.reciprocal(out=f[2][0:64, 0:TB], in_=f[2][0:64, 0:TB]), [f[2]], [f[2]])
                        fw.op(V, lambda h=h: nc.vector.scalar_tensor_tensor(out=mixh[0:64, 4 + h, :], in0=f[1][0:64, 0:TB], scalar=subg[0:64, l:l + 1],
                                                                             in1=f[2][0:64, 0:TB], op0=ALU.mult, op1=ALU.mult),
                              [f[1], subg, f[2]], [mixh])
                if lat_stage < 4:
                    fw.barrier()
                    return

                wu = seg(8)
                wv = seg(9)
                for cc in range(2):
                    pfm(wu, hb, cc, ps[0], 0)
                    fw.op(A_, lambda cc=cc: nc.scalar.activation(out=gu[:, cc, :], in_=ps[0][:, 0:TB], func=AF.Gelu_apprx_tanh),
                          [ps[0]], [gu])
                for tl in range(2):
                    P = ps[1]
                    ptm(wv, hb, tl, P, 0)
                    fw.op(A_, lambda: nc.scalar.activation(out=f[0][:, 0:256], in_=P[:, 0:256], func=AF.Gelu_apprx_tanh), [P], [f[0]])
                    fw.op(V, lambda: nc.vector.bn_stats(out=bnst[:, 0, 0:6], in_=f[0][:, 0:256]), [f[0]], [bnst])
                    fw.op(V, lambda: nc.vector.bn_aggr(out=bnst[:, 1, 0:2], in_=bnst[:, 0, 0:6]), [bnst], [bnst])
                    rs(bnst[:, 1, 2:3], bnst[:, 1, 1:2], [bnst], [bnst])
                    fw.op(V, lambda: nc.vector.tensor_scalar(out=f[0][:, 256:512], in0=f[0][:, 0:256], scalar1=bnst[:, 1, 0:1],
                                                             scalar2=bnst[:, 1, 2:3], op0=ALU.subtract, op1=ALU.mult), [f[0], bnst], [f[0]])
                    fw.op(V, lambda: nc.vector.tensor_tensor(out=f[0][:, 0:256], in0=f[0][:, 256:512], in1=rowt[:, 0:256], op=ALU.mult),
                          [f[0], rowt], [f[0]])
                    for par in range(2):
                        fw.op(V, lambda tl=tl, par=par: nc.vector.tensor_tensor(
                            out=vpad[:, tl, :, :].rearrange("p (c r) n -> p c r n", r=2)[:, :, par, par * 64:par * 64 + 64],
                            in0=f[0][:, 0:256].rearrange("p (c r n) -> p c r n", c=2, r=2)[:, :, par, :],
                            in1=rowt[:, 256:512].rearrange("p (c r n) -> p c r n", c=2, r=2)[:, :, par, :], op=ALU.add),
                            [f[0], rowt], [vpad])
                for cc in range(2):
                    P = ps[2]
                    for tl in range(2):
                        for par in range(2):
                            fw.op(PE_, lambda tl=tl, par=par, cc=cc: nc.tensor.matmul(
                                P[:, tl * 128:(tl + 1) * 128], lhsT=vpad[:, tl, 2 * cc + par, :], rhs=wsb[:, 2 * cc + par, :],
                                start=(par == 0), stop=(par == 1)), [vpad, wsb], [P], inc=(par == 1))
                    fw.op(V, lambda cc=cc: nc.vector.tensor_tensor(
                        out=f[1][:, 0:TB].rearrange("p (a b) -> p a b", a=2), in0=P[:, 0:TB].rearrange("p (a b) -> p a b", a=2),
                        in1=bst[:, cc, :].unsqueeze(1).to_broadcast([128, 2, 128]), op=ALU.add), [P, bst], [f[1]])
                    fw.op(V, lambda cc=cc: nc.vector.tensor_tensor(out=mixed[:, 6 + cc, :], in0=f[1][:, 0:TB], in1=gu[:, cc, :], op=ALU.mult),
                          [f[1], gu], [mixed])

                for j in range(KC):
                    P = ps[3 + (j % 2)]
                    jj = slice(j * 128, (j + 1) * 128)
                    n_mm = 12
                    mi = 0
                    for q_ in range(2):
                        fw.op(PE_, lambda q_=q_, P=P, jj=jj, mi=mi: nc.tensor.matmul(P[:, 0:TB], lhsT=wo[:, q_, jj], rhs=mixed[:, q_, :],
                                                                                       start=(mi == 0), stop=False), [wo, mixed], [P], inc=False)
                        mi += 1
                    for h in range(8):
                        fw.op(PE_, lambda h=h, P=P, jj=jj: nc.tensor.matmul(P[:, 0:TB], lhsT=woh[0:64, h, jj], rhs=mixh[0:64, h, :],
                                                                             start=False, stop=False), [woh, mixh], [P], inc=False)
                    for q_ in range(2):
                        fw.op(PE_, lambda q_=q_, P=P, jj=jj: nc.tensor.matmul(P[:, 0:TB], lhsT=wo[:, 2 + q_, jj], rhs=mixed[:, 6 + q_, :],
                                                                               start=False, stop=(q_ == 1)), [wo, mixed], [P], inc=(q_ == 1))
                    fw.op(V, lambda j=j, P=P: nc.vector.scalar_tensor_tensor(
                        out=xt[:, j, :], in0=P[:, 0:TB], scalar=modt[:, l, 16 + j, 1:2], in1=xt[:, j, :], op0=ALU.mult, op1=ALU.add),
                        [P, modt, xt], [xt])
                fw.dma("sp", yl[:, :, t0:t0 + TB], xt[:, :, :], reads=[xt], writes=[SB["x"]])
                fw.barrier()

        def latent():
            SB = {k: Buf("scr_" + k) for k in ["y", "k", "v", "x"]}
            with ExitStack() as gl0:
                nck = sb("nck", [128, L, 2, 256], BF16, ctx=gl0)
                ncvt = sb("ncvt", [128, L, 2, 256], BF16, ctx=gl0)
                dck = sb("dck", [128, L, 2, 256], BF16, ctx=gl0)
                dcvt = sb("dcvt", [128, L, 2, 256], BF16, ctx=gl0)
                qsc = sb("qsc", [128, L, 2], ctx=gl0)
                zt = sb("zt", [128, 2, 15], BF16, ctx=gl0)
                for (t_, d_) in [(nck, nckT), (ncvt, ncv), (dck, dckT), (dcvt, dcv)]:
                    fw.dma(PO, t_[:, :, :, :], d_.rearrange("l p a b -> p l a b"), writes=[t_])
                fw.op(V, lambda: nc.vector.tensor_scalar(out=qsc[:, :, :], in0=ppt[:, :, PC_NAQE:PC_NAQE + 2], scalar1=0.125, scalar2=None,
                                                         op0=ALU.mult), [ppt], [qsc])
                fw.op(V, lambda: nc.vector.memset(zt[:, :, :], 0.0), [], [zt])
                fw.dma("sp", ybuf[:, :, 0:15], zt[:, :, :], reads=[zt], writes=[SB["y"]])
                fw.dma("sp", ybuf[:, :, 15 + SEQL:30 + SEQL], zt[:, :, :], reads=[zt], writes=[SB["y"]])
                for l in range(nl):
                    src = xl if l == 0 else yl
                    with ExitStack() as gl:
                        dkg = sb("dkg", [128, 2, SEQL], BF16, ctx=gl)
                        dvg = sb("dvg", [128, SEQL // 128, 256], BF16, ctx=gl)
                        biast = sb("biast", [128, NCOMBO * 4, 64], BF16, ctx=gl)
                        fw.dma(PO, biast[:, :, :], biasT[l], writes=[biast])
                        for s in range(n_lat):
                            latA(l, s, src, dkg, dvg, SB)
                        if lat_stage >= 1:
                            for s in range(n_lat):
                                latB(l, s, src, dkg, dvg, SB, (nck, ncvt, dck, dcvt, biast, qsc))
                        fw.barrier()
                    if lat_stage >= 5:
                        fw.barrier()
                        peer_pass(l, n_lat, 1, (lambda s: yl[:, :, s * TB:(s + 1) * TB]), yl, SB["x"])

        SBc = Buf("scr_xc")
        for l in range(nl if n_seq else 0):
            for s in range(n_seq):
                srcx = xc if l == 0 else yc
                fw.dma("sp", xt[:, :, :], srcx[:, :, s * TB:(s + 1) * TB], reads=[SBc], writes=[xt])
                if stage >= 1:
                    mixers(l, s)
                fw.dma("sp", yc[:, :, s * TB:(s + 1) * TB], xt[:, :, :], reads=[xt], writes=[SBc])
                fw.barrier()
            if stage >= 2:
                peer_pass(l, n_seq, 0, (lambda s: yc[:, :, s * TB:(s + 1) * TB]), yc, SBc)
        if n_lat:
            latent()
        fw.finish()
        print("instructions:", fw.nins, fw.cnt)
        import collections
        agg = collections.defaultdict(int)
        for k, cx in ALLOC.items():
            agg[cx._nm.rsplit("_", 1)[0]] = max(agg[cx._nm.rsplit("_", 1)[0]], cx._bytes)
        print("alloc KB by ctx(first tile):", {k: round(v / 1024, 1) for k, v in agg.items()})
    return nc


def _consts():
    c = np.zeros((128, NCC), np.float32)
    p = np.arange(128)
    c[:, CC_ID:CC_ID + 128] = np.eye(128)
    c[:, CC_B64:CC_B64 + 128] = (p[:, None] // 64 == p[None, :] // 64) / 64.0
    c[:, CC_B32:CC_B32 + 128] = (p[:, None] // 32 == p[None, :] // 32) / 32.0
    c[:, CC_O1024:CC_O1024 + 128] = 1.0 / 1024
    c[:, CC_O256:CC_O256 + 128] = 1.0 / 256
    c[:, CC_OE:CC_OE + 128] = (p[None, :] < 64)
    c[:, CC_OO:CC_OO + 128] = (p[None, :] >= 64)
    c[:, CC_IOTA:CC_IOTA + 128] = p[None, :]
    c[:, CC_IOTA16:CC_IOTA16 + 16] = np.arange(16)[None, :]
    c[:, CC_MASK + 0] = ((p % 64) < 32) & (p < 64)
    c[:, CC_MASK + 1] = ((p % 64) < 32) & (p >= 64)
    c[:, CC_MASK + 2] = ((p % 64) >= 32) & (p < 64)
    c[:, CC_MASK + 3] = ((p % 64) >= 32) & (p >= 64)
    for m in range(128):
        if (m % 16) < 8:
            c[m + 8, CC_PSW + m] = -1.0
        else:
            c[m - 8, CC_PSW + m] = 1.0
    return c


def na_tiles(r):
    r0 = min(max(r - 4, 0), 56)
    out = []
    for j in range(r0 // 2, (r0 + 7) // 2 + 1):
        v0 = r0 <= 2 * j < r0 + 8
        v1 = r0 <= 2 * j + 1 < r0 + 8
        out.append((j, 2 * j - r, bool(v0), bool(v1)))
    return out


COMBOS = sorted(set((d, v0, v1) for r in range(64) for (_, d, v0, v1) in na_tiles(r)))
NCOMBO = len(COMBOS)
SEQL = 4096


def _rope_tables():
    f = np.arange(128)
    w = f % 32
    i = np.where(w < 16, w, w - 16)
    inv = (10000.0 ** (-(np.arange(8, dtype=np.float32)) / 8)).astype(np.float32)
    t = np.arange(SEQL)
    rows = (t // 64).astype(np.float32)
    cols = (t % 64).astype(np.float32)
    pos = np.where((w < 16)[:, None], rows[None, :], cols[None, :]).astype(np.float32)
    ang = (pos * inv[i % 8][:, None]).astype(np.float32)
    return np.cos(ang).astype(np.float32), np.sin(ang).astype(np.float32)


def _bias_tables(rel_bias):
    out = np.full((L, 128, NCOMBO * 4, 64), -1e30, np.float32)
    key = np.arange(128)
    wr = key // 64
    kc = key % 64
    qc = np.arange(64)
    c0 = np.clip(qc - 8, 0, 48)
    inwin = (kc[:, None] >= c0[None, :]) & (kc[:, None] < c0[None, :] + 16)
    dci = np.clip(kc[:, None] - qc[None, :] + 15, 0, 30)
    for ci, (d, v0, v1) in enumerate(COMBOS):
        valid = np.where(wr == 0, v0, v1)[:, None] & inwin
        dri = np.clip(d + wr + 7, 0, 14)
        for h in range(4):
            g = rel_bias[:, h][:, dri[:, None], dci]
            out[:, :, ci * 4 + h, :] = np.where(valid[None], g, -1e30)
    return out


def _fm(v):
    return np.ascontiguousarray(v.reshape(-1, 128).T)


def _pack_weights(inp):
    f = lambda a: np.asarray(a, np.float32)
    pp = np.zeros((L, 128, NPC), np.float32)
    p = np.arange(128)
    for l in range(L):
        pp[l, :, PC_N1G:PC_N1G + 8] = _fm(f(inp["norm1_g"][l]))
        pp[l, :, PC_N2G:PC_N2G + 8] = _fm(f(inp["norm2_g"][l]))
        pp[l, :, PC_BMOD:PC_BMOD + 48] = _fm(f(inp["b_mod"][l]))
        cw = f(inp["conv_w"][l])
        for cc in range(2):
            pp[l, :, PC_CONVW + cc * 31:PC_CONVW + (cc + 1) * 31] = cw[:, cc * 128:(cc + 1) * 128].T
        pp[l, :, PC_CONVB:PC_CONVB + 2] = _fm(f(inp["conv_b"][l]))
        pp[l, :, PC_CLNG:PC_CLNG + 2] = _fm(f(inp["conv_ln_g"][l]))
        pp[l, :, PC_CLNB:PC_CLNB + 2] = _fm(f(inp["conv_ln_b"][l]))
        pp[l, :, PC_NAQ] = f(inp["na_qn_g"][l])[p % 64]
        pp[l, :, PC_NAK] = f(inp["na_kn_g"][l])[p % 64]
        dq = f(inp["diff_qn_g"][l])[p % 32]
        pp[l, :, PC_DQ1] = np.where((p % 64) < 32, dq, 0.0)
        pp[l, :, PC_DQ2] = np.where((p % 64) >= 32, dq, 0.0)
        pp[l, :, PC_DK] = f(inp["diff_kn_g"][l])[p % 32]
        pp[l, :, PC_DQ] = dq
        pp[l, :, PC_NAQE] = np.where(p < 64, pp[l, :, PC_NAQ], 0.0)
        pp[l, :, PC_NAQO] = np.where(p >= 64, pp[l, :, PC_NAQ], 0.0)
        pp[l, :, PC_D1E] = np.where(p < 64, pp[l, :, PC_DQ1], 0.0)
        pp[l, :, PC_D1O] = np.where(p >= 64, pp[l, :, PC_DQ1], 0.0)
        pp[l, :, PC_D2E] = np.where(p < 64, pp[l, :, PC_DQ2], 0.0)
        pp[l, :, PC_D2O] = np.where(p >= 64, pp[l, :, PC_DQ2], 0.0)
        pp[l, :, PC_SUB] = f(inp["diff_subln_g"][l])[p % 64]
    rowp = np.concatenate([f(inp["gmlp_ln_g"]), f(inp["gmlp_ln_b"])], axis=1)
    bs = f(inp["gmlp_bs"])
    bsT = np.zeros((L, 128, 2, 128), np.float32)
    for cc in range(2):
        bsT[:, 0:64, cc, :] = bs[:, 2 * cc, None, :]
        bsT[:, 64:128, cc, :] = bs[:, 2 * cc + 1, None, :]
    wsT = np.ascontiguousarray(f(inp["gmlp_ws"]).transpose(0, 3, 1, 2))
    dlam = f(inp["diff_lambda"]).reshape(L, 128)
    skT = np.ascontiguousarray(f(inp["peer_sub_keys"]).transpose(0, 3, 1, 2))
    pu = f(inp["peer_u"])
    puT = np.ascontiguousarray(pu.reshape(L, 128, 128, KC, 128).transpose(0, 2, 4, 3, 1)).reshape(L, 128, 128, KC * 128)
    cosT, sinT = _rope_tables()
    return dict(cosT=cosT, sinT=sinT, biasT=_bias_tables(f(inp["na_rel_bias"])), consts=_consts(), pp=pp, rowp=rowp, bsT=bsT, wsT=wsT, dlam=dlam, skT=skT, puT=puT,
                w_mod=f(inp["w_mod"]), w_in=f(inp["w_in"]), w_out=f(inp["w_out"]), wq=f(inp["peer_wq"]), pv=f(inp["peer_v"]))


def _pack_latent(inp, b):
    f = lambda a: np.asarray(a, np.float32)
    xs = f(inp["x_sample"][b]).reshape(SEQL, KC, 128)
    m = {"xl": np.ascontiguousarray(xs.transpose(2, 1, 0))}
    for nm, key in [("n", "cache_na_kv"), ("d", "cache_diff_kv")]:
        c = f(inp[key][b])
        k = c[:, 0].transpose(0, 1, 3, 2).reshape(L, 2, 128, 256).transpose(0, 2, 1, 3)
        v = c[:, 1].transpose(0, 2, 1, 3).reshape(L, 2, 128, 256).transpose(0, 2, 1, 3)
        m[nm + "ckT"] = np.ascontiguousarray(k)
        m[nm + "cv"] = np.ascontiguousarray(v)
    return m


_NC_CACHE = {}


def kernel(**inputs):
    f = lambda a: np.asarray(a, np.float32)
    W = _pack_weights(inputs)
    xp = f(inputs["x_prompt"])
    c_ctx = f(inputs["c_ctx"])
    cvec = f(inputs["c"])
    if "nc" not in _NC_CACHE:
        _NC_CACHE["nc"] = build_program(4)
    nc = _NC_CACHE["nc"]
    in_maps = []
    lat = [_pack_latent(inputs, b) for b in range(2)]
    for c in range(NCORES):
        xs = xp[4 * c:4 * c + 4].reshape(1024, KC, 128)
        m = dict(W)
        m.update(lat[c // 4])
        m["xc"] = np.ascontiguousarray(xs.transpose(2, 1, 0))
        cond = np.stack([c_ctx, cvec[c // 4]], axis=1)
        m["condT"] = np.ascontiguousarray(cond.reshape(KC, 128, 2).transpose(1, 0, 2))
        in_maps.append(m)
    res = run_bass_kernel_spmd(nc, in_maps, core_ids=list(range(NCORES)))
    y_prompt = np.zeros((32, 256, 1024), np.float32)
    na_kv = np.zeros((32, L, 2, 4, 256, 64), np.float32)
    diff_kv = np.zeros((32, L, 2, 4, 256, 64), np.float32)
    for c in range(NCORES):
        r = res.results[c]
        y_prompt[4 * c:4 * c + 4] = r["yc"].transpose(2, 1, 0).reshape(4, 256, 1024)
        for (kT, vo, dst) in [("nkT", "nvo", na_kv), ("dkT", "dvo", diff_kv)]:
            k = r[kT]
            k = k.transpose(0, 2, 1, 3).reshape(L, 4, 64, 4, 256)
            dst[4 * c:4 * c + 4, :, 0] = k.transpose(3, 0, 1, 4, 2)
            v = r[vo]
            v = v.transpose(0, 2, 1, 3).reshape(L, 4, 256, 4, 64)
            dst[4 * c:4 * c + 4, :, 1] = v.transpose(1, 0, 3, 2, 4)
    y_sample = np.stack([res.results[4 * b]["yl"].transpose(2, 1, 0).reshape(SEQL, 1024) for b in range(2)], axis=0)
    return (y_prompt, y_sample, na_kv, diff_kv)
```
